# Optimizing a Trainium2 kernel written in Bass

```python
import jax, jax.numpy as jnp
from jax import lax
import numpy as np

D_MODEL = 1024
BATCH = 4
SEQ = 8192
DEPTH = 1

MIX_WIDTH = D_MODEL
CONV_WIDTH = MIX_WIDTH // 2
CONV_GROUPS = 8
CONV_K = 3
RET_WIDTH = MIX_WIDTH - CONV_WIDTH
RET_HEADS = 4
RET_HEAD_DIM = RET_WIDTH // RET_HEADS
RET_CHUNK = 128
ROPE_BASE = 10000.0
D_FF = 4 * D_MODEL
NORM_EPS = 1e-6
IN_COLS = 3 * CONV_WIDTH + 4 * RET_WIDTH

kernel_name = "hymba_conv_retention_hybrid"


def rms_norm(x, g, eps=NORM_EPS):
    xf = x.astype(jnp.float32)
    y = xf * lax.rsqrt(jnp.mean(xf * xf, axis=-1, keepdims=True) + eps)
    return (y * g.astype(jnp.float32)).astype(x.dtype)


def group_rms_norm(x, g, n_groups, eps=NORM_EPS):
    shp = x.shape
    xg = x.reshape(shp[:-1] + (n_groups, shp[-1] // n_groups)).astype(jnp.float32)
    y = xg * lax.rsqrt(jnp.mean(xg * xg, axis=-1, keepdims=True) + eps)
    return (y.reshape(shp) * g.astype(jnp.float32)).astype(x.dtype)


def rotary(x, positions):
    half = x.shape[-1] // 2
    inv_freq = 1.0 / (ROPE_BASE ** (jnp.arange(half, dtype=jnp.float32) / half))
    ang = positions.astype(jnp.float32)[:, None] * inv_freq[None, :]
    cos = jnp.cos(ang)[None, :, None, :].astype(x.dtype)
    sin = jnp.sin(ang)[None, :, None, :].astype(x.dtype)
    x1, x2 = x[..., :half], x[..., half:]
    return jnp.concatenate([x1 * cos - x2 * sin, x2 * cos + x1 * sin], axis=-1)


def causal_dwconv3(u, w):
    up = jnp.pad(u, ((0, 0), (CONV_K - 1, 0), (0, 0)))
    s = u.shape[1]
    return up[:, 0:s] * w[0] + up[:, 1:s + 1] * w[1] + up[:, 2:s + 2] * w[2]


def chunkwise_retention(q, k, v):
    b, s, h, d = q.shape
    nc = s // RET_CHUNK
    dt = q.dtype
    to_chunks = lambda t: t.reshape(b, nc, RET_CHUNK, h, d).transpose(0, 3, 1, 2, 4)
    qc, kc, vc = to_chunks(q), to_chunks(k), to_chunks(v)

    log_gamma = jnp.log(1.0 - 2.0 ** (-5.0 - jnp.arange(h, dtype=jnp.float32)))
    idx = jnp.arange(RET_CHUNK, dtype=jnp.float32)
    diff = idx[:, None] - idx[None, :]
    intra_decay = jnp.where(diff[None] >= 0,
                            jnp.exp(log_gamma[:, None, None] * jnp.maximum(diff, 0.0)[None]),
                            0.0).astype(dt)
    zeta = jnp.exp(log_gamma[:, None] * (RET_CHUNK - 1 - idx)[None]).astype(dt)
    xi = jnp.exp(log_gamma[:, None] * (idx + 1.0)[None]).astype(dt)
    chunk_decay = jnp.exp(log_gamma * RET_CHUNK).astype(dt)

    scores = jnp.einsum('bhncd,bhnmd->bhncm', qc, kc) * intra_decay[None, :, None]
    o_intra = jnp.einsum('bhncm,bhnme->bhnce', scores, vc)

    kv = jnp.einsum('bhnmd,bhnme->nbhde', kc * zeta[None, :, None, :, None], vc)

    def step(state, kv_n):
        new_state = chunk_decay[None, :, None, None] * state + kv_n
        return new_state, state

    _, s_prev = lax.scan(step, jnp.zeros_like(kv[0]), kv)
    o_cross = jnp.einsum('bhncd,nbhde->bhnce', qc * xi[None, :, None, :, None], s_prev)

    o = o_intra + o_cross
    return o.transpose(0, 2, 3, 1, 4).reshape(b, s, h, d)


def hybrid_mixer(u, w_in, conv_w, conv_norm_g, ret_norm_g, w_out):
    b, s, _ = u.shape
    z = u @ w_in
    c0 = 3 * CONV_WIDTH
    cb, cc, ch = jnp.split(z[..., :c0], 3, axis=-1)
    rq, rk, rv, rg = jnp.split(z[..., c0:], 4, axis=-1)

    y_conv = cb * causal_dwconv3(cc * ch, conv_w)
    y_conv = group_rms_norm(y_conv, conv_norm_g, CONV_GROUPS)

    positions = jnp.arange(s)
    q = rotary(rq.reshape(b, s, RET_HEADS, RET_HEAD_DIM), positions)
    k = rotary(rk.reshape(b, s, RET_HEADS, RET_HEAD_DIM), positions) * (RET_HEAD_DIM ** -0.5)
    v = rv.reshape(b, s, RET_HEADS, RET_HEAD_DIM)
    o = chunkwise_retention(q, k, v).reshape(b, s, RET_WIDTH)
    y_ret = group_rms_norm(o, ret_norm_g, RET_HEADS) * jax.nn.silu(rg)

    return jnp.concatenate([y_conv, y_ret], axis=-1) @ w_out


def sq_relu_mlp(u, w_up, w_down):
    hdn = jax.nn.relu(u @ w_up)
    return (hdn * hdn) @ w_down


def setup_inputs(seed: int = 0) -> dict:
    key = jax.random.key(seed)
    ks = jax.random.split(key, 11)
    f32 = jnp.float32
    x = jax.random.normal(ks[0], (BATCH, SEQ, D_MODEL), f32)
    norm1_g = 1.0 + 0.05 * jax.random.normal(ks[1], (D_MODEL,), f32)
    w_in = jax.random.normal(ks[2], (D_MODEL, IN_COLS), f32) * D_MODEL ** -0.5
    conv_w = jax.random.normal(ks[3], (CONV_K, CONV_WIDTH), f32) * CONV_K ** -0.5
    conv_norm_g = 1.0 + 0.05 * jax.random.normal(ks[4], (CONV_WIDTH,), f32)
    ret_norm_g = 1.0 + 0.05 * jax.random.normal(ks[5], (RET_WIDTH,), f32)
    w_out = jax.random.normal(ks[6], (MIX_WIDTH, D_MODEL), f32) * MIX_WIDTH ** -0.5
    norm2_g = 1.0 + 0.05 * jax.random.normal(ks[7], (D_MODEL,), f32)
    w_up = jax.random.normal(ks[8], (D_MODEL, D_FF), f32) * D_MODEL ** -0.5
    w_down = jax.random.normal(ks[9], (D_FF, D_MODEL), f32) * D_FF ** -0.5
    final_norm_g = 1.0 + 0.05 * jax.random.normal(ks[10], (D_MODEL,), f32)
    return {"x": x, "norm1_g": norm1_g, "w_in": w_in, "conv_w": conv_w,
            "conv_norm_g": conv_norm_g, "ret_norm_g": ret_norm_g, "w_out": w_out,
            "norm2_g": norm2_g, "w_up": w_up, "w_down": w_down,
            "final_norm_g": final_norm_g}


def reference(x, norm1_g, w_in, conv_w, conv_norm_g, ret_norm_g, w_out,
              norm2_g, w_up, w_down, final_norm_g):
    h = x
    for _ in range(DEPTH):
        h = h + hybrid_mixer(rms_norm(h, norm1_g), w_in, conv_w, conv_norm_g, ret_norm_g, w_out)
        h = h + sq_relu_mlp(rms_norm(h, norm2_g), w_up, w_down)
    return rms_norm(h, final_norm_g)
```

```python
import numpy as np
import ml_dtypes
from contextlib import ExitStack
import concourse.bass as bass
import concourse.mybir as mybir
from concourse.bass_utils import run_bass_kernel_spmd

F32 = mybir.dt.float32
BF16 = mybir.dt.bfloat16
AF = mybir.ActivationFunctionType
ALU = mybir.AluOpType

D = 1024
NH = 4
EPS = 1e-6
NRING = 5
DEBUG = False
STOP = None


class Sched:
    ENGS = ("pe", "act", "dve", "pool", "sp")

    def __init__(self, nc):
        self.nc = nc
        self.ops = []
        self.last_w = {}
        self.readers = {}

    def _deps(self, reads, writes):
        deps = {}
        for k in reads:
            w = self.last_w.get(k)
            if w is not None:
                deps[w] = True
        for k in writes:
            w = self.last_w.get(k)
            if w is not None:
                deps.setdefault(w, False)
            for r in self.readers.get(k, ()):
                deps.setdefault(r, False)
        return deps

    def _commit(self, idx, reads, writes):
        for k in reads:
            self.readers.setdefault(k, []).append(idx)
        for k in writes:
            self.last_w[k] = idx
            self.readers[k] = []

    def op(self, eng, fn, reads=(), writes=()):
        writes = list(writes) + [k for k in reads if k.startswith("pb") and k not in writes]
        deps = self._deps(reads, writes)
        idx = len(self.ops)
        self.ops.append(dict(eng=eng, fn=fn, deps=deps, kind="c", sig=False))
        self._commit(idx, reads, writes)
        return idx

    def dma(self, eng, fn, sem, reads=(), writes=()):
        deps = self._deps(reads, writes)
        idx = len(self.ops)
        self.ops.append(dict(eng=eng, fn=fn, deps=deps, kind="d", sem=sem))
        self._commit(idx, reads, writes)
        return idx

    def emit(self, stack):
        nc = self.nc
        ops = self.ops

        def skip(p, o):
            return (p["kind"] == "c" and o["kind"] == "c" and p["eng"] == o["eng"])

        for o in ops:
            for d, raw in o["deps"].items():
                p = ops[d]
                if p["kind"] == "d":
                    continue
                if skip(p, o) and (p["eng"] == "pe" or not raw):
                    continue
                p["sig"] = True
        esem = {e: stack.enter_context(nc.semaphore("s_" + e)) for e in ("pe", "act", "dve", "pool")}
        dsem, dcnt = {}, {}
        cnt = {e: 0 for e in esem}
        for o in ops:
            if o["kind"] == "c":
                if o["sig"]:
                    cnt[o["eng"]] += 1
                    o["val"] = cnt[o["eng"]]
            else:
                s = o["sem"]
                if s not in dsem:
                    dsem[s] = stack.enter_context(nc.semaphore("d_" + s))
                    dcnt[s] = 0
                dcnt[s] += 16
                o["val"] = dcnt[s]
        per_eng = {e: [] for e in self.ENGS}
        for i, o in enumerate(ops):
            per_eng[o["eng"]].append(i)
        seen = {e: {} for e in self.ENGS}

        def run(eng_name, engine):
            sn = seen[eng_name]
            for i in per_eng[eng_name]:
                o = ops[i]
                waits = {}
                for d, raw in o["deps"].items():
                    p = ops[d]
                    if p["kind"] == "d":
                        key, h = ("d", p["sem"]), dsem[p["sem"]]
                    else:
                        if skip(p, o) and (eng_name == "pe" or not raw):
                            continue
                        key, h = ("c", p["eng"]), esem[p["eng"]]
                    v = p["val"]
                    if key not in waits or waits[key][1] < v:
                        waits[key] = (h, v)
                for key, (h, v) in waits.items():
                    if sn.get(key, 0) >= v:
                        continue
                    sn[key] = v
                    engine.wait_ge(h, v)
                ins = o["fn"](engine)
                if o["kind"] == "d":
                    ins.then_inc(dsem[o["sem"]], 16)
                elif o["sig"]:
                    ins.then_inc(esem[o["eng"]], 1)
            if eng_name == "sp":
                for s in dsem:
                    engine.wait_ge(dsem[s], dcnt[s])

        block = stack.enter_context(nc.Block())
        block.tensor(lambda e: run("pe", e))
        block.scalar(lambda e: run("act", e))
        block.vector(lambda e: run("dve", e))
        block.gpsimd(lambda e: run("pool", e))
        block.sync(lambda e: run("sp", e))


def build_nc(nt_main=32, nt_pre=32):
    assert nt_main % 4 == 0
    nblk = nt_main // 4
    lg = [float(np.log(np.float32(1.0) - np.float32(2.0) ** np.float32(-5.0 - h))) for h in range(NH)]
    cdec = [float(np.exp(np.float32(lg[h]) * np.float32(128.0))) for h in range(NH)]

    nc = bass.Bass("TRN2", target_bir_lowering=False)

    def din(name, shape, dt=F32):
        return nc.dram_tensor(name, shape, dt, kind="ExternalInput").ap()

    xm = din("xm", [nt_main * 128, D])
    xp = din("xp", [nt_pre * 128, D])
    csm = din("csm", [nt_main, 128, 256])
    csp = din("csp", [nt_pre, 128, 256])
    w_in = din("w_in", [D, 3584])
    w_out = din("w_out", [D, D])
    w_up = din("w_up", [D, 4096])
    w_down = din("w_down", [4096, D])
    d_g1 = din("g1t", [128, 8])
    d_g2 = din("g2t", [128, 8])
    d_gf = din("gft", [128, D])
    d_cw = din("cwt", [128, 12])
    d_cg = din("cgt", [128, 4])
    d_rg = din("rgt", [128, 4])
    d_zt = din("ztt", [128, 4])
    d_mask = din("maskt", [128, 512])
    d_xi = din("xit", [128, 512])
    d_id = din("ident", [128, 128], BF16)
    d_ones = din("onesblk", [128, 128])
    y = nc.dram_tensor("y", [nt_main * 128, D], F32, kind="ExternalOutput").ap()
    if DEBUG:
        dbg_x1 = nc.dram_tensor("dbg_x1", [nt_main * 128, D], F32, kind="ExternalOutput").ap()
        dbg_mix = nc.dram_tensor("dbg_mix", [nt_main // 4, 128, 4096], F32, kind="ExternalOutput").ap()
    wc_scr = nc.dram_tensor("wc_scr", [3, 128, 4096], BF16, kind="Internal").ap()
    wo_scr = nc.dram_tensor("wo_scr", [2, 128, 4096], BF16, kind="Internal").ap()
    wu_scr = nc.dram_tensor("wu_scr", [8, 128, 4096], BF16, kind="Internal").ap()
    wd_scr = nc.dram_tensor("wd_scr", [8, 128, 4096], BF16, kind="Internal").ap()

    with ExitStack() as st:
        S = Sched(nc)

        def sb(name, shape, dt=F32):
            return st.enter_context(nc.sbuf_tensor("s_" + name, shape, dt))

        wr = sb("wr", [128, 8, 2048], BF16)
        ring = [sb(f"ring{i}", [128, 4096], BF16) for i in range(NRING)]
        xt = [sb(f"xt{i}", [128, D]) for i in range(4)]
        cst = [sb(f"cst{i}", [128, 256]) for i in range(2)]
        ot = [sb(f"ot{i}", [128, D]) for i in range(2)]
        uT = sb("uT", [128, 8, 512], BF16)
        mixT = sb("mixT", [128, 8, 512], BF16)
        hT = sb("hT", [128, 32, 512], BF16)
        chb = sb("chb", [128, 4, 514])
        ctmp = sb("ctmp", [128, 512])
        cyv = sb("cyv", [128, 512])
        csq = sb("csq", [128, 512])
        junk = sb("junk", [128, D])
        xn = sb("xn", [128, D], BF16)
        ss = sb("ss", [128, 1])
        rs = sb("rs", [128, 1])
        rA = sb("rA", [128, 512])
        rB = sb("rB", [128, 512])
        qr = sb("qr", [128, 512], BF16)
        kr = sb("kr", [128, 512], BF16)
        kz = sb("kz", [128, 512], BF16)
        vb = sb("vb", [128, 512], BF16)
        sg = sb("sg", [128, 512])
        qT = sb("qT", [128, 4, 128], BF16)
        qxT = sb("qxT", [128, 4, 128], BF16)
        kT = sb("kT", [128, 4, 128], BF16)
        sc = sb("sc", [128, 4, 128], BF16)
        T32 = sb("T32", [128, 4, 128])
        Sb = sb("Sb", [128, 4, 128], BF16)
        y2 = sb("y2", [128, 512], BF16)
        ssr = sb("ssr", [128, 4])
        rr = sb("rr", [128, 4])
        rl = [sb(f"rl{i}", [128, 512]) for i in range(2)]
        g1t = sb("g1t", [128, 8]); g2t = sb("g2t", [128, 8]); gft = sb("gft", [128, D])
        cwt = sb("cwt", [128, 12]); cgt = sb("cgt", [128, 4]); rgt = sb("rgt", [128, 4])
        ztt = sb("ztt", [128, 4]); maskt = sb("maskt", [128, 4, 128]); xit = sb("xit", [128, 4, 128])
        ident = sb("ident", [128, 128], BF16); onesb = sb("onesb", [128, 128])
        epst = sb("epst", [128, 1])
        pb = [st.enter_context(nc.psum_tensor(f"pb{i}", [128, 512], F32)) for i in range(8)]

        def pbf(i):
            return pb[i][:].bitcast(BF16).rearrange("p (k n) -> p k n", n=128)

        for dst, src, key in ((g1t, d_g1, "g1t"), (g2t, d_g2, "g2t"), (gft, d_gf, "gft"), (cwt, d_cw, "cwt"),
                              (cgt, d_cg, "cgt"), (rgt, d_rg, "rgt"), (ztt, d_zt, "ztt"), (ident, d_id, "ident"),
                              (onesb, d_ones, "onesb")):
            S.dma("sp", lambda e, dst=dst, src=src: e.dma_start(out=dst[:], in_=src), "c_" + key, writes=[key])
        S.dma("sp", lambda e: e.dma_start(out=maskt[:], in_=d_mask.rearrange("p (h n) -> p h n", h=4)), "c_maskt", writes=["maskt"])
        S.dma("sp", lambda e: e.dma_start(out=xit[:], in_=d_xi.rearrange("p (h n) -> p h n", h=4)), "c_xit", writes=["xit"])
        S.op("pool", lambda e: e.memset(epst[:], EPS), writes=["epst"])
        S.op("pool", lambda e: e.memset(T32[:], 0.0), writes=["T32"])
        S.op("pool", lambda e: e.memset(Sb[:], 0.0), writes=["Sb"])
        S.op("pool", lambda e: e.memset(chb[:], 0.0), writes=["chb0", "chb1", "chb2", "chb3"])
        stg = [hT[:, 16 * i:16 * (i + 1), :].rearrange("p a b -> p (a b)").bitcast(F32) for i in range(2)]
        stgk = [[f"hT{j}" for j in range(16 * i, 16 * (i + 1))] for i in range(2)]
        groups = []
        for gi in (1, 2, 0, 3):
            groups.append((w_in[:, 1536 + gi * 512:1536 + (gi + 1) * 512].rearrange("(k p) c -> p k c", p=128), 8,
                           wr[:, :, gi * 512:(gi + 1) * 512], None, f"wr{gi}"))
        for si, col0 in ((0, 1024), (1, 512), (2, 0)):
            groups.append((w_in[:, col0:col0 + 512].rearrange("(k p) c -> p k c", p=128), 8, None, wc_scr[si], f"wc_scr{si}"))
        for hf in range(2):
            groups.append((w_out[:, hf * 512:(hf + 1) * 512].rearrange("(k p) c -> p k c", p=128), 8, None, wo_scr[hf], f"wo_scr{hf}"))
        for g in range(8):
            groups.append((w_up[:, g * 512:(g + 1) * 512].rearrange("(k p) c -> p k c", p=128), 8, None, wu_scr[g], f"wu_scr{g}"))
        for g in range(8):
            groups.append((w_down[g * 512:(g + 1) * 512, :].rearrange("(j p) c -> p j c", p=128), 4, None, wd_scr[g], f"wd_scr{g}"))

        def cast_load(i):
            src, a, dsb, scr, key = groups[i]
            si = i % 2
            S.dma("sp", lambda e: e.dma_start(out=stg[si].rearrange("p (a b) -> p a b", a=a), in_=src), f"stg{si}", writes=stgk[si])

        def cast_do(i):
            src, a, dsb, scr, key = groups[i]
            si = i % 2
            eng = "act" if i % 2 == 0 else "dve"
            if dsb is not None:
                dst, wk = dsb, [key]
                src_v = stg[si].rearrange("p (a b) -> p a b", a=a)
            else:
                slot = i % NRING
                dst, wk = ring[slot][:], [f"ring{slot}"]
                src_v = stg[si]
            if eng == "act":
                S.op("act", lambda e: e.activation(out=dst, in_=src_v, func=AF.Copy), reads=stgk[si], writes=wk)
            else:
                S.op("dve", lambda e: e.tensor_copy(out=dst, in_=src_v), reads=stgk[si], writes=wk)
            if scr is not None:
                S.dma("sp", lambda e: e.dma_start(out=scr, in_=dst), "k_" + key, reads=wk, writes=[key])

        cast_state = dict(loaded=0, done=0)

        def cast_step():
            if cast_state["loaded"] < len(groups) and cast_state["loaded"] <= cast_state["done"] + 1:
                cast_load(cast_state["loaded"])
                cast_state["loaded"] += 1
            if cast_state["loaded"] < len(groups) and cast_state["loaded"] <= cast_state["done"] + 1:
                cast_load(cast_state["loaded"])
                cast_state["loaded"] += 1
            if cast_state["done"] < cast_state["loaded"]:
                cast_do(cast_state["done"])
                cast_state["done"] += 1

        n_upfront = 4 if nt_pre >= 28 else len(groups)
        while cast_state["done"] < n_upfront:
            cast_step()

        items = []
        items += [(wc_scr[0], "wc_scr0"), (wc_scr[1], "wc_scr1")]
        blk_base = []
        for b in range(nblk):
            blk_base.append(len(items))
            items += [(wc_scr[0], "wc_scr0"), (wc_scr[1], "wc_scr1"), (wc_scr[2], "wc_scr2")]
            items += [(wo_scr[0], "wo_scr0"), (wo_scr[1], "wo_scr1")]
            items += [(wu_scr[g], f"wu_scr{g}") for g in range(8)]
            items += [(wd_scr[g], f"wd_scr{g}") for g in range(8)]
        rstate = dict(issued=0, released=set())

        def ring_pump(upto):
            while rstate["issued"] < len(items):
                i = rstate["issued"]
                if i >= NRING and (i - NRING) not in rstate["released"]:
                    assert i > upto, f"ring deadlock at item {i}"
                    break
                if i > upto + NRING - 1:
                    break
                src, skey = items[i]
                slot = i % NRING
                S.dma("sp", lambda e, slot=slot, src=src: e.dma_start(out=ring[slot][:], in_=src), f"ring{slot}",
                      reads=[skey], writes=[f"ring{slot}"])
                rstate["issued"] += 1

        def ring_get(i):
            ring_pump(i)
            assert rstate["issued"] > i
            return i % NRING

        def ring_release(i):
            rstate["released"].add(i)
            ring_pump(i)

        def r8(slot):
            return ring[slot][:].rearrange("p (k c) -> p k c", k=8)

        def r4(slot):
            return ring[slot][:].rearrange("p (j c) -> p j c", j=4)

        def load_tile(xsrc, cssrc, t, xs, cs):
            S.dma("sp", lambda e: e.dma_start(out=xt[xs][:], in_=xsrc[t * 128:(t + 1) * 128, :]), f"x{xs}", writes=[f"xt{xs}"])
            if cssrc is not None:
                S.dma("sp", lambda e: e.dma_start(out=cst[cs][:], in_=cssrc[t]), f"cs{cs}", writes=[f"cst{cs}"])

        def rstd_small(src, dst, n, srck, dstk):
            S.op("dve", lambda e: e.tensor_scalar(out=dst, in0=src, scalar1=1.0 / n, scalar2=EPS, op0=ALU.mult, op1=ALU.add),
                 reads=[srck], writes=[dstk])
            S.op("act", lambda e: e.activation(out=dst, in_=dst, func=AF.Sqrt), reads=[dstk], writes=[dstk])
            S.op("dve", lambda e: e.reciprocal(out=dst, in_=dst), reads=[dstk], writes=[dstk])

        def norm_to_uT(xs, gt, gk, tcol):
            xk = f"xt{xs}"
            S.op("act", lambda e: e.activation(out=junk[:], in_=xt[xs][:], func=AF.Square, accum_out=ss[:]),
                 reads=[xk], writes=["junk", "ss"])
            rstd_small(ss[:], rs[:], D, "ss", "rs")
            S.op("act", lambda e: e.activation(out=xn[:], in_=xt[xs][:], func=AF.Copy, scale=rs[:]),
                 reads=[xk, "rs"], writes=["xn"])
            for k in range(8):
                S.op("pe", lambda e, k=k: e.transpose(out=pbf(0)[:, k, :], in_=xn[:, k * 128:(k + 1) * 128], identity=ident[:]),
                     reads=["xn", "ident"], writes=["pb0"])
            S.op("dve", lambda e: e.tensor_tensor(out=uT[:, :, tcol:tcol + 128], in0=pbf(0),
                                                  in1=gt[:].unsqueeze(2).broadcast_to([128, 8, 128]), op=ALU.mult),
                 reads=["pb0", gk], writes=[f"uT{tcol // 128}"])

        def tm_proj(gi, bank, tcol):
            for k in range(8):
                S.op("pe", lambda e, k=k: e.matmul(pb[bank][:], lhsT=uT[:, k, tcol:tcol + 128], rhs=wr[:, k, gi * 512:(gi + 1) * 512],
                                                   start=(k == 0), stop=(k == 7)),
                     reads=[f"uT{tcol // 128}", f"wr{gi}"], writes=[f"pb{bank}"])

        def rotary(bank, cs, dst, dstk):
            pv = pb[bank][:].rearrange("p (h two j) -> p h two j", two=2, j=64)
            psw = pv[:, :, ::-1, :]
            cc = cst[cs][:, 0:128].rearrange("p (two j) -> p two j", two=2).unsqueeze(1).broadcast_to([128, 4, 2, 64])
            s_ = cst[cs][:, 128:256].rearrange("p (two j) -> p two j", two=2).unsqueeze(1).broadcast_to([128, 4, 2, 64])
            a4 = rA[:].rearrange("p (h two j) -> p h two j", two=2, j=64)
            b4 = rB[:].rearrange("p (h two j) -> p h two j", two=2, j=64)
            S.op("dve", lambda e: e.tensor_tensor(out=a4, in0=pv, in1=cc, op=ALU.mult), reads=[f"pb{bank}", f"cst{cs}"], writes=["rA"])
            S.op("dve", lambda e: e.tensor_tensor(out=b4, in0=psw, in1=s_, op=ALU.mult), reads=[f"pb{bank}", f"cst{cs}"], writes=["rB"])
            S.op("pool", lambda e: e.tensor_tensor(out=dst[:], in0=rA[:], in1=rB[:], op=ALU.add), reads=["rA", "rB"], writes=[dstk])

        def kv_state(cs):
            S.op("pool", lambda e: e.tensor_tensor(out=kz[:].rearrange("p (h n) -> p h n", h=4),
                                                   in0=kr[:].rearrange("p (h n) -> p h n", h=4),
                                                   in1=ztt[:].unsqueeze(2).broadcast_to([128, 4, 128]), op=ALU.mult),
                 reads=["kr", "ztt"], writes=["kz"])
            for h in range(NH):
                S.op("pe", lambda e, h=h: e.matmul(pb[1][:, h * 128:(h + 1) * 128], lhsT=kz[:, h * 128:(h + 1) * 128],
                                                   rhs=vb[:, h * 128:(h + 1) * 128], start=True, stop=True),
                     reads=["kz", "vb"], writes=["pb1"])
            for h in range(NH):
                S.op("dve", lambda e, h=h: e.scalar_tensor_tensor(out=T32[:, h, :], in0=T32[:, h, :], scalar=cdec[h],
                                                                  in1=pb[1][:, h * 128:(h + 1) * 128], op0=ALU.mult, op1=ALU.add),
                     reads=["T32", "pb1"], writes=["T32"])
            S.op("pool", lambda e: e.tensor_copy(out=Sb[:], in_=T32[:]), reads=["T32"], writes=["Sb"])

        def conv_front(n, rhs_of_k, rkeys, it_h, it_c):
            sl = ring_get(it_h)
            for c in range(4):
                bank = 2 + c
                for k in range(8):
                    S.op("pe", lambda e, k=k, c=c, bank=bank, sl=sl: e.matmul(pb[bank][:, 0:n], lhsT=r8(sl)[:, k, c * 128:(c + 1) * 128],
                                                                             rhs=rhs_of_k(k), start=(k == 0), stop=(k == 7)),
                         reads=rkeys + [f"ring{sl}"], writes=[f"pb{bank}"])
                S.op("act", lambda e, c=c, bank=bank: e.activation(out=chb[:, c, 2:2 + n], in_=pb[bank][:, 0:n], func=AF.Copy),
                     reads=[f"pb{bank}"], writes=[f"chb{c}"])
            ring_release(it_h)
            sl = ring_get(it_c)
            for c in range(4):
                bank = 2 + c
                for k in range(8):
                    S.op("pe", lambda e, k=k, c=c, bank=bank, sl=sl: e.matmul(pb[bank][:, 0:n], lhsT=r8(sl)[:, k, c * 128:(c + 1) * 128],
                                                                             rhs=rhs_of_k(k), start=(k == 0), stop=(k == 7)),
                         reads=rkeys + [f"ring{sl}"], writes=[f"pb{bank}"])
                S.op("dve", lambda e, c=c, bank=bank: e.tensor_tensor(out=chb[:, c, 2:2 + n], in0=pb[bank][:, 0:n], in1=chb[:, c, 2:2 + n], op=ALU.mult),
                     reads=[f"pb{bank}", f"chb{c}"], writes=[f"chb{c}"])
            ring_release(it_c)

        for t in range(nt_pre if STOP != "casts" else 0):
            xs, cs = t % 4, t % 2
            if cast_state["done"] < len(groups):
                cast_step()
            load_tile(xp, csp, t, xs, cs)
            norm_to_uT(xs, g1t, "g1t", 0)
            tm_proj(1, 3, 0)
            tm_proj(2, 4, 0)
            rotary(3, cs, kr, "kr")
            S.op("act", lambda e: e.activation(out=vb[:], in_=pb[4][:], func=AF.Copy), reads=["pb4"], writes=["vb"])
            kv_state(cs)
            if t == nt_pre - 1:
                conv_front(128, lambda k: uT[:, k, 0:128], ["uT0"], 0, 1)
                for c in range(4):
                    S.op("pool", lambda e, c=c: e.tensor_copy(out=chb[:, c, 0:2], in_=chb[:, c, 128:130]),
                         reads=[f"chb{c}"], writes=[f"chb{c}"])

        while cast_state["done"] < len(groups):
            cast_step()
        ot_i = 0
        for b in range(nblk if STOP not in ("casts", "prefix") else 0):
            base = blk_base[b]
            for tt in range(4):
                t = b * 4 + tt
                load_tile(xm, None, t, tt, 0)
                norm_to_uT(tt, g1t, "g1t", tt * 128)
            ukeys = ["uT0", "uT1", "uT2", "uT3"]
            conv_front(512, lambda k: uT[:, k, :], ukeys, base + 0, base + 1)
            slB = ring_get(base + 2)
            for c in range(4):
                bank = 2 + c
                for k in range(8):
                    S.op("pe", lambda e, k=k, c=c, bank=bank, slB=slB: e.matmul(pb[bank][:], lhsT=r8(slB)[:, k, c * 128:(c + 1) * 128],
                                                                      rhs=uT[:, k, :], start=(k == 0), stop=(k == 7)),
                         reads=ukeys + [f"ring{slB}"], writes=[f"pb{bank}"])
                ck = f"chb{c}"
                S.op("act", lambda e, c=c: e.activation(out=ctmp[:], in_=chb[:, c, 0:512], func=AF.Copy, scale=cwt[:, c * 3:c * 3 + 1]),
                     reads=[ck, "cwt"], writes=["ctmp"])
                S.op("dve", lambda e, c=c: e.scalar_tensor_tensor(out=ctmp[:], in0=chb[:, c, 1:513], scalar=cwt[:, c * 3 + 1:c * 3 + 2],
                                                                  in1=ctmp[:], op0=ALU.mult, op1=ALU.add),
                     reads=[ck, "cwt", "ctmp"], writes=["ctmp"])
                S.op("dve", lambda e, c=c: e.scalar_tensor_tensor(out=ctmp[:], in0=chb[:, c, 2:514], scalar=cwt[:, c * 3 + 2:c * 3 + 3],
                                                                  in1=ctmp[:], op0=ALU.mult, op1=ALU.add),
                     reads=[ck, "cwt", "ctmp"], writes=["ctmp"])
                S.op("pool", lambda e, c=c: e.tensor_copy(out=chb[:, c, 0:2], in_=chb[:, c, 512:514]), reads=[ck], writes=[ck])
                S.op("dve", lambda e, bank=bank: e.tensor_tensor(out=cyv[:], in0=pb[bank][:], in1=ctmp[:], op=ALU.mult),
                     reads=[f"pb{bank}", "ctmp"], writes=["cyv"])
                S.op("act", lambda e: e.activation(out=csq[:], in_=cyv[:], func=AF.Square), reads=["cyv"], writes=["csq"])
                S.op("pe", lambda e: e.matmul(pb[6][:], lhsT=onesb[:], rhs=csq[:], start=True, stop=True),
                     reads=["onesb", "csq"], writes=["pb6"])
                S.op("act", lambda e: e.activation(out=csq[:], in_=pb[6][:], func=AF.Sqrt, bias=epst[:], scale=1.0),
                     reads=["pb6", "epst"], writes=["csq"])
                S.op("dve", lambda e: e.reciprocal(out=csq[:], in_=csq[:]), reads=["csq"], writes=["csq"])
                S.op("dve", lambda e, c=c: e.scalar_tensor_tensor(out=mixT[:, c, :], in0=cyv[:], scalar=cgt[:, c:c + 1], in1=csq[:],
                                                                  op0=ALU.mult, op1=ALU.mult),
                     reads=["cyv", "csq", "cgt"], writes=[f"mixc{c}"])
            ring_release(base + 2)
            for tt in range(4):
                t = b * 4 + tt
                cs = t % 2
                tc0 = tt * 128
                S.dma("sp", lambda e, t=t, cs=cs: e.dma_start(out=cst[cs][:], in_=csm[t]), f"cs{cs}", writes=[f"cst{cs}"])
                for gi in range(4):
                    tm_proj(gi, 2 + gi, tc0)
                rotary(2, cs, qr, "qr")
                rotary(3, cs, kr, "kr")
                S.op("act", lambda e: e.activation(out=vb[:], in_=pb[4][:], func=AF.Copy), reads=["pb4"], writes=["vb"])
                S.op("act", lambda e: e.activation(out=sg[:], in_=pb[5][:], func=AF.Silu), reads=["pb5"], writes=["sg"])
                for h in range(NH):
                    S.op("pe", lambda e, h=h: e.transpose(out=pbf(0)[:, h, :], in_=qr[:, h * 128:(h + 1) * 128], identity=ident[:]),
                         reads=["qr", "ident"], writes=["pb0"])
                S.op("act", lambda e: e.activation(out=qT[:], in_=pbf(0)[:, 0:4, :], func=AF.Copy), reads=["pb0"], writes=["qT"])
                S.op("dve", lambda e: e.tensor_tensor(out=qxT[:], in0=pbf(0)[:, 0:4, :], in1=xit[:], op=ALU.mult),
                     reads=["pb0", "xit"], writes=["qxT"])
                for h in range(NH):
                    S.op("pe", lambda e, h=h: e.transpose(out=pbf(7)[:, h, :], in_=kr[:, h * 128:(h + 1) * 128], identity=ident[:]),
                         reads=["kr", "ident"], writes=["pb7"])
                S.op("act", lambda e: e.activation(out=kT[:], in_=pbf(7)[:, 0:4, :], func=AF.Copy), reads=["pb7"], writes=["kT"])
                for h in range(NH):
                    S.op("pe", lambda e, h=h: e.matmul(pb[6][:, h * 128:(h + 1) * 128], lhsT=kT[:, h, :], rhs=qT[:, h, :], start=True, stop=True),
                         reads=["kT", "qT"], writes=["pb6"])
                S.op("dve", lambda e: e.tensor_tensor(out=sc[:], in0=pb[6][:].rearrange("p (h n) -> p h n", h=4), in1=maskt[:], op=ALU.mult),
                     reads=["pb6", "maskt"], writes=["sc"])
                for h in range(NH):
                    S.op("pe", lambda e, h=h: e.matmul(pb[2][:, h * 128:(h + 1) * 128], lhsT=sc[:, h, :], rhs=vb[:, h * 128:(h + 1) * 128],
                                                       start=True, stop=False), reads=["sc", "vb"], writes=["pb2"])
                    S.op("pe", lambda e, h=h: e.matmul(pb[2][:, h * 128:(h + 1) * 128], lhsT=qxT[:, h, :], rhs=Sb[:, h, :],
                                                       start=False, stop=True), reads=["qxT", "Sb"], writes=["pb2"])
                kv_state(cs)
                for h in range(NH):
                    S.op("act", lambda e, h=h: e.activation(out=junk[:, 0:128], in_=pb[2][:, h * 128:(h + 1) * 128], func=AF.Square,
                                                            accum_out=ssr[:, h:h + 1]), reads=["pb2"], writes=["junk", "ssr"])
                rstd_small(ssr[:], rr[:], 128, "ssr", "rr")
                for h in range(NH):
                    S.op("dve", lambda e, h=h: e.scalar_tensor_tensor(out=y2[:, h * 128:(h + 1) * 128], in0=pb[2][:, h * 128:(h + 1) * 128],
                                                                      scalar=rr[:, h:h + 1], in1=sg[:, h * 128:(h + 1) * 128],
                                                                      op0=ALU.mult, op1=ALU.mult),
                         reads=["pb2", "rr", "sg"], writes=["y2"])
                for h in range(NH):
                    S.op("pe", lambda e, h=h: e.transpose(out=pbf(0)[:, h, :], in_=y2[:, h * 128:(h + 1) * 128], identity=ident[:]),
                         reads=["y2", "ident"], writes=["pb0"])
                for h in range(NH):
                    S.op("act", lambda e, h=h, tc0=tc0: e.activation(out=mixT[:, 4 + h, tc0:tc0 + 128], in_=pbf(0)[:, h, :], func=AF.Copy,
                                                                     scale=rgt[:, h:h + 1]),
                         reads=["pb0", "rgt"], writes=[f"mixr{tt}"])
            slo = [ring_get(base + 3), ring_get(base + 4)]
            mkeys = ["mixc0", "mixc1", "mixc2", "mixc3"]
            for tt in range(4):
                tc0 = tt * 128
                for hf in range(2):
                    bank = 3 + hf
                    for c in range(8):
                        S.op("pe", lambda e, c=c, hf=hf, bank=bank, tc0=tc0, so=slo[hf]: e.matmul(pb[bank][:], lhsT=mixT[:, c, tc0:tc0 + 128],
                                                                                     rhs=r8(so)[:, c, :], start=(c == 0), stop=(c == 7)),
                             reads=mkeys + [f"mixr{tt}", f"ring{slo[hf]}"], writes=[f"pb{bank}"])
                    S.op("dve", lambda e, hf=hf, bank=bank, tt=tt: e.tensor_tensor(out=xt[tt][:, hf * 512:(hf + 1) * 512], in0=pb[bank][:],
                                                                                  in1=xt[tt][:, hf * 512:(hf + 1) * 512], op=ALU.add),
                         reads=[f"pb{bank}", f"xt{tt}"], writes=[f"xt{tt}"])
                if DEBUG:
                    S.dma("sp", lambda e, tt=tt, t=b * 4 + tt: e.dma_start(out=dbg_x1[t * 128:(t + 1) * 128, :], in_=xt[tt][:]), f"dbgx{tt}", reads=[f"xt{tt}"])
                norm_to_uT(tt, g2t, "g2t", tc0)
            if DEBUG:
                S.dma("pool", lambda e, b=b: e.dma_start(out=dbg_mix[b], in_=mixT[:].rearrange("p k n -> p (k n)")), "dbgm", reads=mkeys + ["mixr0", "mixr1", "mixr2", "mixr3"])
            ring_release(base + 3)
            ring_release(base + 4)
            for g in range(8):
                sl = ring_get(base + 5 + g)
                for jj in range(4):
                    j = g * 4 + jj
                    bank = j % 8
                    for k in range(8):
                        S.op("pe", lambda e, k=k, jj=jj, bank=bank, sl=sl: e.matmul(pb[bank][:], lhsT=r8(sl)[:, k, jj * 128:(jj + 1) * 128],
                                                                                   rhs=uT[:, k, :], start=(k == 0), stop=(k == 7)),
                             reads=ukeys + [f"ring{sl}"], writes=[f"pb{bank}"])
                    ri = j % 2
                    if j % 2 == 0:
                        S.op("act", lambda e, bank=bank, ri=ri: e.activation(out=rl[ri][:], in_=pb[bank][:], func=AF.Relu),
                             reads=[f"pb{bank}"], writes=[f"rl{ri}"])
                    else:
                        S.op("dve", lambda e, bank=bank, ri=ri: e.tensor_scalar(out=rl[ri][:], in0=pb[bank][:], scalar1=0.0, scalar2=None, op0=ALU.max),
                             reads=[f"pb{bank}"], writes=[f"rl{ri}"])
                    S.op("pool", lambda e, j=j, ri=ri: e.tensor_tensor(out=hT[:, j, :], in0=rl[ri][:], in1=rl[ri][:], op=ALU.mult),
                         reads=[f"rl{ri}"], writes=[f"hT{j}"])
                ring_release(base + 5 + g)
            for g in range(8):
                sl = ring_get(base + 13 + g)
                for tt in range(4):
                    for hf in range(2):
                        bank = tt * 2 + hf
                        for jj in range(4):
                            j = g * 4 + jj
                            S.op("pe", lambda e, jj=jj, j=j, hf=hf, bank=bank, tt=tt, sl=sl: e.matmul(
                                pb[bank][:], lhsT=hT[:, j, tt * 128:(tt + 1) * 128], rhs=r4(sl)[:, jj, hf * 512:(hf + 1) * 512],
                                start=(j == 0), stop=(j == 31)),
                                reads=[f"hT{j}", f"ring{sl}"], writes=[f"pb{bank}"])
                ring_release(base + 13 + g)
            for tt in range(4):
                t = b * 4 + tt
                xk = f"xt{tt}"
                for hf in range(2):
                    bank = tt * 2 + hf
                    S.op("dve", lambda e, hf=hf, bank=bank, tt=tt: e.tensor_tensor(out=xt[tt][:, hf * 512:(hf + 1) * 512], in0=pb[bank][:],
                                                                                  in1=xt[tt][:, hf * 512:(hf + 1) * 512], op=ALU.add),
                         reads=[f"pb{bank}", xk], writes=[xk])
                S.op("act", lambda e, tt=tt: e.activation(out=junk[:], in_=xt[tt][:], func=AF.Square, accum_out=ss[:]),
                     reads=[xk], writes=["junk", "ss"])
                rstd_small(ss[:], rs[:], D, "ss", "rs")
                S.op("act", lambda e, tt=tt: e.activation(out=xt[tt][:], in_=xt[tt][:], func=AF.Copy, scale=rs[:]),
                     reads=[xk, "rs"], writes=[xk])
                oi = ot_i % 2
                ot_i += 1
                S.op("pool", lambda e, tt=tt, oi=oi: e.tensor_tensor(out=ot[oi][:], in0=xt[tt][:], in1=gft[:], op=ALU.mult),
                     reads=[xk, "gft"], writes=[f"ot{oi}"])
                S.dma("pool", lambda e, t=t, oi=oi: e.dma_start(out=y[t * 128:(t + 1) * 128, :], in_=ot[oi][:]), f"out{oi}", reads=[f"ot{oi}"])
        S.emit(st)
    return nc


def _host_consts():
    f32 = np.float32
    lg = np.log(f32(1.0) - f32(2.0) ** (f32(-5.0) - np.arange(NH, dtype=f32))).astype(f32)
    idx = np.arange(128, dtype=f32)
    diff = idx[:, None] - idx[None, :]
    intra = np.where(diff[None] >= 0, np.exp(lg[:, None, None] * np.maximum(diff, 0.0)[None]), 0.0).astype(f32)
    scale = f32(128.0 ** -0.5)
    maskT = np.ascontiguousarray(np.transpose(intra, (2, 0, 1)) * scale).astype(f32).reshape(128, 512)
    zeta = np.exp(lg[:, None] * (f32(127.0) - idx)[None]).astype(f32)
    ztt = np.ascontiguousarray(zeta.T * scale).astype(f32)
    xi = np.exp(lg[:, None] * (idx + f32(1.0))[None]).astype(f32)
    xit = np.ascontiguousarray(np.broadcast_to(xi.reshape(1, 512), (128, 512))).astype(f32)
    ident = np.eye(128, dtype=f32).astype(ml_dtypes.bfloat16)
    onesblk = np.zeros((128, 128), f32)
    onesblk[:64, :64] = 1.0 / 64
    onesblk[64:, 64:] = 1.0 / 64
    return dict(maskt=maskT, ztt=ztt, xit=xit, ident=ident, onesblk=onesblk)


def _cs_table(pos0, ntiles):
    f32 = np.float32
    half = 64
    inv_freq = (f32(1.0) / (f32(10000.0) ** (np.arange(half, dtype=f32) / f32(half)))).astype(f32)
    pos = (pos0 + np.arange(ntiles * 128)).astype(f32)
    ang = (pos[:, None] * inv_freq[None, :]).astype(f32)
    c = np.cos(ang).astype(f32)
    s = np.sin(ang).astype(f32)
    tab = np.concatenate([c, c, -s, s], axis=1).astype(f32)
    return np.ascontiguousarray(tab.reshape(ntiles, 128, 256))


_NC_CACHE = {}


def kernel(x, norm1_g, w_in, conv_w, conv_norm_g, ret_norm_g, w_out, norm2_g, w_up, w_down, final_norm_g):
    x = np.asarray(x, dtype=np.float32)
    B, SEQ, _ = x.shape
    half = SEQ // 2
    nt = half // 128
    key = (nt, nt)
    if key not in _NC_CACHE:
        _NC_CACHE[key] = build_nc(nt, nt)
    nc = _NC_CACHE[key]
    f32 = np.float32
    consts = _host_consts()
    shared = dict(
        w_in=np.ascontiguousarray(w_in, dtype=f32), w_out=np.ascontiguousarray(w_out, dtype=f32),
        w_up=np.ascontiguousarray(w_up, dtype=f32), w_down=np.ascontiguousarray(w_down, dtype=f32),
        g1t=np.ascontiguousarray(np.asarray(norm1_g, f32).reshape(8, 128).T),
        g2t=np.ascontiguousarray(np.asarray(norm2_g, f32).reshape(8, 128).T),
        gft=np.ascontiguousarray(np.broadcast_to(np.asarray(final_norm_g, f32)[None, :], (128, D))),
        cwt=np.ascontiguousarray(np.asarray(conv_w, f32).reshape(3, 4, 128).transpose(2, 1, 0).reshape(128, 12)),
        cgt=np.ascontiguousarray(np.asarray(conv_norm_g, f32).reshape(4, 128).T),
        rgt=np.ascontiguousarray(np.asarray(ret_norm_g, f32).reshape(4, 128).T),
        **consts,
    )
    cs_lo = _cs_table(0, nt)
    cs_hi = _cs_table(half, nt)
    in_maps = []
    for c in range(8):
        b, hf = c // 2, c % 2
        m = dict(shared)
        m["xm"] = np.ascontiguousarray(x[b, hf * half:(hf + 1) * half])
        m["xp"] = np.ascontiguousarray(x[b, 0:half]) if hf == 1 else np.zeros((half, D), f32)
        m["csm"] = cs_hi if hf == 1 else cs_lo
        m["csp"] = cs_lo
        in_maps.append(m)
    res = run_bass_kernel_spmd(nc, in_maps, core_ids=list(range(8)))
    if DEBUG:
        kernel.dbg = [dict(r) for r in res.results]
    out = np.empty((B, SEQ, D), f32)
    for c in range(8):
        b, hf = c // 2, c % 2
        out[b, hf * half:(hf + 1) * half] = res.results[c]["y"]
    return out
```

```python
import numpy as np
import ml_dtypes
from contextlib import ExitStack
import concourse.bass as bass
import concourse.mybir as mybir
from concourse.bass_utils import run_bass_kernel_spmd

F32 = mybir.dt.float32
BF16 = mybir.dt.bfloat16
AF = mybir.ActivationFunctionType
ALU = mybir.AluOpType

D = 1024
NH = 4
EPS = 1e-6
NRING = 4
DEBUG = False
STOP = None


class Sched:
    ENGS = ("pe", "act", "dve", "pool", "sp")

    def __init__(self, nc):
        self.nc = nc
        self.ops = []
        self.last_w = {}
        self.readers = {}

    def _deps(self, reads, writes):
        deps = {}
        for k in reads:
            w = self.last_w.get(k)
            if w is not None:
                deps[w] = True
        for k in writes:
            w = self.last_w.get(k)
            if w is not None:
                deps.setdefault(w, False)
            for r in self.readers.get(k, ()):
                deps.setdefault(r, False)
        return deps

    def _commit(self, idx, reads, writes):
        for k in reads:
            self.readers.setdefault(k, []).append(idx)
        for k in writes:
            self.last_w[k] = idx
            self.readers[k] = []

    def op(self, eng, fn, reads=(), writes=()):
        writes = list(writes) + [k for k in reads if k.startswith("pb") and k not in writes]
        deps = self._deps(reads, writes)
        idx = len(self.ops)
        self.ops.append(dict(eng=eng, fn=fn, deps=deps, kind="c", sig=False))
        self._commit(idx, reads, writes)
        return idx

    def dma(self, eng, fn, sem, reads=(), writes=()):
        deps = self._deps(reads, writes)
        idx = len(self.ops)
        self.ops.append(dict(eng=eng, fn=fn, deps=deps, kind="d", sem=sem))
        self._commit(idx, reads, writes)
        return idx

    def emit(self, stack):
        nc = self.nc
        ops = self.ops

        def skip(p, o):
            return (p["kind"] == "c" and o["kind"] == "c" and p["eng"] == o["eng"])

        for o in ops:
            for d, raw in o["deps"].items():
                p = ops[d]
                if p["kind"] == "d":
                    continue
                if skip(p, o) and (p["eng"] == "pe" or not raw):
                    continue
                p["sig"] = True
        esem = {e: stack.enter_context(nc.semaphore("s_" + e)) for e in ("pe", "act", "dve", "pool")}
        dsem, dcnt = {}, {}
        cnt = {e: 0 for e in esem}
        for o in ops:
            if o["kind"] == "c":
                if o["sig"]:
                    cnt[o["eng"]] += 1
                    o["val"] = cnt[o["eng"]]
            else:
                s = o["sem"]
                if s not in dsem:
                    dsem[s] = stack.enter_context(nc.semaphore("d_" + s))
                    dcnt[s] = 0
                dcnt[s] += 16
                o["val"] = dcnt[s]
        per_eng = {e: [] for e in self.ENGS}
        for i, o in enumerate(ops):
            per_eng[o["eng"]].append(i)
        seen = {e: {} for e in self.ENGS}

        def run(eng_name, engine):
            sn = seen[eng_name]
            for i in per_eng[eng_name]:
                o = ops[i]
                waits = {}
                for d, raw in o["deps"].items():
                    p = ops[d]
                    if p["kind"] == "d":
                        key, h = ("d", p["sem"]), dsem[p["sem"]]
                    else:
                        if skip(p, o) and (eng_name == "pe" or not raw):
                            continue
                        key, h = ("c", p["eng"]), esem[p["eng"]]
                    v = p["val"]
                    if key not in waits or waits[key][1] < v:
                        waits[key] = (h, v)
                for key, (h, v) in waits.items():
                    if sn.get(key, 0) >= v:
                        continue
                    sn[key] = v
                    engine.wait_ge(h, v)
                ins = o["fn"](engine)
                if o["kind"] == "d":
                    ins.then_inc(dsem[o["sem"]], 16)
                elif o["sig"]:
                    ins.then_inc(esem[o["eng"]], 1)
            if eng_name == "sp":
                for s in dsem:
                    engine.wait_ge(dsem[s], dcnt[s])

        block = stack.enter_context(nc.Block())
        block.tensor(lambda e: run("pe", e))
        block.scalar(lambda e: run("act", e))
        block.vector(lambda e: run("dve", e))
        block.gpsimd(lambda e: run("pool", e))
        block.sync(lambda e: run("sp", e))


def build_nc(nt_main=32, nt_pre=32):
    assert nt_main % 4 == 0
    nblk = nt_main // 4
    lg = [float(np.log(np.float32(1.0) - np.float32(2.0) ** np.float32(-5.0 - h))) for h in range(NH)]
    cdec = [float(np.exp(np.float32(lg[h]) * np.float32(128.0))) for h in range(NH)]

    nc = bass.Bass("TRN2", target_bir_lowering=False)

    def din(name, shape, dt=F32):
        return nc.dram_tensor(name, shape, dt, kind="ExternalInput").ap()

    xm = din("xm", [nt_main * 128, D])
    xp = din("xp", [nt_pre * 128, D])
    csm = din("csm", [nt_main, 128, 256])
    csp = din("csp", [nt_pre, 128, 256])
    w_in = din("w_in", [D, 3584])
    w_out = din("w_out", [D, D])
    w_up = din("w_up", [D, 4096])
    w_down = din("w_down", [4096, D])
    d_g1 = din("g1t", [128, 8])
    d_g2 = din("g2t", [128, 8])
    d_gf = din("gft", [128, D])
    d_cw = din("cwt", [128, 12])
    d_cg = din("cgt", [128, 4])
    d_rg = din("rgt", [128, 4])
    d_zt = din("ztt", [128, 4])
    d_mask = din("maskt", [128, 512])
    d_xi = din("xit", [128, 512])
    d_id = din("ident", [128, 128], BF16)
    d_ones = din("onesblk", [128, 128])
    y = nc.dram_tensor("y", [nt_main * 128, D], F32, kind="ExternalOutput").ap()
    if DEBUG:
        dbg_x1 = nc.dram_tensor("dbg_x1", [nt_main * 128, D], F32, kind="ExternalOutput").ap()
        dbg_mix = nc.dram_tensor("dbg_mix", [nt_main // 4, 128, 4096], F32, kind="ExternalOutput").ap()
    wc_scr = nc.dram_tensor("wc_scr", [3, 128, 4096], BF16, kind="Internal").ap()
    wo_scr = nc.dram_tensor("wo_scr", [2, 128, 4096], BF16, kind="Internal").ap()
    wu_scr = nc.dram_tensor("wu_scr", [8, 128, 4096], BF16, kind="Internal").ap()
    wd_scr = nc.dram_tensor("wd_scr", [8, 128, 4096], BF16, kind="Internal").ap()

    with ExitStack() as st:
        S = Sched(nc)

        def sb(name, shape, dt=F32):
            return st.enter_context(nc.sbuf_tensor("s_" + name, shape, dt))

        wr = sb("wr", [128, 8, 2048], BF16)
        ring = [sb(f"ring{i}", [128, 4096], BF16) for i in range(NRING)]
        xt = [sb(f"xt{i}", [128, D]) for i in range(8)]
        cst = [sb(f"cst{i}", [128, 256]) for i in range(2)]
        uT = sb("uT", [128, 8, 512], BF16)
        u2T = sb("u2T", [128, 8, 512], BF16)
        mixT = sb("mixT", [128, 8, 512], BF16)
        hT = sb("hT", [128, 32, 512], BF16)
        chb = sb("chb", [128, 4, 514])
        ctmp = sb("ctmp", [128, 512])
        cyv = sb("cyv", [128, 512])
        csq = sb("csq", [128, 512])
        junk = sb("junk", [128, D])
        xn = sb("xn", [128, D], BF16)
        ss = sb("ss", [128, 1])
        rs = sb("rs", [128, 1])
        rA = sb("rA", [128, 512])
        rB = sb("rB", [128, 512])
        qr = sb("qr", [128, 512], BF16)
        kr = sb("kr", [128, 512], BF16)
        kz = sb("kz", [128, 512], BF16)
        vb = sb("vb", [128, 512], BF16)
        sg = sb("sg", [128, 512])
        qT = sb("qT", [128, 4, 128], BF16)
        qxT = sb("qxT", [128, 4, 128], BF16)
        kT = sb("kT", [128, 4, 128], BF16)
        sc = sb("sc", [128, 4, 128], BF16)
        T32 = sb("T32", [128, 4, 128])
        Sb = sb("Sb", [128, 4, 128], BF16)
        y2 = sb("y2", [128, 512], BF16)
        ssr = sb("ssr", [128, 4])
        rr = sb("rr", [128, 4])
        rl = [sb(f"rl{i}", [128, 512]) for i in range(2)]
        g1t = sb("g1t", [128, 8]); g2t = sb("g2t", [128, 8]); gft = sb("gft", [128, D])
        cwt = sb("cwt", [128, 12]); cgt = sb("cgt", [128, 4]); rgt = sb("rgt", [128, 4])
        ztt = sb("ztt", [128, 4]); maskt = sb("maskt", [128, 4, 128]); xit = sb("xit", [128, 4, 128])
        ident = sb("ident", [128, 128], BF16); onesb = sb("onesb", [128, 128])
        epst = sb("epst", [128, 1])
        pb = [st.enter_context(nc.psum_tensor(f"pb{i}", [128, 512], F32)) for i in range(8)]

        def pbf(i):
            return pb[i][:].bitcast(BF16).rearrange("p (k n) -> p k n", n=128)

        for dst, src, key in ((g1t, d_g1, "g1t"), (g2t, d_g2, "g2t"), (gft, d_gf, "gft"), (cwt, d_cw, "cwt"),
                              (cgt, d_cg, "cgt"), (rgt, d_rg, "rgt"), (ztt, d_zt, "ztt"), (ident, d_id, "ident"),
                              (onesb, d_ones, "onesb")):
            S.dma("sp", lambda e, dst=dst, src=src: e.dma_start(out=dst[:], in_=src), "c_" + key, writes=[key])
        S.dma("sp", lambda e: e.dma_start(out=maskt[:], in_=d_mask.rearrange("p (h n) -> p h n", h=4)), "c_maskt", writes=["maskt"])
        S.dma("sp", lambda e: e.dma_start(out=xit[:], in_=d_xi.rearrange("p (h n) -> p h n", h=4)), "c_xit", writes=["xit"])
        S.op("pool", lambda e: e.memset(epst[:], EPS), writes=["epst"])
        S.op("pool", lambda e: e.memset(T32[:], 0.0), writes=["T32"])
        S.op("pool", lambda e: e.memset(Sb[:], 0.0), writes=["Sb"])
        S.op("pool", lambda e: e.memset(chb[:], 0.0), writes=["chb0", "chb1", "chb2", "chb3"])
        stg = [hT[:, 16 * i:16 * (i + 1), :].rearrange("p a b -> p (a b)").bitcast(F32) for i in range(2)]
        stgk = [[f"hT{j}" for j in range(16 * i, 16 * (i + 1))] for i in range(2)]
        groups = []
        for gi in (1, 2, 0, 3):
            groups.append((w_in[:, 1536 + gi * 512:1536 + (gi + 1) * 512].rearrange("(k p) c -> p k c", p=128), 8,
                           wr[:, :, gi * 512:(gi + 1) * 512], None, f"wr{gi}"))
        for si, col0 in ((0, 1024), (1, 512), (2, 0)):
            groups.append((w_in[:, col0:col0 + 512].rearrange("(k p) c -> p k c", p=128), 8, None, wc_scr[si], f"wc_scr{si}"))
        for hf in range(2):
            groups.append((w_out[:, hf * 512:(hf + 1) * 512].rearrange("(k p) c -> p k c", p=128), 8, None, wo_scr[hf], f"wo_scr{hf}"))
        for g in range(8):
            groups.append((w_up[:, g * 512:(g + 1) * 512].rearrange("(k p) c -> p k c", p=128), 8, None, wu_scr[g], f"wu_scr{g}"))
        for g in range(8):
            groups.append((w_down[g * 512:(g + 1) * 512, :].rearrange("(j p) c -> p j c", p=128), 4, None, wd_scr[g], f"wd_scr{g}"))

        def cast_load(i):
            src, a, dsb, scr, key = groups[i]
            si = i % 2
            S.dma("sp", lambda e: e.dma_start(out=stg[si].rearrange("p (a b) -> p a b", a=a), in_=src), f"stg{si}", writes=stgk[si])

        def cast_do(i):
            src, a, dsb, scr, key = groups[i]
            si = i % 2
            eng = "act" if i % 2 == 0 else "dve"
            if dsb is not None:
                dst, wk = dsb, [key]
                src_v = stg[si].rearrange("p (a b) -> p a b", a=a)
            else:
                slot = i % NRING
                dst, wk = ring[slot][:], [f"ring{slot}"]
                src_v = stg[si]
            if eng == "act":
                S.op("act", lambda e: e.activation(out=dst, in_=src_v, func=AF.Copy), reads=stgk[si], writes=wk)
            else:
                S.op("dve", lambda e: e.tensor_copy(out=dst, in_=src_v), reads=stgk[si], writes=wk)
            if scr is not None:
                S.dma("sp", lambda e: e.dma_start(out=scr, in_=dst), "k_" + key, reads=wk, writes=[key])

        cast_state = dict(loaded=0, done=0)

        def cast_step():
            if cast_state["loaded"] < len(groups) and cast_state["loaded"] <= cast_state["done"] + 1:
                cast_load(cast_state["loaded"])
                cast_state["loaded"] += 1
            if cast_state["loaded"] < len(groups) and cast_state["loaded"] <= cast_state["done"] + 1:
                cast_load(cast_state["loaded"])
                cast_state["loaded"] += 1
            if cast_state["done"] < cast_state["loaded"]:
                cast_do(cast_state["done"])
                cast_state["done"] += 1

        n_upfront = 4 if nt_pre >= 28 else len(groups)
        while cast_state["done"] < n_upfront:
            cast_step()

        items = []
        ridx = {}

        def add_item(name, src, key):
            ridx[name] = len(items)
            items.append((src, key))

        def add_mixer_items(b, part):
            if part == 1:
                add_item(("h", b), wc_scr[0], "wc_scr0"); add_item(("C", b), wc_scr[1], "wc_scr1")
            elif part == 2:
                add_item(("B", b), wc_scr[2], "wc_scr2")
            elif part == 7:
                add_item(("wo0", b), wo_scr[0], "wo_scr0"); add_item(("wo1", b), wo_scr[1], "wo_scr1")

        add_item(("h", -1), wc_scr[0], "wc_scr0"); add_item(("C", -1), wc_scr[1], "wc_scr1")
        for part in range(8):
            add_mixer_items(0, part)
        for b in range(nblk):
            for g in range(8):
                add_item(("wu", b, g), wu_scr[g], f"wu_scr{g}")
                if b + 1 < nblk:
                    add_mixer_items(b + 1, g)
            for g in range(8):
                add_item(("wd", b, g), wd_scr[g], f"wd_scr{g}")
        rstate = dict(issued=0, released=set(), nget=0)

        def ring_pump(upto):
            while rstate["issued"] < len(items):
                i = rstate["issued"]
                if i >= NRING and (i - NRING) not in rstate["released"]:
                    assert i > upto, f"ring deadlock at item {i}"
                    break
                if i > upto + NRING - 1:
                    break
                src, skey = items[i]
                slot = i % NRING
                S.dma("sp", lambda e, slot=slot, src=src: e.dma_start(out=ring[slot][:], in_=src), f"ring{slot}",
                      reads=[skey], writes=[f"ring{slot}"])
                rstate["issued"] += 1

        def ring_get(name):
            i = ridx[name]
            assert i >= rstate["nget"], f"ring order violated at {name}"
            rstate["nget"] = i
            ring_pump(i)
            assert rstate["issued"] > i
            return i % NRING

        def ring_release(name):
            i = ridx[name]
            rstate["released"].add(i)
            ring_pump(i)

        def r8(slot):
            return ring[slot][:].rearrange("p (k c) -> p k c", k=8)

        def r4(slot):
            return ring[slot][:].rearrange("p (j c) -> p j c", j=4)

        TRB = 2
        def load_tile(xsrc, cssrc, t, xs, cs):
            S.dma("sp", lambda e: e.dma_start(out=xt[xs][:], in_=xsrc[t * 128:(t + 1) * 128, :]), f"x{xs}", writes=[f"xt{xs}"])
            if cssrc is not None:
                S.dma("sp", lambda e: e.dma_start(out=cst[cs][:], in_=cssrc[t]), f"cs{cs}", writes=[f"cst{cs}"])

        def rstd_small(src, dst, n, srck, dstk):
            S.op("dve", lambda e: e.tensor_scalar(out=dst, in0=src, scalar1=1.0 / n, scalar2=EPS, op0=ALU.mult, op1=ALU.add),
                 reads=[srck], writes=[dstk])
            S.op("act", lambda e: e.activation(out=dst, in_=dst, func=AF.Sqrt), reads=[dstk], writes=[dstk])
            S.op("dve", lambda e: e.reciprocal(out=dst, in_=dst), reads=[dstk], writes=[dstk])

        ss1 = sb("ss1", [128, 1]); rs1 = sb("rs1", [128, 1])
        set0 = dict(xn=xn[:], xnk="xn", rA=rA[:], rAk="rA", rB=rB[:], rBk="rB", kr=kr[:], krk="kr", kz=kz[:], kzk="kz",
                    vb=vb[:], vbk="vb", ss=ss[:], ssk="ss", rs=rs[:], rsk="rs", trb=2)
        set1 = dict(xn=xt[4][:, 0:512].bitcast(BF16), xnk="xt4", rA=xt[5][:, 0:512], rAk="xt5", rB=xt[5][:, 512:1024], rBk="xt5",
                    kr=xt[6][:, 0:256].bitcast(BF16), krk="xt6", kz=xt[6][:, 256:512].bitcast(BF16), kzk="xt6",
                    vb=xt[7][:, 0:256].bitcast(BF16), vbk="xt7", ss=ss1[:], ssk="ss1", rs=rs1[:], rsk="rs1", trb=3)

        def norm_to_uT(xs, gt, gk, dstT, dk, tcol, B=set0):
            xk = f"xt{xs}"
            trb = B["trb"]
            S.op("act", lambda e: e.activation(out=junk[:], in_=xt[xs][:], func=AF.Square, accum_out=B["ss"]),
                 reads=[xk], writes=["junk", B["ssk"]])
            rstd_small(B["ss"], B["rs"], D, B["ssk"], B["rsk"])
            S.op("act", lambda e: e.activation(out=B["xn"], in_=xt[xs][:], func=AF.Copy, scale=B["rs"]),
                 reads=[xk, B["rsk"]], writes=[B["xnk"]])
            for k in range(8):
                S.op("pe", lambda e, k=k: e.transpose(out=pbf(trb)[:, k, :], in_=B["xn"][:, k * 128:(k + 1) * 128], identity=ident[:]),
                     reads=[B["xnk"], "ident"], writes=[f"pb{trb}"])
            S.op("dve", lambda e: e.tensor_tensor(out=dstT[:, :, tcol:tcol + 128], in0=pbf(trb),
                                                  in1=gt[:].unsqueeze(2).broadcast_to([128, 8, 128]), op=ALU.mult),
                 reads=[f"pb{trb}", gk], writes=[f"{dk}{tcol // 128}"])

        def tm_proj(gi, bank, tcol):
            for k in range(8):
                S.op("pe", lambda e, k=k: e.matmul(pb[bank][:], lhsT=uT[:, k, tcol:tcol + 128], rhs=wr[:, k, gi * 512:(gi + 1) * 512],
                                                   start=(k == 0), stop=(k == 7)),
                     reads=[f"uT{tcol // 128}", f"wr{gi}"], writes=[f"pb{bank}"])

        def rotary(bank, cs, dst, dstk, B=set0):
            pv = pb[bank][:].rearrange("p (h two j) -> p h two j", two=2, j=64)
            psw = pv[:, :, ::-1, :]
            cc = cst[cs][:, 0:128].rearrange("p (two j) -> p two j", two=2).unsqueeze(1).broadcast_to([128, 4, 2, 64])
            s_ = cst[cs][:, 128:256].rearrange("p (two j) -> p two j", two=2).unsqueeze(1).broadcast_to([128, 4, 2, 64])
            a4 = B["rA"].rearrange("p (h two j) -> p h two j", two=2, j=64)
            b4 = B["rB"].rearrange("p (h two j) -> p h two j", two=2, j=64)
            S.op("dve", lambda e: e.tensor_tensor(out=a4, in0=pv, in1=cc, op=ALU.mult), reads=[f"pb{bank}", f"cst{cs}"], writes=[B["rAk"]])
            S.op("dve", lambda e: e.tensor_tensor(out=b4, in0=psw, in1=s_, op=ALU.mult), reads=[f"pb{bank}", f"cst{cs}"], writes=[B["rBk"]])
            S.op("pool", lambda e: e.tensor_tensor(out=dst, in0=B["rA"], in1=B["rB"], op=ALU.add), reads=[B["rAk"], B["rBk"]], writes=[dstk])

        def kv_state(kvb, B=set0):
            kr_, kz_, vb_ = B["kr"], B["kz"], B["vb"]
            S.op("pool", lambda e: e.tensor_tensor(out=kz_.rearrange("p (h n) -> p h n", h=4),
                                                   in0=kr_.rearrange("p (h n) -> p h n", h=4),
                                                   in1=ztt[:].unsqueeze(2).broadcast_to([128, 4, 128]), op=ALU.mult),
                 reads=[B["krk"], "ztt"], writes=[B["kzk"]])
            for h in range(NH):
                S.op("pe", lambda e, h=h: e.matmul(pb[kvb][:, h * 128:(h + 1) * 128], lhsT=kz_[:, h * 128:(h + 1) * 128],
                                                   rhs=vb_[:, h * 128:(h + 1) * 128], start=True, stop=True),
                     reads=[B["kzk"], B["vbk"]], writes=[f"pb{kvb}"])
            for h in range(NH):
                S.op("dve", lambda e, h=h: e.scalar_tensor_tensor(out=T32[:, h, :], in0=T32[:, h, :], scalar=cdec[h],
                                                                  in1=pb[kvb][:, h * 128:(h + 1) * 128], op0=ALU.mult, op1=ALU.add),
                     reads=["T32", f"pb{kvb}"], writes=["T32"])
            S.op("pool", lambda e: e.tensor_copy(out=Sb[:], in_=T32[:]), reads=["T32"], writes=["Sb"])

        def conv_front(n, rhs_of_k, rkeys, it_h, it_c):
            sl = ring_get(it_h)
            for c in range(4):
                bank = 4 + c
                for k in range(8):
                    S.op("pe", lambda e, k=k, c=c, bank=bank, sl=sl: e.matmul(pb[bank][:, 0:n], lhsT=r8(sl)[:, k, c * 128:(c + 1) * 128],
                                                                             rhs=rhs_of_k(k), start=(k == 0), stop=(k == 7)),
                         reads=rkeys + [f"ring{sl}"], writes=[f"pb{bank}"])
                S.op("act", lambda e, c=c, bank=bank: e.activation(out=chb[:, c, 2:2 + n], in_=pb[bank][:, 0:n], func=AF.Copy),
                     reads=[f"pb{bank}"], writes=[f"chb{c}"])
                yield 0
            ring_release(it_h)
            sl = ring_get(it_c)
            for c in range(4):
                bank = 4 + c
                for k in range(8):
                    S.op("pe", lambda e, k=k, c=c, bank=bank, sl=sl: e.matmul(pb[bank][:, 0:n], lhsT=r8(sl)[:, k, c * 128:(c + 1) * 128],
                                                                             rhs=rhs_of_k(k), start=(k == 0), stop=(k == 7)),
                         reads=rkeys + [f"ring{sl}"], writes=[f"pb{bank}"])
                S.op("dve", lambda e, c=c, bank=bank: e.tensor_tensor(out=chb[:, c, 2:2 + n], in0=pb[bank][:, 0:n], in1=chb[:, c, 2:2 + n], op=ALU.mult),
                     reads=[f"pb{bank}", f"chb{c}"], writes=[f"chb{c}"])
                yield 0
            ring_release(it_c)

        psets = [(set0, 5, 6), (set1, 7, 4)]

        def prefix_front(t):
            B, kb, vbk_ = psets[t % 2]
            xs, cs, tcol = t % 4, t % 2, (t % 2) * 128
            if cast_state["done"] < len(groups):
                cast_step()
            load_tile(xp, csp, t, xs, cs)
            norm_to_uT(xs, g1t, "g1t", uT, "uT", tcol, B)
            tm_proj(1, kb, tcol)
            tm_proj(2, vbk_, tcol)

        def prefix_back(t):
            B, kb, vbk_ = psets[t % 2]
            cs, tcol = t % 2, (t % 2) * 128
            rotary(kb, cs, B["kr"], B["krk"], B)
            S.op("act", lambda e: e.activation(out=B["vb"], in_=pb[vbk_][:], func=AF.Copy), reads=[f"pb{vbk_}"], writes=[B["vbk"]])
            kv_state(kb, B)
            if t == nt_pre - 1:
                for _ in conv_front(128, lambda k: uT[:, k, tcol:tcol + 128], [f"uT{t % 2}"], ("h", -1), ("C", -1)):
                    pass
                for c in range(4):
                    S.op("pool", lambda e, c=c: e.tensor_copy(out=chb[:, c, 0:2], in_=chb[:, c, 128:130]),
                         reads=[f"chb{c}"], writes=[f"chb{c}"])

        npre = nt_pre if STOP != "casts" else 0
        if npre:
            prefix_front(0)
        for t in range(npre):
            if t + 1 < npre:
                prefix_front(t + 1)
            prefix_back(t)

        while cast_state["done"] < len(groups):
            cast_step()
        ukeys = ["uT0", "uT1", "uT2", "uT3"]
        u2keys = ["u2T0", "u2T1", "u2T2", "u2T3"]
        mkeys = ["mixc0", "mixc1", "mixc2", "mixc3"]

        def mixer(b):
            xs0 = (b % 2) * 4
            for tt in range(4):
                t = b * 4 + tt
                load_tile(xm, None, t, xs0 + tt, 0)
            for tt in range(4):
                norm_to_uT(xs0 + tt, g1t, "g1t", uT, "uT", tt * 128)
                yield 0
            yield 1
            yield from conv_front(512, lambda k: uT[:, k, :], ukeys, ("h", b), ("C", b))
            yield 1
            slB = ring_get(("B", b))
            for c in range(4):
                bank = 4 + c
                for k in range(8):
                    S.op("pe", lambda e, k=k, c=c, bank=bank, slB=slB: e.matmul(pb[bank][:], lhsT=r8(slB)[:, k, c * 128:(c + 1) * 128],
                                                                               rhs=uT[:, k, :], start=(k == 0), stop=(k == 7)),
                         reads=ukeys + [f"ring{slB}"], writes=[f"pb{bank}"])
                ck = f"chb{c}"
                S.op("act", lambda e, c=c: e.activation(out=ctmp[:], in_=chb[:, c, 0:512], func=AF.Copy, scale=cwt[:, c * 3:c * 3 + 1]),
                     reads=[ck, "cwt"], writes=["ctmp"])
                S.op("dve", lambda e, c=c: e.scalar_tensor_tensor(out=ctmp[:], in0=chb[:, c, 1:513], scalar=cwt[:, c * 3 + 1:c * 3 + 2],
                                                                  in1=ctmp[:], op0=ALU.mult, op1=ALU.add),
                     reads=[ck, "cwt", "ctmp"], writes=["ctmp"])
                S.op("dve", lambda e, c=c: e.scalar_tensor_tensor(out=ctmp[:], in0=chb[:, c, 2:514], scalar=cwt[:, c * 3 + 2:c * 3 + 3],
                                                                  in1=ctmp[:], op0=ALU.mult, op1=ALU.add),
                     reads=[ck, "cwt", "ctmp"], writes=["ctmp"])
                S.op("pool", lambda e, c=c: e.tensor_copy(out=chb[:, c, 0:2], in_=chb[:, c, 512:514]), reads=[ck], writes=[ck])
                S.op("dve", lambda e, bank=bank: e.tensor_tensor(out=cyv[:], in0=pb[bank][:], in1=ctmp[:], op=ALU.mult),
                     reads=[f"pb{bank}", "ctmp"], writes=["cyv"])
                S.op("act", lambda e: e.activation(out=csq[:], in_=cyv[:], func=AF.Square), reads=["cyv"], writes=["csq"])
                yield 0
                S.op("pe", lambda e: e.matmul(pb[3][:], lhsT=onesb[:], rhs=csq[:], start=True, stop=True),
                     reads=["onesb", "csq"], writes=["pb3"])
                S.op("act", lambda e: e.activation(out=csq[:], in_=pb[3][:], func=AF.Sqrt, bias=epst[:], scale=1.0),
                     reads=["pb3", "epst"], writes=["csq"])
                S.op("dve", lambda e: e.reciprocal(out=csq[:], in_=csq[:]), reads=["csq"], writes=["csq"])
                S.op("dve", lambda e, c=c: e.scalar_tensor_tensor(out=mixT[:, c, :], in0=cyv[:], scalar=cgt[:, c:c + 1], in1=csq[:],
                                                                  op0=ALU.mult, op1=ALU.mult),
                     reads=["cyv", "csq", "cgt"], writes=[f"mixc{c}"])
                yield 0
            ring_release(("B", b))
            yield 1
            for tt in range(4):
                t = b * 4 + tt
                cs = t % 2
                tc0 = tt * 128
                S.dma("sp", lambda e, t=t, cs=cs: e.dma_start(out=cst[cs][:], in_=csm[t]), f"cs{cs}", writes=[f"cst{cs}"])
                for gi in range(4):
                    tm_proj(gi, 4 + gi, tc0)
                    yield 0
                rotary(4, cs, qr[:], "qr")
                rotary(5, cs, kr[:], "kr")
                S.op("act", lambda e: e.activation(out=vb[:], in_=pb[6][:], func=AF.Copy), reads=["pb6"], writes=["vb"])
                S.op("act", lambda e: e.activation(out=sg[:], in_=pb[7][:], func=AF.Silu), reads=["pb7"], writes=["sg"])
                yield 0
                for h in range(NH):
                    S.op("pe", lambda e, h=h: e.transpose(out=pbf(TRB)[:, h, :], in_=qr[:, h * 128:(h + 1) * 128], identity=ident[:]),
                         reads=["qr", "ident"], writes=[f"pb{TRB}"])
                S.op("act", lambda e: e.activation(out=qT[:], in_=pbf(TRB)[:, 0:4, :], func=AF.Copy), reads=[f"pb{TRB}"], writes=["qT"])
                S.op("dve", lambda e: e.tensor_tensor(out=qxT[:], in0=pbf(TRB)[:, 0:4, :], in1=xit[:], op=ALU.mult),
                     reads=[f"pb{TRB}", "xit"], writes=["qxT"])
                yield 0
                for h in range(NH):
                    S.op("pe", lambda e, h=h: e.transpose(out=pbf(3)[:, h, :], in_=kr[:, h * 128:(h + 1) * 128], identity=ident[:]),
                         reads=["kr", "ident"], writes=["pb3"])
                S.op("act", lambda e: e.activation(out=kT[:], in_=pbf(3)[:, 0:4, :], func=AF.Copy), reads=["pb3"], writes=["kT"])
                yield 0
                for h in range(NH):
                    S.op("pe", lambda e, h=h: e.matmul(pb[3][:, h * 128:(h + 1) * 128], lhsT=kT[:, h, :], rhs=qT[:, h, :], start=True, stop=True),
                         reads=["kT", "qT"], writes=["pb3"])
                S.op("dve", lambda e: e.tensor_tensor(out=sc[:], in0=pb[3][:].rearrange("p (h n) -> p h n", h=4), in1=maskt[:], op=ALU.mult),
                     reads=["pb3", "maskt"], writes=["sc"])
                yield 0
                for h in range(NH):
                    S.op("pe", lambda e, h=h: e.matmul(pb[4][:, h * 128:(h + 1) * 128], lhsT=sc[:, h, :], rhs=vb[:, h * 128:(h + 1) * 128],
                                                       start=True, stop=False), reads=["sc", "vb"], writes=["pb4"])
                    S.op("pe", lambda e, h=h: e.matmul(pb[4][:, h * 128:(h + 1) * 128], lhsT=qxT[:, h, :], rhs=Sb[:, h, :],
                                                       start=False, stop=True), reads=["qxT", "Sb"], writes=["pb4"])
                kv_state(5)
                yield 0
                for h in range(NH):
                    S.op("act", lambda e, h=h: e.activation(out=junk[:, 0:128], in_=pb[4][:, h * 128:(h + 1) * 128], func=AF.Square,
                                                            accum_out=ssr[:, h:h + 1]), reads=["pb4"], writes=["junk", "ssr"])
                rstd_small(ssr[:], rr[:], 128, "ssr", "rr")
                for h in range(NH):
                    S.op("dve", lambda e, h=h: e.scalar_tensor_tensor(out=y2[:, h * 128:(h + 1) * 128], in0=pb[4][:, h * 128:(h + 1) * 128],
                                                                      scalar=rr[:, h:h + 1], in1=sg[:, h * 128:(h + 1) * 128],
                                                                      op0=ALU.mult, op1=ALU.mult),
                         reads=["pb4", "rr", "sg"], writes=["y2"])
                yield 0
                for h in range(NH):
                    S.op("pe", lambda e, h=h: e.transpose(out=pbf(TRB)[:, h, :], in_=y2[:, h * 128:(h + 1) * 128], identity=ident[:]),
                         reads=["y2", "ident"], writes=[f"pb{TRB}"])
                for h in range(NH):
                    S.op("act", lambda e, h=h, tc0=tc0: e.activation(out=mixT[:, 4 + h, tc0:tc0 + 128], in_=pbf(TRB)[:, h, :], func=AF.Copy,
                                                                     scale=rgt[:, h:h + 1]),
                         reads=[f"pb{TRB}", "rgt"], writes=[f"mixr{tt}"])
                yield 1
            slo = [ring_get(("wo0", b)), ring_get(("wo1", b))]
            for tt in range(4):
                tc0 = tt * 128
                xs = xs0 + tt
                for hf in range(2):
                    bank = 6 + hf
                    for c in range(8):
                        S.op("pe", lambda e, c=c, bank=bank, tc0=tc0, so=slo[hf]: e.matmul(pb[bank][:], lhsT=mixT[:, c, tc0:tc0 + 128],
                                                                                          rhs=r8(so)[:, c, :], start=(c == 0), stop=(c == 7)),
                             reads=mkeys + [f"mixr{tt}", f"ring{slo[hf]}"], writes=[f"pb{bank}"])
                    S.op("dve", lambda e, hf=hf, bank=bank, xs=xs: e.tensor_tensor(out=xt[xs][:, hf * 512:(hf + 1) * 512], in0=pb[bank][:],
                                                                                  in1=xt[xs][:, hf * 512:(hf + 1) * 512], op=ALU.add),
                         reads=[f"pb{bank}", f"xt{xs}"], writes=[f"xt{xs}"])
                yield 0
            ring_release(("wo0", b))
            ring_release(("wo1", b))
            yield 2
            for tt in range(4):
                norm_to_uT(xs0 + tt, g2t, "g2t", u2T, "u2T", tt * 128)
                yield 2
            yield 1

        def ffn_up_chunk(b, g, jj, sl):
            j = g * 4 + jj
            bank = j % 2
            for k in range(8):
                S.op("pe", lambda e, k=k: e.matmul(pb[bank][:], lhsT=r8(sl)[:, k, jj * 128:(jj + 1) * 128],
                                                   rhs=u2T[:, k, :], start=(k == 0), stop=(k == 7)),
                     reads=u2keys + [f"ring{sl}"], writes=[f"pb{bank}"])
            ri = j % 2
            if j % 2 == 0:
                S.op("act", lambda e: e.activation(out=rl[ri][:], in_=pb[bank][:], func=AF.Relu),
                     reads=[f"pb{bank}"], writes=[f"rl{ri}"])
            else:
                S.op("dve", lambda e: e.tensor_scalar(out=rl[ri][:], in0=pb[bank][:], scalar1=0.0, scalar2=None, op0=ALU.max),
                     reads=[f"pb{bank}"], writes=[f"rl{ri}"])
            S.op("pool", lambda e: e.tensor_tensor(out=hT[:, j, :], in0=rl[ri][:], in1=rl[ri][:], op=ALU.mult),
                 reads=[f"rl{ri}"], writes=[f"hT{j}"])

        def ffn_down(b):
            for g in range(8):
                sl = ring_get(("wd", b, g))
                for tt in range(4):
                    for hf in range(2):
                        bank = tt * 2 + hf
                        for jj in range(4):
                            j = g * 4 + jj
                            S.op("pe", lambda e, jj=jj, j=j, hf=hf, bank=bank, tt=tt, sl=sl: e.matmul(
                                pb[bank][:], lhsT=hT[:, j, tt * 128:(tt + 1) * 128], rhs=r4(sl)[:, jj, hf * 512:(hf + 1) * 512],
                                start=(j == 0), stop=(j == 31)),
                                reads=[f"hT{j}", f"ring{sl}"], writes=[f"pb{bank}"])
                ring_release(("wd", b, g))

        def finish(b):
            xs0 = (b % 2) * 4
            for tt in range(4):
                t = b * 4 + tt
                xs = xs0 + tt
                xk = f"xt{xs}"
                for hf in range(2):
                    bank = tt * 2 + hf
                    S.op("dve", lambda e, hf=hf, bank=bank, xs=xs: e.tensor_tensor(out=xt[xs][:, hf * 512:(hf + 1) * 512], in0=pb[bank][:],
                                                                                  in1=xt[xs][:, hf * 512:(hf + 1) * 512], op=ALU.add),
                         reads=[f"pb{bank}", xk], writes=[xk])
                S.op("act", lambda e, xs=xs: e.activation(out=junk[:], in_=xt[xs][:], func=AF.Square, accum_out=ss[:]),
                     reads=[xk], writes=["junk", "ss"])
                rstd_small(ss[:], rs[:], D, "ss", "rs")
                S.op("act", lambda e, xs=xs: e.activation(out=xt[xs][:], in_=xt[xs][:], func=AF.Copy, scale=rs[:]),
                     reads=[xk, "rs"], writes=[xk])
                S.op("pool", lambda e, xs=xs: e.tensor_tensor(out=xt[xs][:], in0=xt[xs][:], in1=gft[:], op=ALU.mult),
                     reads=[xk, "gft"], writes=[xk])
                S.dma("pool", lambda e, t=t, xs=xs: e.dma_start(out=y[t * 128:(t + 1) * 128, :], in_=xt[xs][:]), f"out{xs}", reads=[xk])

        if STOP not in ("casts", "prefix"):
            for _ in mixer(0):
                pass
            for b in range(nblk):
                gen = mixer(b + 1) if b + 1 < nblk else None
                for g in range(8):
                    sl = ring_get(("wu", b, g))
                    at_boundary = gen is None
                    hold = False
                    for jj in range(4):
                        ffn_up_chunk(b, g, jj, sl)
                        if not at_boundary and not hold:
                            v = next(gen)
                            at_boundary = (v == 1)
                            hold = (v == 2)
                    ring_release(("wu", b, g))
                    while not at_boundary:
                        at_boundary = (next(gen) == 1)
                ffn_down(b)
                finish(b)
        S.emit(st)
    return nc


def _host_consts():
    f32 = np.float32
    lg = np.log(f32(1.0) - f32(2.0) ** (f32(-5.0) - np.arange(NH, dtype=f32))).astype(f32)
    idx = np.arange(128, dtype=f32)
    diff = idx[:, None] - idx[None, :]
    intra = np.where(diff[None] >= 0, np.exp(lg[:, None, None] * np.maximum(diff, 0.0)[None]), 0.0).astype(f32)
    scale = f32(128.0 ** -0.5)
    maskT = np.ascontiguousarray(np.transpose(intra, (2, 0, 1)) * scale).astype(f32).reshape(128, 512)
    zeta = np.exp(lg[:, None] * (f32(127.0) - idx)[None]).astype(f32)
    ztt = np.ascontiguousarray(zeta.T * scale).astype(f32)
    xi = np.exp(lg[:, None] * (idx + f32(1.0))[None]).astype(f32)
    xit = np.ascontiguousarray(np.broadcast_to(xi.reshape(1, 512), (128, 512))).astype(f32)
    ident = np.eye(128, dtype=f32).astype(ml_dtypes.bfloat16)
    onesblk = np.zeros((128, 128), f32)
    onesblk[:64, :64] = 1.0 / 64
    onesblk[64:, 64:] = 1.0 / 64
    return dict(maskt=maskT, ztt=ztt, xit=xit, ident=ident, onesblk=onesblk)


def _cs_table(pos0, ntiles):
    f32 = np.float32
    half = 64
    inv_freq = (f32(1.0) / (f32(10000.0) ** (np.arange(half, dtype=f32) / f32(half)))).astype(f32)
    pos = (pos0 + np.arange(ntiles * 128)).astype(f32)
    ang = (pos[:, None] * inv_freq[None, :]).astype(f32)
    c = np.cos(ang).astype(f32)
    s = np.sin(ang).astype(f32)
    tab = np.concatenate([c, c, -s, s], axis=1).astype(f32)
    return np.ascontiguousarray(tab.reshape(ntiles, 128, 256))


_NC_CACHE = {}


def kernel(x, norm1_g, w_in, conv_w, conv_norm_g, ret_norm_g, w_out, norm2_g, w_up, w_down, final_norm_g):
    x = np.asarray(x, dtype=np.float32)
    B, SEQ, _ = x.shape
    half = SEQ // 2
    nt = half // 128
    key = (nt, nt)
    if key not in _NC_CACHE:
        _NC_CACHE[key] = build_nc(nt, nt)
    nc = _NC_CACHE[key]
    f32 = np.float32
    consts = _host_consts()
    shared = dict(
        w_in=np.ascontiguousarray(w_in, dtype=f32), w_out=np.ascontiguousarray(w_out, dtype=f32),
        w_up=np.ascontiguousarray(w_up, dtype=f32), w_down=np.ascontiguousarray(w_down, dtype=f32),
        g1t=np.ascontiguousarray(np.asarray(norm1_g, f32).reshape(8, 128).T),
        g2t=np.ascontiguousarray(np.asarray(norm2_g, f32).reshape(8, 128).T),
        gft=np.ascontiguousarray(np.broadcast_to(np.asarray(final_norm_g, f32)[None, :], (128, D))),
        cwt=np.ascontiguousarray(np.asarray(conv_w, f32).reshape(3, 4, 128).transpose(2, 1, 0).reshape(128, 12)),
        cgt=np.ascontiguousarray(np.asarray(conv_norm_g, f32).reshape(4, 128).T),
        rgt=np.ascontiguousarray(np.asarray(ret_norm_g, f32).reshape(4, 128).T),
        **consts,
    )
    cs_lo = _cs_table(0, nt)
    cs_hi = _cs_table(half, nt)
    in_maps = []
    for c in range(8):
        b, hf = c // 2, c % 2
        m = dict(shared)
        m["xm"] = np.ascontiguousarray(x[b, hf * half:(hf + 1) * half])
        m["xp"] = np.ascontiguousarray(x[b, 0:half]) if hf == 1 else np.zeros((half, D), f32)
        m["csm"] = cs_hi if hf == 1 else cs_lo
        m["csp"] = cs_lo
        in_maps.append(m)
    res = run_bass_kernel_spmd(nc, in_maps, core_ids=list(range(8)))
    if DEBUG:
        kernel.dbg = [dict(r) for r in res.results]
    out = np.empty((B, SEQ, D), f32)
    for c in range(8):
        b, hf = c // 2, c % 2
        out[b, hf * half:(hf + 1) * half] = res.results[c]["y"]
    return out
```

```python
import numpy as np
import ml_dtypes
from contextlib import ExitStack
import concourse.bass as bass
import concourse.mybir as mybir
from concourse.bass_utils import run_bass_kernel_spmd

F32 = mybir.dt.float32
BF16 = mybir.dt.bfloat16
AF = mybir.ActivationFunctionType
ALU = mybir.AluOpType

D = 1024
NH = 4
EPS = 1e-6
NRING = 4
DEBUG = False
STOP = None


class Sched:
    ENGS = ("pe", "act", "dve", "pool", "sp")

    def __init__(self, nc):
        self.nc = nc
        self.ops = []
        self.last_w = {}
        self.readers = {}

    def _deps(self, reads, writes):
        deps = {}
        for k in reads:
            w = self.last_w.get(k)
            if w is not None:
                deps[w] = True
        for k in writes:
            w = self.last_w.get(k)
            if w is not None:
                deps.setdefault(w, False)
            for r in self.readers.get(k, ()):
                deps.setdefault(r, False)
        return deps

    def _commit(self, idx, reads, writes):
        for k in reads:
            self.readers.setdefault(k, []).append(idx)
        for k in writes:
            self.last_w[k] = idx
            self.readers[k] = []

    def op(self, eng, fn, reads=(), writes=()):
        writes = list(writes) + [k for k in reads if k.startswith("pb") and k not in writes]
        deps = self._deps(reads, writes)
        idx = len(self.ops)
        self.ops.append(dict(eng=eng, fn=fn, deps=deps, kind="c", sig=False))
        self._commit(idx, reads, writes)
        return idx

    def dma(self, eng, fn, sem, reads=(), writes=()):
        deps = self._deps(reads, writes)
        idx = len(self.ops)
        self.ops.append(dict(eng=eng, fn=fn, deps=deps, kind="d", sem=sem))
        self._commit(idx, reads, writes)
        return idx

    def emit(self, stack):
        nc = self.nc
        ops = self.ops

        def skip(p, o):
            return (p["kind"] == "c" and o["kind"] == "c" and p["eng"] == o["eng"])

        for o in ops:
            for d, raw in o["deps"].items():
                p = ops[d]
                if p["kind"] == "d":
                    continue
                if skip(p, o) and (p["eng"] == "pe" or not raw):
                    continue
                p["sig"] = True
        esem = {e: stack.enter_context(nc.semaphore("s_" + e)) for e in ("pe", "act", "dve", "pool")}
        dsem, dcnt = {}, {}
        cnt = {e: 0 for e in esem}
        for o in ops:
            if o["kind"] == "c":
                if o["sig"]:
                    cnt[o["eng"]] += 1
                    o["val"] = cnt[o["eng"]]
            else:
                s = o["sem"]
                if s not in dsem:
                    dsem[s] = stack.enter_context(nc.semaphore("d_" + s))
                    dcnt[s] = 0
                dcnt[s] += 16
                o["val"] = dcnt[s]
        per_eng = {e: [] for e in self.ENGS}
        for i, o in enumerate(ops):
            per_eng[o["eng"]].append(i)
        seen = {e: {} for e in self.ENGS}

        def run(eng_name, engine):
            sn = seen[eng_name]
            for i in per_eng[eng_name]:
                o = ops[i]
                waits = {}
                for d, raw in o["deps"].items():
                    p = ops[d]
                    if p["kind"] == "d":
                        key, h = ("d", p["sem"]), dsem[p["sem"]]
                    else:
                        if skip(p, o) and (eng_name == "pe" or not raw):
                            continue
                        key, h = ("c", p["eng"]), esem[p["eng"]]
                    v = p["val"]
                    if key not in waits or waits[key][1] < v:
                        waits[key] = (h, v)
                for key, (h, v) in waits.items():
                    if sn.get(key, 0) >= v:
                        continue
                    sn[key] = v
                    engine.wait_ge(h, v)
                ins = o["fn"](engine)
                if o["kind"] == "d":
                    ins.then_inc(dsem[o["sem"]], 16)
                elif o["sig"]:
                    ins.then_inc(esem[o["eng"]], 1)
            if eng_name == "sp":
                for s in dsem:
                    engine.wait_ge(dsem[s], dcnt[s])

        block = stack.enter_context(nc.Block())
        block.tensor(lambda e: run("pe", e))
        block.scalar(lambda e: run("act", e))
        block.vector(lambda e: run("dve", e))
        block.gpsimd(lambda e: run("pool", e))
        block.sync(lambda e: run("sp", e))


def build_nc(nt_main=32, nt_pre=32):
    assert nt_main % 4 == 0
    nblk = nt_main // 4
    lg = [float(np.log(np.float32(1.0) - np.float32(2.0) ** np.float32(-5.0 - h))) for h in range(NH)]
    cdec = [float(np.exp(np.float32(lg[h]) * np.float32(128.0))) for h in range(NH)]

    nc = bass.Bass("TRN2", target_bir_lowering=False)

    def din(name, shape, dt=F32):
        return nc.dram_tensor(name, shape, dt, kind="ExternalInput").ap()

    xm = din("xm", [nt_main * 128, D])
    xp = din("xp", [nt_pre * 128, D])
    csm = din("csm", [nt_main, 128, 256])
    csp = din("csp", [nt_pre, 128, 256])
    w_in = din("w_in", [D, 3584])
    w_out = din("w_out", [D, D])
    w_up = din("w_up", [D, 4096])
    w_down = din("w_down", [4096, D])
    d_g1 = din("g1t", [128, 8])
    d_g2 = din("g2t", [128, 8])
    d_gf = din("gft", [128, D])
    d_cw = din("cwt", [128, 12])
    d_cg = din("cgt", [128, 4])
    d_rg = din("rgt", [128, 4])
    d_zt = din("ztt", [128, 4])
    d_mask = din("maskt", [128, 512])
    d_xi = din("xit", [128, 512])
    d_id = din("ident", [128, 128], BF16)
    d_ones = din("onesblk", [128, 128])
    y = nc.dram_tensor("y", [nt_main * 128, D], F32, kind="ExternalOutput").ap()
    if DEBUG:
        dbg_x1 = nc.dram_tensor("dbg_x1", [nt_main * 128, D], F32, kind="ExternalOutput").ap()
        dbg_mix = nc.dram_tensor("dbg_mix", [nt_main // 4, 128, 4096], F32, kind="ExternalOutput").ap()
    wc_scr = nc.dram_tensor("wc_scr", [3, 128, 4096], BF16, kind="Internal").ap()
    wo_scr = nc.dram_tensor("wo_scr", [2, 128, 4096], BF16, kind="Internal").ap()
    wu_scr = nc.dram_tensor("wu_scr", [8, 128, 4096], BF16, kind="Internal").ap()
    wd_scr = nc.dram_tensor("wd_scr", [8, 128, 4096], BF16, kind="Internal").ap()

    with ExitStack() as st:
        S = Sched(nc)

        def sb(name, shape, dt=F32):
            return st.enter_context(nc.sbuf_tensor("s_" + name, shape, dt))

        wr = sb("wr", [128, 8, 2048], BF16)
        ring = [sb(f"ring{i}", [128, 4096], BF16) for i in range(NRING)]
        xt = [sb(f"xt{i}", [128, D]) for i in range(8)]
        cst = [sb(f"cst{i}", [128, 256]) for i in range(2)]
        uT = sb("uT", [128, 8, 512], BF16)
        u2T = sb("u2T", [128, 8, 512], BF16)
        mixT = sb("mixT", [128, 8, 512], BF16)
        hT = sb("hT", [128, 32, 512], BF16)
        chb = sb("chb", [128, 4, 514])
        ctmp = sb("ctmp", [128, 512])
        cyv = sb("cyv", [128, 512])
        csq = sb("csq", [128, 512])
        junk = sb("junk", [128, D])
        xn = sb("xn", [128, D], BF16)
        ss = sb("ss", [128, 1])
        rs = sb("rs", [128, 1])
        rA = sb("rA", [128, 512])
        rB = sb("rB", [128, 512])
        qr = sb("qr", [128, 512], BF16)
        kr = sb("kr", [128, 512], BF16)
        kz = sb("kz", [128, 512], BF16)
        vb = sb("vb", [128, 512], BF16)
        sg = sb("sg", [128, 512])
        qT = sb("qT", [128, 4, 128], BF16)
        qxT = sb("qxT", [128, 4, 128], BF16)
        kT = sb("kT", [128, 4, 128], BF16)
        sc = sb("sc", [128, 4, 128], BF16)
        T32 = sb("T32", [128, 4, 128])
        Sb = sb("Sb", [128, 4, 128], BF16)
        y2 = sb("y2", [128, 512], BF16)
        ssr = sb("ssr", [128, 4])
        rr = sb("rr", [128, 4])
        rl = [sb(f"rl{i}", [128, 512]) for i in range(2)]
        g1t = sb("g1t", [128, 8]); g2t = sb("g2t", [128, 8]); gft = sb("gft", [128, D])
        cwt = sb("cwt", [128, 12]); cgt = sb("cgt", [128, 4]); rgt = sb("rgt", [128, 4])
        ztt = sb("ztt", [128, 4]); maskt = sb("maskt", [128, 4, 128]); xit = sb("xit", [128, 4, 128])
        ident = sb("ident", [128, 128], BF16); onesb = sb("onesb", [128, 128])
        epst = sb("epst", [128, 1])
        pb = [st.enter_context(nc.psum_tensor(f"pb{i}", [128, 512], F32)) for i in range(8)]

        def pbf(i):
            return pb[i][:].bitcast(BF16).rearrange("p (k n) -> p k n", n=128)

        for dst, src, key in ((g1t, d_g1, "g1t"), (g2t, d_g2, "g2t"), (gft, d_gf, "gft"), (cwt, d_cw, "cwt"),
                              (cgt, d_cg, "cgt"), (rgt, d_rg, "rgt"), (ztt, d_zt, "ztt"), (ident, d_id, "ident"),
                              (onesb, d_ones, "onesb")):
            S.dma("sp", lambda e, dst=dst, src=src: e.dma_start(out=dst[:], in_=src), "c_" + key, writes=[key])
        S.dma("sp", lambda e: e.dma_start(out=maskt[:], in_=d_mask.rearrange("p (h n) -> p h n", h=4)), "c_maskt", writes=["maskt"])
        S.dma("sp", lambda e: e.dma_start(out=xit[:], in_=d_xi.rearrange("p (h n) -> p h n", h=4)), "c_xit", writes=["xit"])
        S.op("pool", lambda e: e.memset(epst[:], EPS), writes=["epst"])
        S.op("pool", lambda e: e.memset(T32[:], 0.0), writes=["T32"])
        S.op("pool", lambda e: e.memset(Sb[:], 0.0), writes=["Sb"])
        S.op("pool", lambda e: e.memset(chb[:], 0.0), writes=["chb0", "chb1", "chb2", "chb3"])
        stg = [hT[:, 16 * i:16 * (i + 1), :].rearrange("p a b -> p (a b)").bitcast(F32) for i in range(2)]
        stgk = [[f"hT{j}" for j in range(16 * i, 16 * (i + 1))] for i in range(2)]
        groups = []
        for gi in (1, 2, 0, 3):
            groups.append((w_in[:, 1536 + gi * 512:1536 + (gi + 1) * 512].rearrange("(k p) c -> p k c", p=128), 8,
                           wr[:, :, gi * 512:(gi + 1) * 512], None, f"wr{gi}"))
        for si, col0 in ((0, 1024), (1, 512), (2, 0)):
            groups.append((w_in[:, col0:col0 + 512].rearrange("(k p) c -> p k c", p=128), 8, None, wc_scr[si], f"wc_scr{si}"))
        for hf in range(2):
            groups.append((w_out[:, hf * 512:(hf + 1) * 512].rearrange("(k p) c -> p k c", p=128), 8, None, wo_scr[hf], f"wo_scr{hf}"))
        for g in range(8):
            groups.append((w_up[:, g * 512:(g + 1) * 512].rearrange("(k p) c -> p k c", p=128), 8, None, wu_scr[g], f"wu_scr{g}"))
        for g in range(8):
            groups.append((w_down[g * 512:(g + 1) * 512, :].rearrange("(j p) c -> p j c", p=128), 4, None, wd_scr[g], f"wd_scr{g}"))

        def cast_load(i):
            src, a, dsb, scr, key = groups[i]
            si = i % 2
            S.dma("sp", lambda e: e.dma_start(out=stg[si].rearrange("p (a b) -> p a b", a=a), in_=src), f"stg{si}", writes=stgk[si])

        def cast_do(i):
            src, a, dsb, scr, key = groups[i]
            si = i % 2
            eng = "act" if i % 2 == 0 else "dve"
            if dsb is not None:
                dst, wk = dsb, [key]
                src_v = stg[si].rearrange("p (a b) -> p a b", a=a)
            else:
                slot = i % NRING
                dst, wk = ring[slot][:], [f"ring{slot}"]
                src_v = stg[si]
            if eng == "act":
                S.op("act", lambda e: e.activation(out=dst, in_=src_v, func=AF.Copy), reads=stgk[si], writes=wk)
            else:
                S.op("dve", lambda e: e.tensor_copy(out=dst, in_=src_v), reads=stgk[si], writes=wk)
            if scr is not None:
                S.dma("sp", lambda e: e.dma_start(out=scr, in_=dst), "k_" + key, reads=wk, writes=[key])

        cast_state = dict(loaded=0, done=0)

        def cast_step():
            if cast_state["loaded"] < len(groups) and cast_state["loaded"] <= cast_state["done"] + 1:
                cast_load(cast_state["loaded"])
                cast_state["loaded"] += 1
            if cast_state["loaded"] < len(groups) and cast_state["loaded"] <= cast_state["done"] + 1:
                cast_load(cast_state["loaded"])
                cast_state["loaded"] += 1
            if cast_state["done"] < cast_state["loaded"]:
                cast_do(cast_state["done"])
                cast_state["done"] += 1

        n_upfront = 4 if nt_pre >= 28 else len(groups)
        while cast_state["done"] < n_upfront:
            cast_step()

        items = []
        ridx = {}

        def add_item(name, src, key):
            ridx[name] = len(items)
            items.append((src, key))

        def add_mixer_items(b, part):
            if part == 1:
                add_item(("h", b), wc_scr[0], "wc_scr0"); add_item(("C", b), wc_scr[1], "wc_scr1")
            elif part == 2:
                add_item(("B", b), wc_scr[2], "wc_scr2")
            elif part == 7:
                add_item(("wo0", b), wo_scr[0], "wo_scr0"); add_item(("wo1", b), wo_scr[1], "wo_scr1")

        add_item(("h", -1), wc_scr[0], "wc_scr0"); add_item(("C", -1), wc_scr[1], "wc_scr1")
        for part in range(8):
            add_mixer_items(0, part)
        for b in range(nblk):
            for g in range(8):
                add_item(("wu", b, g), wu_scr[g], f"wu_scr{g}")
                if b + 1 < nblk:
                    add_mixer_items(b + 1, g)
            for g in range(8):
                add_item(("wd", b, g), wd_scr[g], f"wd_scr{g}")
        rstate = dict(issued=0, released=set(), nget=0)

        def ring_pump(upto):
            while rstate["issued"] < len(items):
                i = rstate["issued"]
                if i >= NRING and (i - NRING) not in rstate["released"]:
                    assert i > upto, f"ring deadlock at item {i}"
                    break
                if i > upto + NRING - 1:
                    break
                src, skey = items[i]
                slot = i % NRING
                S.dma("sp", lambda e, slot=slot, src=src: e.dma_start(out=ring[slot][:], in_=src), f"ring{slot}",
                      reads=[skey], writes=[f"ring{slot}"])
                rstate["issued"] += 1

        def ring_get(name):
            i = ridx[name]
            assert i >= rstate["nget"], f"ring order violated at {name}"
            rstate["nget"] = i
            ring_pump(i)
            assert rstate["issued"] > i
            return i % NRING

        def ring_release(name):
            i = ridx[name]
            rstate["released"].add(i)
            ring_pump(i)

        def r8(slot):
            return ring[slot][:].rearrange("p (k c) -> p k c", k=8)

        def r4(slot):
            return ring[slot][:].rearrange("p (j c) -> p j c", j=4)

        TRB = 2
        def load_tile(xsrc, cssrc, t, xs, cs):
            S.dma("act", lambda e: e.dma_start(out=xt[xs][:], in_=xsrc[t * 128:(t + 1) * 128, :]), f"x{xs}", writes=[f"xt{xs}"])
            if cssrc is not None:
                S.dma("act", lambda e: e.dma_start(out=cst[cs][:], in_=cssrc[t]), f"cs{cs}", writes=[f"cst{cs}"])

        def rstd_small(src, dst, n, srck, dstk):
            S.op("dve", lambda e: e.tensor_scalar(out=dst, in0=src, scalar1=1.0 / n, scalar2=EPS, op0=ALU.mult, op1=ALU.add),
                 reads=[srck], writes=[dstk])
            S.op("act", lambda e: e.activation(out=dst, in_=dst, func=AF.Sqrt), reads=[dstk], writes=[dstk])
            S.op("dve", lambda e: e.reciprocal(out=dst, in_=dst), reads=[dstk], writes=[dstk])

        ss1 = sb("ss1", [128, 1]); rs1 = sb("rs1", [128, 1])
        set0 = dict(xn=xn[:], xnk="xn", rA=rA[:], rAk="rA", rB=rB[:], rBk="rB", kr=kr[:], krk="kr", kz=kz[:], kzk="kz",
                    vb=vb[:], vbk="vb", ss=ss[:], ssk="ss", rs=rs[:], rsk="rs", trb=2)
        set1 = dict(xn=xt[4][:, 0:512].bitcast(BF16), xnk="xt4", rA=xt[5][:, 0:512], rAk="xt5", rB=xt[5][:, 512:1024], rBk="xt5",
                    kr=xt[6][:, 0:256].bitcast(BF16), krk="xt6", kz=xt[6][:, 256:512].bitcast(BF16), kzk="xt6",
                    vb=xt[7][:, 0:256].bitcast(BF16), vbk="xt7", ss=ss1[:], ssk="ss1", rs=rs1[:], rsk="rs1", trb=3)

        def norm_to_uT(xs, gt, gk, dstT, dk, tcol, B=set0):
            xk = f"xt{xs}"
            trb = B["trb"]
            S.op("act", lambda e: e.activation(out=junk[:], in_=xt[xs][:], func=AF.Square, accum_out=B["ss"]),
                 reads=[xk], writes=["junk", B["ssk"]])
            rstd_small(B["ss"], B["rs"], D, B["ssk"], B["rsk"])
            S.op("act", lambda e: e.activation(out=B["xn"], in_=xt[xs][:], func=AF.Copy, scale=B["rs"]),
                 reads=[xk, B["rsk"]], writes=[B["xnk"]])
            for k in range(8):
                S.op("pe", lambda e, k=k: e.transpose(out=pbf(trb)[:, k, :], in_=B["xn"][:, k * 128:(k + 1) * 128], identity=ident[:]),
                     reads=[B["xnk"], "ident"], writes=[f"pb{trb}"])
            S.op("dve", lambda e: e.tensor_tensor(out=dstT[:, :, tcol:tcol + 128], in0=pbf(trb),
                                                  in1=gt[:].unsqueeze(2).broadcast_to([128, 8, 128]), op=ALU.mult),
                 reads=[f"pb{trb}", gk], writes=[f"{dk}{tcol // 128}"])

        def tm_proj(gi, bank, tcol):
            for k in range(8):
                S.op("pe", lambda e, k=k: e.matmul(pb[bank][:], lhsT=uT[:, k, tcol:tcol + 128], rhs=wr[:, k, gi * 512:(gi + 1) * 512],
                                                   start=(k == 0), stop=(k == 7)),
                     reads=[f"uT{tcol // 128}", f"wr{gi}"], writes=[f"pb{bank}"])

        def rotary(bank, cs, dst, dstk, B=set0):
            pv = pb[bank][:].rearrange("p (h two j) -> p h two j", two=2, j=64)
            psw = pv[:, :, ::-1, :]
            cc = cst[cs][:, 0:128].rearrange("p (two j) -> p two j", two=2).unsqueeze(1).broadcast_to([128, 4, 2, 64])
            s_ = cst[cs][:, 128:256].rearrange("p (two j) -> p two j", two=2).unsqueeze(1).broadcast_to([128, 4, 2, 64])
            a4 = B["rA"].rearrange("p (h two j) -> p h two j", two=2, j=64)
            b4 = B["rB"].rearrange("p (h two j) -> p h two j", two=2, j=64)
            S.op("dve", lambda e: e.tensor_tensor(out=a4, in0=pv, in1=cc, op=ALU.mult), reads=[f"pb{bank}", f"cst{cs}"], writes=[B["rAk"]])
            S.op("dve", lambda e: e.tensor_tensor(out=b4, in0=psw, in1=s_, op=ALU.mult), reads=[f"pb{bank}", f"cst{cs}"], writes=[B["rBk"]])
            S.op("pool", lambda e: e.tensor_tensor(out=dst, in0=B["rA"], in1=B["rB"], op=ALU.add), reads=[B["rAk"], B["rBk"]], writes=[dstk])

        def kv_state(kvb, B=set0, need_sb=True):
            kr_, kz_, vb_ = B["kr"], B["kz"], B["vb"]
            S.op("pool", lambda e: e.tensor_tensor(out=kz_.rearrange("p (h n) -> p h n", h=4),
                                                   in0=kr_.rearrange("p (h n) -> p h n", h=4),
                                                   in1=ztt[:].unsqueeze(2).broadcast_to([128, 4, 128]), op=ALU.mult),
                 reads=[B["krk"], "ztt"], writes=[B["kzk"]])
            for h in range(NH):
                S.op("pe", lambda e, h=h: e.matmul(pb[kvb][:, h * 128:(h + 1) * 128], lhsT=kz_[:, h * 128:(h + 1) * 128],
                                                   rhs=vb_[:, h * 128:(h + 1) * 128], start=True, stop=True),
                     reads=[B["kzk"], B["vbk"]], writes=[f"pb{kvb}"])
            for h in range(NH):
                S.op("dve", lambda e, h=h: e.scalar_tensor_tensor(out=T32[:, h, :], in0=T32[:, h, :], scalar=cdec[h],
                                                                  in1=pb[kvb][:, h * 128:(h + 1) * 128], op0=ALU.mult, op1=ALU.add),
                     reads=["T32", f"pb{kvb}"], writes=["T32"])
            if need_sb:
                S.op("pool", lambda e: e.tensor_copy(out=Sb[:], in_=T32[:]), reads=["T32"], writes=["Sb"])

        def conv_front(n, rhs_of_k, rkeys, it_h, it_c):
            sl = ring_get(it_h)
            for c in range(4):
                bank = 4 + c
                for k in range(8):
                    S.op("pe", lambda e, k=k, c=c, bank=bank, sl=sl: e.matmul(pb[bank][:, 0:n], lhsT=r8(sl)[:, k, c * 128:(c + 1) * 128],
                                                                             rhs=rhs_of_k(k), start=(k == 0), stop=(k == 7)),
                         reads=rkeys + [f"ring{sl}"], writes=[f"pb{bank}"])
                S.op("act", lambda e, c=c, bank=bank: e.activation(out=chb[:, c, 2:2 + n], in_=pb[bank][:, 0:n], func=AF.Copy),
                     reads=[f"pb{bank}"], writes=[f"chb{c}"])
                yield 0
            ring_release(it_h)
            sl = ring_get(it_c)
            for c in range(4):
                bank = 4 + c
                for k in range(8):
                    S.op("pe", lambda e, k=k, c=c, bank=bank, sl=sl: e.matmul(pb[bank][:, 0:n], lhsT=r8(sl)[:, k, c * 128:(c + 1) * 128],
                                                                             rhs=rhs_of_k(k), start=(k == 0), stop=(k == 7)),
                         reads=rkeys + [f"ring{sl}"], writes=[f"pb{bank}"])
                S.op("dve", lambda e, c=c, bank=bank: e.tensor_tensor(out=chb[:, c, 2:2 + n], in0=pb[bank][:, 0:n], in1=chb[:, c, 2:2 + n], op=ALU.mult),
                     reads=[f"pb{bank}", f"chb{c}"], writes=[f"chb{c}"])
                yield 0
            ring_release(it_c)

        psets = [(set0, 5, 6), (set1, 7, 4)]

        def prefix_front(t):
            B, kb, vbk_ = psets[t % 2]
            xs, cs, tcol = t % 4, t % 2, (t % 2) * 128
            if cast_state["done"] < len(groups):
                cast_step()
            load_tile(xp, csp, t, xs, cs)
            norm_to_uT(xs, g1t, "g1t", uT, "uT", tcol, B)
            tm_proj(1, kb, tcol)
            tm_proj(2, vbk_, tcol)

        def prefix_back(t):
            B, kb, vbk_ = psets[t % 2]
            cs, tcol = t % 2, (t % 2) * 128
            rotary(kb, cs, B["kr"], B["krk"], B)
            S.op("act", lambda e: e.activation(out=B["vb"], in_=pb[vbk_][:], func=AF.Copy), reads=[f"pb{vbk_}"], writes=[B["vbk"]])
            kv_state(kb, B, need_sb=(t == nt_pre - 1))
            if t == nt_pre - 1:
                for _ in conv_front(128, lambda k: uT[:, k, tcol:tcol + 128], [f"uT{t % 2}"], ("h", -1), ("C", -1)):
                    pass
                for c in range(4):
                    S.op("pool", lambda e, c=c: e.tensor_copy(out=chb[:, c, 0:2], in_=chb[:, c, 128:130]),
                         reads=[f"chb{c}"], writes=[f"chb{c}"])

        npre = nt_pre if STOP != "casts" else 0
        if npre:
            prefix_front(0)
        for t in range(npre):
            if t + 1 < npre:
                prefix_front(t + 1)
            prefix_back(t)

        while cast_state["done"] < len(groups):
            cast_step()
        ukeys = ["uT0", "uT1", "uT2", "uT3"]
        u2keys = ["u2T0", "u2T1", "u2T2", "u2T3"]
        mkeys = ["mixc0", "mixc1", "mixc2", "mixc3"]

        def mixer(b):
            xs0 = (b % 2) * 4
            for tt in range(4):
                t = b * 4 + tt
                load_tile(xm, None, t, xs0 + tt, 0)
            for tt in range(4):
                norm_to_uT(xs0 + tt, g1t, "g1t", uT, "uT", tt * 128)
                yield 0
            yield 1
            yield from conv_front(512, lambda k: uT[:, k, :], ukeys, ("h", b), ("C", b))
            yield 1
            slB = ring_get(("B", b))
            for c in range(4):
                bank = 4 + c
                for k in range(8):
                    S.op("pe", lambda e, k=k, c=c, bank=bank, slB=slB: e.matmul(pb[bank][:], lhsT=r8(slB)[:, k, c * 128:(c + 1) * 128],
                                                                               rhs=uT[:, k, :], start=(k == 0), stop=(k == 7)),
                         reads=ukeys + [f"ring{slB}"], writes=[f"pb{bank}"])
                ck = f"chb{c}"
                S.op("act", lambda e, c=c: e.activation(out=ctmp[:], in_=chb[:, c, 0:512], func=AF.Copy, scale=cwt[:, c * 3:c * 3 + 1]),
                     reads=[ck, "cwt"], writes=["ctmp"])
                S.op("dve", lambda e, c=c: e.scalar_tensor_tensor(out=ctmp[:], in0=chb[:, c, 1:513], scalar=cwt[:, c * 3 + 1:c * 3 + 2],
                                                                  in1=ctmp[:], op0=ALU.mult, op1=ALU.add),
                     reads=[ck, "cwt", "ctmp"], writes=["ctmp"])
                S.op("dve", lambda e, c=c: e.scalar_tensor_tensor(out=ctmp[:], in0=chb[:, c, 2:514], scalar=cwt[:, c * 3 + 2:c * 3 + 3],
                                                                  in1=ctmp[:], op0=ALU.mult, op1=ALU.add),
                     reads=[ck, "cwt", "ctmp"], writes=["ctmp"])
                S.op("pool", lambda e, c=c: e.tensor_copy(out=chb[:, c, 0:2], in_=chb[:, c, 512:514]), reads=[ck], writes=[ck])
                S.op("dve", lambda e, bank=bank: e.tensor_tensor(out=cyv[:], in0=pb[bank][:], in1=ctmp[:], op=ALU.mult),
                     reads=[f"pb{bank}", "ctmp"], writes=["cyv"])
                S.op("act", lambda e: e.activation(out=csq[:], in_=cyv[:], func=AF.Square), reads=["cyv"], writes=["csq"])
                yield 0
                S.op("pe", lambda e: e.matmul(pb[3][:], lhsT=onesb[:], rhs=csq[:], start=True, stop=True),
                     reads=["onesb", "csq"], writes=["pb3"])
                S.op("act", lambda e: e.activation(out=csq[:], in_=pb[3][:], func=AF.Sqrt, bias=epst[:], scale=1.0),
                     reads=["pb3", "epst"], writes=["csq"])
                S.op("dve", lambda e: e.reciprocal(out=csq[:], in_=csq[:]), reads=["csq"], writes=["csq"])
                S.op("dve", lambda e, c=c: e.scalar_tensor_tensor(out=mixT[:, c, :], in0=cyv[:], scalar=cgt[:, c:c + 1], in1=csq[:],
                                                                  op0=ALU.mult, op1=ALU.mult),
                     reads=["cyv", "csq", "cgt"], writes=[f"mixc{c}"])
                yield 0
            ring_release(("B", b))
            yield 1
            for tt in range(4):
                t = b * 4 + tt
                cs = t % 2
                tc0 = tt * 128
                S.dma("act", lambda e, t=t, cs=cs: e.dma_start(out=cst[cs][:], in_=csm[t]), f"cs{cs}", writes=[f"cst{cs}"])
                for gi in range(4):
                    tm_proj(gi, 4 + gi, tc0)
                    yield 0
                rotary(4, cs, qr[:], "qr")
                rotary(5, cs, kr[:], "kr")
                S.op("act", lambda e: e.activation(out=vb[:], in_=pb[6][:], func=AF.Copy), reads=["pb6"], writes=["vb"])
                S.op("act", lambda e: e.activation(out=sg[:], in_=pb[7][:], func=AF.Silu), reads=["pb7"], writes=["sg"])
                yield 0
                for h in range(NH):
                    S.op("pe", lambda e, h=h: e.transpose(out=pbf(TRB)[:, h, :], in_=qr[:, h * 128:(h + 1) * 128], identity=ident[:]),
                         reads=["qr", "ident"], writes=[f"pb{TRB}"])
                S.op("act", lambda e: e.activation(out=qT[:], in_=pbf(TRB)[:, 0:4, :], func=AF.Copy), reads=[f"pb{TRB}"], writes=["qT"])
                S.op("dve", lambda e: e.tensor_tensor(out=qxT[:], in0=pbf(TRB)[:, 0:4, :], in1=xit[:], op=ALU.mult),
                     reads=[f"pb{TRB}", "xit"], writes=["qxT"])
                yield 0
                for h in range(NH):
                    S.op("pe", lambda e, h=h: e.transpose(out=pbf(3)[:, h, :], in_=kr[:, h * 128:(h + 1) * 128], identity=ident[:]),
                         reads=["kr", "ident"], writes=["pb3"])
                S.op("act", lambda e: e.activation(out=kT[:], in_=pbf(3)[:, 0:4, :], func=AF.Copy), reads=["pb3"], writes=["kT"])
                yield 0
                for h in range(NH):
                    S.op("pe", lambda e, h=h: e.matmul(pb[3][:, h * 128:(h + 1) * 128], lhsT=kT[:, h, :], rhs=qT[:, h, :], start=True, stop=True),
                         reads=["kT", "qT"], writes=["pb3"])
                S.op("dve", lambda e: e.tensor_tensor(out=sc[:], in0=pb[3][:].rearrange("p (h n) -> p h n", h=4), in1=maskt[:], op=ALU.mult),
                     reads=["pb3", "maskt"], writes=["sc"])
                yield 0
                for h in range(NH):
                    S.op("pe", lambda e, h=h: e.matmul(pb[4][:, h * 128:(h + 1) * 128], lhsT=sc[:, h, :], rhs=vb[:, h * 128:(h + 1) * 128],
                                                       start=True, stop=False), reads=["sc", "vb"], writes=["pb4"])
                    S.op("pe", lambda e, h=h: e.matmul(pb[4][:, h * 128:(h + 1) * 128], lhsT=qxT[:, h, :], rhs=Sb[:, h, :],
                                                       start=False, stop=True), reads=["qxT", "Sb"], writes=["pb4"])
                kv_state(5)
                yield 0
                for h in range(NH):
                    S.op("act", lambda e, h=h: e.activation(out=junk[:, 0:128], in_=pb[4][:, h * 128:(h + 1) * 128], func=AF.Square,
                                                            accum_out=ssr[:, h:h + 1]), reads=["pb4"], writes=["junk", "ssr"])
                rstd_small(ssr[:], rr[:], 128, "ssr", "rr")
                for h in range(NH):
                    S.op("dve", lambda e, h=h: e.scalar_tensor_tensor(out=y2[:, h * 128:(h + 1) * 128], in0=pb[4][:, h * 128:(h + 1) * 128],
                                                                      scalar=rr[:, h:h + 1], in1=sg[:, h * 128:(h + 1) * 128],
                                                                      op0=ALU.mult, op1=ALU.mult),
                         reads=["pb4", "rr", "sg"], writes=["y2"])
                yield 0
                for h in range(NH):
                    S.op("pe", lambda e, h=h: e.transpose(out=pbf(TRB)[:, h, :], in_=y2[:, h * 128:(h + 1) * 128], identity=ident[:]),
                         reads=["y2", "ident"], writes=[f"pb{TRB}"])
                for h in range(NH):
                    S.op("act", lambda e, h=h, tc0=tc0: e.activation(out=mixT[:, 4 + h, tc0:tc0 + 128], in_=pbf(TRB)[:, h, :], func=AF.Copy,
                                                                     scale=rgt[:, h:h + 1]),
                         reads=[f"pb{TRB}", "rgt"], writes=[f"mixr{tt}"])
                yield 1
            slo = [ring_get(("wo0", b)), ring_get(("wo1", b))]
            for tt in range(4):
                tc0 = tt * 128
                xs = xs0 + tt
                for hf in range(2):
                    bank = 6 + hf
                    for c in range(8):
                        S.op("pe", lambda e, c=c, bank=bank, tc0=tc0, so=slo[hf]: e.matmul(pb[bank][:], lhsT=mixT[:, c, tc0:tc0 + 128],
                                                                                          rhs=r8(so)[:, c, :], start=(c == 0), stop=(c == 7)),
                             reads=mkeys + [f"mixr{tt}", f"ring{slo[hf]}"], writes=[f"pb{bank}"])
                    S.op("dve", lambda e, hf=hf, bank=bank, xs=xs: e.tensor_tensor(out=xt[xs][:, hf * 512:(hf + 1) * 512], in0=pb[bank][:],
                                                                                  in1=xt[xs][:, hf * 512:(hf + 1) * 512], op=ALU.add),
                         reads=[f"pb{bank}", f"xt{xs}"], writes=[f"xt{xs}"])
                yield 0
            ring_release(("wo0", b))
            ring_release(("wo1", b))
            yield 2
            for tt in range(4):
                norm_to_uT(xs0 + tt, g2t, "g2t", u2T, "u2T", tt * 128)
                yield 2
            yield 1

        def ffn_up_chunk(b, g, jj, sl):
            j = g * 4 + jj
            bank = j % 2
            for k in range(8):
                S.op("pe", lambda e, k=k: e.matmul(pb[bank][:], lhsT=r8(sl)[:, k, jj * 128:(jj + 1) * 128],
                                                   rhs=u2T[:, k, :], start=(k == 0), stop=(k == 7)),
                     reads=u2keys + [f"ring{sl}"], writes=[f"pb{bank}"])
            ri = j % 2
            if j % 2 == 0:
                S.op("act", lambda e: e.activation(out=rl[ri][:], in_=pb[bank][:], func=AF.Relu),
                     reads=[f"pb{bank}"], writes=[f"rl{ri}"])
            else:
                S.op("dve", lambda e: e.tensor_scalar(out=rl[ri][:], in0=pb[bank][:], scalar1=0.0, scalar2=None, op0=ALU.max),
                     reads=[f"pb{bank}"], writes=[f"rl{ri}"])
            S.op("pool", lambda e: e.tensor_tensor(out=hT[:, j, :], in0=rl[ri][:], in1=rl[ri][:], op=ALU.mult),
                 reads=[f"rl{ri}"], writes=[f"hT{j}"])

        def ffn_down(b):
            for g in range(8):
                sl = ring_get(("wd", b, g))
                for tt in range(4):
                    for hf in range(2):
                        bank = tt * 2 + hf
                        for jj in range(4):
                            j = g * 4 + jj
                            S.op("pe", lambda e, jj=jj, j=j, hf=hf, bank=bank, tt=tt, sl=sl: e.matmul(
                                pb[bank][:], lhsT=hT[:, j, tt * 128:(tt + 1) * 128], rhs=r4(sl)[:, jj, hf * 512:(hf + 1) * 512],
                                start=(j == 0), stop=(j == 31)),
                                reads=[f"hT{j}", f"ring{sl}"], writes=[f"pb{bank}"])
                ring_release(("wd", b, g))

        def finish(b):
            xs0 = (b % 2) * 4
            for tt in range(4):
                t = b * 4 + tt
                xs = xs0 + tt
                xk = f"xt{xs}"
                for hf in range(2):
                    bank = tt * 2 + hf
                    S.op("dve", lambda e, hf=hf, bank=bank, xs=xs: e.tensor_tensor(out=xt[xs][:, hf * 512:(hf + 1) * 512], in0=pb[bank][:],
                                                                                  in1=xt[xs][:, hf * 512:(hf + 1) * 512], op=ALU.add),
                         reads=[f"pb{bank}", xk], writes=[xk])
                S.op("act", lambda e, xs=xs: e.activation(out=junk[:], in_=xt[xs][:], func=AF.Square, accum_out=ss[:]),
                     reads=[xk], writes=["junk", "ss"])
                rstd_small(ss[:], rs[:], D, "ss", "rs")
                S.op("act", lambda e, xs=xs: e.activation(out=xt[xs][:], in_=xt[xs][:], func=AF.Copy, scale=rs[:]),
                     reads=[xk, "rs"], writes=[xk])
                S.op("pool", lambda e, xs=xs: e.tensor_tensor(out=xt[xs][:], in0=xt[xs][:], in1=gft[:], op=ALU.mult),
                     reads=[xk, "gft"], writes=[xk])
                S.dma("pool", lambda e, t=t, xs=xs: e.dma_start(out=y[t * 128:(t + 1) * 128, :], in_=xt[xs][:]), f"out{xs}", reads=[xk])

        if STOP not in ("casts", "prefix"):
            for _ in mixer(0):
                pass
            for b in range(nblk):
                gen = mixer(b + 1) if b + 1 < nblk else None
                for g in range(8):
                    sl = ring_get(("wu", b, g))
                    at_boundary = gen is None
                    hold = False
                    for jj in range(4):
                        ffn_up_chunk(b, g, jj, sl)
                        if not at_boundary and not hold:
                            v = next(gen)
                            at_boundary = (v == 1)
                            hold = (v == 2)
                    ring_release(("wu", b, g))
                    while not at_boundary:
                        at_boundary = (next(gen) == 1)
                ffn_down(b)
                finish(b)
        S.emit(st)
    return nc


def _host_consts():
    f32 = np.float32
    lg = np.log(f32(1.0) - f32(2.0) ** (f32(-5.0) - np.arange(NH, dtype=f32))).astype(f32)
    idx = np.arange(128, dtype=f32)
    diff = idx[:, None] - idx[None, :]
    intra = np.where(diff[None] >= 0, np.exp(lg[:, None, None] * np.maximum(diff, 0.0)[None]), 0.0).astype(f32)
    scale = f32(128.0 ** -0.5)
    maskT = np.ascontiguousarray(np.transpose(intra, (2, 0, 1)) * scale).astype(f32).reshape(128, 512)
    zeta = np.exp(lg[:, None] * (f32(127.0) - idx)[None]).astype(f32)
    ztt = np.ascontiguousarray(zeta.T * scale).astype(f32)
    xi = np.exp(lg[:, None] * (idx + f32(1.0))[None]).astype(f32)
    xit = np.ascontiguousarray(np.broadcast_to(xi.reshape(1, 512), (128, 512))).astype(f32)
    ident = np.eye(128, dtype=f32).astype(ml_dtypes.bfloat16)
    onesblk = np.zeros((128, 128), f32)
    onesblk[:64, :64] = 1.0 / 64
    onesblk[64:, 64:] = 1.0 / 64
    return dict(maskt=maskT, ztt=ztt, xit=xit, ident=ident, onesblk=onesblk)


def _cs_table(pos0, ntiles):
    f32 = np.float32
    half = 64
    inv_freq = (f32(1.0) / (f32(10000.0) ** (np.arange(half, dtype=f32) / f32(half)))).astype(f32)
    pos = (pos0 + np.arange(ntiles * 128)).astype(f32)
    ang = (pos[:, None] * inv_freq[None, :]).astype(f32)
    c = np.cos(ang).astype(f32)
    s = np.sin(ang).astype(f32)
    tab = np.concatenate([c, c, -s, s], axis=1).astype(f32)
    return np.ascontiguousarray(tab.reshape(ntiles, 128, 256))


_NC_CACHE = {}


def kernel(x, norm1_g, w_in, conv_w, conv_norm_g, ret_norm_g, w_out, norm2_g, w_up, w_down, final_norm_g):
    x = np.asarray(x, dtype=np.float32)
    B, SEQ, _ = x.shape
    half = SEQ // 2
    nt = half // 128
    key = (nt, nt)
    if key not in _NC_CACHE:
        _NC_CACHE[key] = build_nc(nt, nt)
    nc = _NC_CACHE[key]
    f32 = np.float32
    consts = _host_consts()
    shared = dict(
        w_in=np.ascontiguousarray(w_in, dtype=f32), w_out=np.ascontiguousarray(w_out, dtype=f32),
        w_up=np.ascontiguousarray(w_up, dtype=f32), w_down=np.ascontiguousarray(w_down, dtype=f32),
        g1t=np.ascontiguousarray(np.asarray(norm1_g, f32).reshape(8, 128).T),
        g2t=np.ascontiguousarray(np.asarray(norm2_g, f32).reshape(8, 128).T),
        gft=np.ascontiguousarray(np.broadcast_to(np.asarray(final_norm_g, f32)[None, :], (128, D))),
        cwt=np.ascontiguousarray(np.asarray(conv_w, f32).reshape(3, 4, 128).transpose(2, 1, 0).reshape(128, 12)),
        cgt=np.ascontiguousarray(np.asarray(conv_norm_g, f32).reshape(4, 128).T),
        rgt=np.ascontiguousarray(np.asarray(ret_norm_g, f32).reshape(4, 128).T),
        **consts,
    )
    cs_lo = _cs_table(0, nt)
    cs_hi = _cs_table(half, nt)
    in_maps = []
    for c in range(8):
        b, hf = c // 2, c % 2
        m = dict(shared)
        m["xm"] = np.ascontiguousarray(x[b, hf * half:(hf + 1) * half])
        m["xp"] = np.ascontiguousarray(x[b, 0:half]) if hf == 1 else np.zeros((half, D), f32)
        m["csm"] = cs_hi if hf == 1 else cs_lo
        m["csp"] = cs_lo
        in_maps.append(m)
    res = run_bass_kernel_spmd(nc, in_maps, core_ids=list(range(8)))
    if DEBUG:
        kernel.dbg = [dict(r) for r in res.results]
    out = np.empty((B, SEQ, D), f32)
    for c in range(8):
        b, hf = c // 2, c % 2
        out[b, hf * half:(hf + 1) * half] = res.results[c]["y"]
    return out
```

```python
import numpy as np
import ml_dtypes
from contextlib import ExitStack
import concourse.bass as bass
import concourse.mybir as mybir
from concourse.bass_utils import run_bass_kernel_spmd

F32 = mybir.dt.float32
BF16 = mybir.dt.bfloat16
AF = mybir.ActivationFunctionType
ALU = mybir.AluOpType

D = 1024
NH = 4
EPS = 1e-6
NRING = 4
DEBUG = False
STOP = None


class Sched:
    ENGS = ("pe", "act", "dve", "pool", "sp")

    def __init__(self, nc):
        self.nc = nc
        self.ops = []
        self.last_w = {}
        self.readers = {}

    def _deps(self, reads, writes):
        deps = {}
        for k in reads:
            w = self.last_w.get(k)
            if w is not None:
                deps[w] = True
        for k in writes:
            w = self.last_w.get(k)
            if w is not None:
                deps.setdefault(w, False)
            for r in self.readers.get(k, ()):
                deps.setdefault(r, False)
        return deps

    def _commit(self, idx, reads, writes):
        for k in reads:
            self.readers.setdefault(k, []).append(idx)
        for k in writes:
            self.last_w[k] = idx
            self.readers[k] = []

    def op(self, eng, fn, reads=(), writes=()):
        writes = list(writes) + [k for k in reads if k.startswith("pb") and k not in writes]
        deps = self._deps(reads, writes)
        idx = len(self.ops)
        self.ops.append(dict(eng=eng, fn=fn, deps=deps, kind="c", sig=False))
        self._commit(idx, reads, writes)
        return idx

    def dma(self, eng, fn, sem, reads=(), writes=()):
        deps = self._deps(reads, writes)
        idx = len(self.ops)
        self.ops.append(dict(eng=eng, fn=fn, deps=deps, kind="d", sem=sem))
        self._commit(idx, reads, writes)
        return idx

    def emit(self, stack):
        nc = self.nc
        ops = self.ops

        def skip(p, o):
            return (p["kind"] == "c" and o["kind"] == "c" and p["eng"] == o["eng"])

        for o in ops:
            for d, raw in o["deps"].items():
                p = ops[d]
                if p["kind"] == "d":
                    continue
                if skip(p, o) and (p["eng"] == "pe" or not raw):
                    continue
                p["sig"] = True
        esem = {e: stack.enter_context(nc.semaphore("s_" + e)) for e in ("pe", "act", "dve", "pool")}
        dsem, dcnt = {}, {}
        cnt = {e: 0 for e in esem}
        for o in ops:
            if o["kind"] == "c":
                if o["sig"]:
                    cnt[o["eng"]] += 1
                    o["val"] = cnt[o["eng"]]
            else:
                s = o["sem"]
                if s not in dsem:
                    dsem[s] = stack.enter_context(nc.semaphore("d_" + s))
                    dcnt[s] = 0
                dcnt[s] += 16
                o["val"] = dcnt[s]
        per_eng = {e: [] for e in self.ENGS}
        for i, o in enumerate(ops):
            per_eng[o["eng"]].append(i)
        seen = {e: {} for e in self.ENGS}

        def run(eng_name, engine):
            sn = seen[eng_name]
            for i in per_eng[eng_name]:
                o = ops[i]
                waits = {}
                for d, raw in o["deps"].items():
                    p = ops[d]
                    if p["kind"] == "d":
                        key, h = ("d", p["sem"]), dsem[p["sem"]]
                    else:
                        if skip(p, o) and (eng_name == "pe" or not raw):
                            continue
                        key, h = ("c", p["eng"]), esem[p["eng"]]
                    v = p["val"]
                    if key not in waits or waits[key][1] < v:
                        waits[key] = (h, v)
                for key, (h, v) in waits.items():
                    if sn.get(key, 0) >= v:
                        continue
                    sn[key] = v
                    engine.wait_ge(h, v)
                ins = o["fn"](engine)
                if o["kind"] == "d":
                    ins.then_inc(dsem[o["sem"]], 16)
                elif o["sig"]:
                    ins.then_inc(esem[o["eng"]], 1)
            if eng_name == "sp":
                for s in dsem:
                    engine.wait_ge(dsem[s], dcnt[s])

        block = stack.enter_context(nc.Block())
        block.tensor(lambda e: run("pe", e))
        block.scalar(lambda e: run("act", e))
        block.vector(lambda e: run("dve", e))
        block.gpsimd(lambda e: run("pool", e))
        block.sync(lambda e: run("sp", e))


def build_nc(nt_main=32, nt_pre=32):
    assert nt_main % 4 == 0
    nblk = nt_main // 4
    lg = [float(np.log(np.float32(1.0) - np.float32(2.0) ** np.float32(-5.0 - h))) for h in range(NH)]
    cdec = [float(np.exp(np.float32(lg[h]) * np.float32(128.0))) for h in range(NH)]

    nc = bass.Bass("TRN2", target_bir_lowering=False)

    def din(name, shape, dt=F32):
        return nc.dram_tensor(name, shape, dt, kind="ExternalInput").ap()

    xm = din("xm", [nt_main * 128, D])
    xp = din("xp", [nt_pre * 128, D])
    csm = din("csm", [nt_main, 128, 256])
    csp = din("csp", [nt_pre, 128, 256])
    w_in = din("w_in", [D, 3584])
    w_out = din("w_out", [D, D])
    w_up = din("w_up", [D, 4096])
    w_down = din("w_down", [4096, D])
    d_g1 = din("g1t", [128, 8])
    d_g2 = din("g2t", [128, 8])
    d_gf = din("gft", [128, D])
    d_cw = din("cwt", [128, 12])
    d_cg = din("cgt", [128, 4])
    d_rg = din("rgt", [128, 4])
    d_zt = din("ztt", [128, 4])
    d_mask = din("maskt", [128, 512])
    d_xi = din("xit", [128, 512])
    d_id = din("ident", [128, 128], BF16)
    d_ones = din("onesblk", [128, 128])
    y = nc.dram_tensor("y", [nt_main * 128, D], F32, kind="ExternalOutput").ap()
    if DEBUG:
        dbg_x1 = nc.dram_tensor("dbg_x1", [nt_main * 128, D], F32, kind="ExternalOutput").ap()
        dbg_mix = nc.dram_tensor("dbg_mix", [nt_main // 4, 128, 4096], F32, kind="ExternalOutput").ap()
    wc_scr = nc.dram_tensor("wc_scr", [3, 128, 4096], BF16, kind="Internal").ap()
    wo_scr = nc.dram_tensor("wo_scr", [2, 128, 4096], BF16, kind="Internal").ap()
    wu_scr = nc.dram_tensor("wu_scr", [8, 128, 4096], BF16, kind="Internal").ap()
    wd_scr = nc.dram_tensor("wd_scr", [8, 128, 4096], BF16, kind="Internal").ap()

    with ExitStack() as st:
        S = Sched(nc)

        def sb(name, shape, dt=F32):
            return st.enter_context(nc.sbuf_tensor("s_" + name, shape, dt))

        wr = sb("wr", [128, 8, 2048], BF16)
        ring = [sb(f"ring{i}", [128, 4096], BF16) for i in range(NRING)]
        xt = [sb(f"xt{i}", [128, D]) for i in range(8)]
        cst = [sb(f"cst{i}", [128, 256]) for i in range(2)]
        uT = sb("uT", [128, 8, 512], BF16)
        u2T = sb("u2T", [128, 8, 512], BF16)
        mixT = sb("mixT", [128, 8, 512], BF16)
        hT = sb("hT", [128, 32, 512], BF16)
        chb = sb("chb", [128, 4, 514])
        ctmp = sb("ctmp", [128, 512])
        cyv = sb("cyv", [128, 512])
        csq = sb("csq", [128, 512])
        junk = sb("junk", [128, D])
        xn = sb("xn", [128, D], BF16)
        ss = sb("ss", [128, 1])
        rs = sb("rs", [128, 1])
        rA = sb("rA", [128, 512])
        rB = sb("rB", [128, 512])
        qr = sb("qr", [128, 512], BF16)
        kr = sb("kr", [128, 512], BF16)
        kz = sb("kz", [128, 512], BF16)
        vb = sb("vb", [128, 512], BF16)
        sg = sb("sg", [128, 512])
        qT = sb("qT", [128, 4, 128], BF16)
        qxT = sb("qxT", [128, 4, 128], BF16)
        kT = sb("kT", [128, 4, 128], BF16)
        sc = sb("sc", [128, 4, 128], BF16)
        T32 = sb("T32", [128, 4, 128])
        Sb = sb("Sb", [128, 4, 128], BF16)
        y2 = sb("y2", [128, 512], BF16)
        ssr = sb("ssr", [128, 4])
        rr = sb("rr", [128, 4])
        rl = [sb(f"rl{i}", [128, 512]) for i in range(2)]
        g1t = sb("g1t", [128, 8]); g2t = sb("g2t", [128, 8]); gft = sb("gft", [128, D])
        cwt = sb("cwt", [128, 12]); cgt = sb("cgt", [128, 4]); rgt = sb("rgt", [128, 4])
        ztt = sb("ztt", [128, 4]); maskt = sb("maskt", [128, 4, 128]); xit = sb("xit", [128, 4, 128])
        ident = sb("ident", [128, 128], BF16); onesb = sb("onesb", [128, 128])
        epst = sb("epst", [128, 1])
        pb = [st.enter_context(nc.psum_tensor(f"pb{i}", [128, 512], F32)) for i in range(8)]

        def pbf(i):
            return pb[i][:].bitcast(BF16).rearrange("p (k n) -> p k n", n=128)

        for dst, src, key in ((g1t, d_g1, "g1t"), (g2t, d_g2, "g2t"), (gft, d_gf, "gft"), (cwt, d_cw, "cwt"),
                              (cgt, d_cg, "cgt"), (rgt, d_rg, "rgt"), (ztt, d_zt, "ztt"), (ident, d_id, "ident"),
                              (onesb, d_ones, "onesb")):
            S.dma("sp", lambda e, dst=dst, src=src: e.dma_start(out=dst[:], in_=src), "c_" + key, writes=[key])
        S.dma("sp", lambda e: e.dma_start(out=maskt[:], in_=d_mask.rearrange("p (h n) -> p h n", h=4)), "c_maskt", writes=["maskt"])
        S.dma("sp", lambda e: e.dma_start(out=xit[:], in_=d_xi.rearrange("p (h n) -> p h n", h=4)), "c_xit", writes=["xit"])
        S.op("pool", lambda e: e.memset(epst[:], EPS), writes=["epst"])
        S.op("pool", lambda e: e.memset(T32[:], 0.0), writes=["T32"])
        S.op("pool", lambda e: e.memset(Sb[:], 0.0), writes=["Sb"])
        S.op("pool", lambda e: e.memset(chb[:], 0.0), writes=["chb0", "chb1", "chb2", "chb3"])
        stg = [hT[:, 16 * i:16 * (i + 1), :].rearrange("p a b -> p (a b)").bitcast(F32) for i in range(2)]
        stgk = [[f"hT{j}" for j in range(16 * i, 16 * (i + 1))] for i in range(2)]
        groups = []
        for gi in (1, 2, 0, 3):
            groups.append((w_in[:, 1536 + gi * 512:1536 + (gi + 1) * 512].rearrange("(k p) c -> p k c", p=128), 8,
                           wr[:, :, gi * 512:(gi + 1) * 512], None, f"wr{gi}"))
        for si, col0 in ((0, 1024), (1, 512), (2, 0)):
            groups.append((w_in[:, col0:col0 + 512].rearrange("(k p) c -> p k c", p=128), 8, None, wc_scr[si], f"wc_scr{si}"))
        for hf in range(2):
            groups.append((w_out[:, hf * 512:(hf + 1) * 512].rearrange("(k p) c -> p k c", p=128), 8, None, wo_scr[hf], f"wo_scr{hf}"))
        for g in range(8):
            groups.append((w_up[:, g * 512:(g + 1) * 512].rearrange("(k p) c -> p k c", p=128), 8, None, wu_scr[g], f"wu_scr{g}"))
        for g in range(8):
            groups.append((w_down[g * 512:(g + 1) * 512, :].rearrange("(j p) c -> p j c", p=128), 4, None, wd_scr[g], f"wd_scr{g}"))

        def cast_load(i):
            src, a, dsb, scr, key = groups[i]
            si = i % 2
            S.dma("sp", lambda e: e.dma_start(out=stg[si].rearrange("p (a b) -> p a b", a=a), in_=src), f"stg{si}", writes=stgk[si])

        def cast_do(i):
            src, a, dsb, scr, key = groups[i]
            si = i % 2
            eng = "act" if i % 2 == 0 else "dve"
            if dsb is not None:
                dst, wk = dsb, [key]
                src_v = stg[si].rearrange("p (a b) -> p a b", a=a)
            else:
                slot = i % NRING
                dst, wk = ring[slot][:], [f"ring{slot}"]
                src_v = stg[si]
            if eng == "act":
                S.op("act", lambda e: e.activation(out=dst, in_=src_v, func=AF.Copy), reads=stgk[si], writes=wk)
            else:
                S.op("dve", lambda e: e.tensor_copy(out=dst, in_=src_v), reads=stgk[si], writes=wk)
            if scr is not None:
                S.dma("sp", lambda e: e.dma_start(out=scr, in_=dst), "k_" + key, reads=wk, writes=[key])

        cast_state = dict(loaded=0, done=0)

        def cast_step():
            if cast_state["loaded"] < len(groups) and cast_state["loaded"] <= cast_state["done"] + 1:
                cast_load(cast_state["loaded"])
                cast_state["loaded"] += 1
            if cast_state["loaded"] < len(groups) and cast_state["loaded"] <= cast_state["done"] + 1:
                cast_load(cast_state["loaded"])
                cast_state["loaded"] += 1
            if cast_state["done"] < cast_state["loaded"]:
                cast_do(cast_state["done"])
                cast_state["done"] += 1

        n_upfront = 4 if nt_pre >= 28 else len(groups)
        while cast_state["done"] < n_upfront:
            cast_step()

        items = []
        ridx = {}

        def add_item(name, src, key):
            ridx[name] = len(items)
            items.append((src, key))

        def add_mixer_items(b, part):
            if part == 1:
                add_item(("h", b), wc_scr[0], "wc_scr0"); add_item(("C", b), wc_scr[1], "wc_scr1")
            elif part == 2:
                add_item(("B", b), wc_scr[2], "wc_scr2")
            elif part == 7:
                add_item(("wo0", b), wo_scr[0], "wo_scr0"); add_item(("wo1", b), wo_scr[1], "wo_scr1")

        add_item(("h", -1), wc_scr[0], "wc_scr0"); add_item(("C", -1), wc_scr[1], "wc_scr1")
        for part in range(8):
            add_mixer_items(0, part)
        for b in range(nblk):
            for g in range(8):
                add_item(("wu", b, g), wu_scr[g], f"wu_scr{g}")
                if b + 1 < nblk:
                    add_mixer_items(b + 1, g)
            for g in range(8):
                add_item(("wd", b, g), wd_scr[g], f"wd_scr{g}")
        rstate = dict(issued=0, released=set(), nget=0)

        def ring_pump(upto):
            while rstate["issued"] < len(items):
                i = rstate["issued"]
                if i >= NRING and (i - NRING) not in rstate["released"]:
                    assert i > upto, f"ring deadlock at item {i}"
                    break
                if i > upto + NRING - 1:
                    break
                src, skey = items[i]
                slot = i % NRING
                S.dma("sp", lambda e, slot=slot, src=src: e.dma_start(out=ring[slot][:], in_=src), f"ring{slot}",
                      reads=[skey], writes=[f"ring{slot}"])
                rstate["issued"] += 1

        def ring_get(name):
            i = ridx[name]
            assert i >= rstate["nget"], f"ring order violated at {name}"
            rstate["nget"] = i
            ring_pump(i)
            assert rstate["issued"] > i
            return i % NRING

        def ring_release(name):
            i = ridx[name]
            rstate["released"].add(i)
            ring_pump(i)

        def r8(slot):
            return ring[slot][:].rearrange("p (k c) -> p k c", k=8)

        def r4(slot):
            return ring[slot][:].rearrange("p (j c) -> p j c", j=4)

        TRB = 2
        def load_tile(xsrc, cssrc, t, xs, cs):
            S.dma("act", lambda e: e.dma_start(out=xt[xs][:], in_=xsrc[t * 128:(t + 1) * 128, :]), f"x{xs}", writes=[f"xt{xs}"])
            if cssrc is not None:
                S.dma("act", lambda e: e.dma_start(out=cst[cs][:], in_=cssrc[t]), f"cs{cs}", writes=[f"cst{cs}"])

        def rstd_small(src, dst, n, srck, dstk):
            S.op("act", lambda e: e.activation(out=dst, in_=src, func=AF.Sqrt, bias=epst[:], scale=1.0 / n),
                 reads=[srck, "epst"], writes=[dstk])
            S.op("dve", lambda e: e.reciprocal(out=dst, in_=dst), reads=[dstk], writes=[dstk])

        ss1 = sb("ss1", [128, 1]); rs1 = sb("rs1", [128, 1])
        set0 = dict(xn=xn[:], xnk="xn", rA=rA[:], rAk="rA", rB=rB[:], rBk="rB", kr=kr[:], krk="kr", kz=kz[:], kzk="kz",
                    vb=vb[:], vbk="vb", ss=ss[:], ssk="ss", rs=rs[:], rsk="rs", trb=2)
        set1 = dict(xn=xt[4][:, 0:512].bitcast(BF16), xnk="xt4", rA=xt[5][:, 0:512], rAk="xt5", rB=xt[5][:, 512:1024], rBk="xt5",
                    kr=xt[6][:, 0:256].bitcast(BF16), krk="xt6", kz=xt[6][:, 256:512].bitcast(BF16), kzk="xt6",
                    vb=xt[7][:, 0:256].bitcast(BF16), vbk="xt7", ss=ss1[:], ssk="ss1", rs=rs1[:], rsk="rs1", trb=3)

        def norm_to_uT(xs, gt, gk, dstT, dk, tcol, B=set0):
            xk = f"xt{xs}"
            trb = B["trb"]
            S.op("act", lambda e: e.activation(out=junk[:], in_=xt[xs][:], func=AF.Square, accum_out=B["ss"]),
                 reads=[xk], writes=["junk", B["ssk"]])
            rstd_small(B["ss"], B["rs"], D, B["ssk"], B["rsk"])
            S.op("act", lambda e: e.activation(out=B["xn"], in_=xt[xs][:], func=AF.Copy, scale=B["rs"]),
                 reads=[xk, B["rsk"]], writes=[B["xnk"]])
            for k in range(8):
                S.op("pe", lambda e, k=k: e.transpose(out=pbf(trb)[:, k, :], in_=B["xn"][:, k * 128:(k + 1) * 128], identity=ident[:]),
                     reads=[B["xnk"], "ident"], writes=[f"pb{trb}"])
            S.op("dve", lambda e: e.tensor_tensor(out=dstT[:, :, tcol:tcol + 128], in0=pbf(trb),
                                                  in1=gt[:].unsqueeze(2).broadcast_to([128, 8, 128]), op=ALU.mult),
                 reads=[f"pb{trb}", gk], writes=[f"{dk}{tcol // 128}"])

        def tm_proj(gi, bank, tcol):
            for k in range(8):
                S.op("pe", lambda e, k=k: e.matmul(pb[bank][:], lhsT=uT[:, k, tcol:tcol + 128], rhs=wr[:, k, gi * 512:(gi + 1) * 512],
                                                   start=(k == 0), stop=(k == 7)),
                     reads=[f"uT{tcol // 128}", f"wr{gi}"], writes=[f"pb{bank}"])

        def rotary(bank, cs, dst, dstk, B=set0):
            pv = pb[bank][:].rearrange("p (h two j) -> p h two j", two=2, j=64)
            psw = pv[:, :, ::-1, :]
            cc = cst[cs][:, 0:128].rearrange("p (two j) -> p two j", two=2).unsqueeze(1).broadcast_to([128, 4, 2, 64])
            s_ = cst[cs][:, 128:256].rearrange("p (two j) -> p two j", two=2).unsqueeze(1).broadcast_to([128, 4, 2, 64])
            a4 = B["rA"].rearrange("p (h two j) -> p h two j", two=2, j=64)
            b4 = B["rB"].rearrange("p (h two j) -> p h two j", two=2, j=64)
            S.op("dve", lambda e: e.tensor_tensor(out=a4, in0=pv, in1=cc, op=ALU.mult), reads=[f"pb{bank}", f"cst{cs}"], writes=[B["rAk"]])
            S.op("dve", lambda e: e.tensor_tensor(out=b4, in0=psw, in1=s_, op=ALU.mult), reads=[f"pb{bank}", f"cst{cs}"], writes=[B["rBk"]])
            S.op("pool", lambda e: e.tensor_tensor(out=dst, in0=B["rA"], in1=B["rB"], op=ALU.add), reads=[B["rAk"], B["rBk"]], writes=[dstk])

        def kv_state(kvb, B=set0, need_sb=True):
            kr_, kz_, vb_ = B["kr"], B["kz"], B["vb"]
            S.op("pool", lambda e: e.tensor_tensor(out=kz_.rearrange("p (h n) -> p h n", h=4),
                                                   in0=kr_.rearrange("p (h n) -> p h n", h=4),
                                                   in1=ztt[:].unsqueeze(2).broadcast_to([128, 4, 128]), op=ALU.mult),
                 reads=[B["krk"], "ztt"], writes=[B["kzk"]])
            for h in range(NH):
                S.op("pe", lambda e, h=h: e.matmul(pb[kvb][:, h * 128:(h + 1) * 128], lhsT=kz_[:, h * 128:(h + 1) * 128],
                                                   rhs=vb_[:, h * 128:(h + 1) * 128], start=True, stop=True),
                     reads=[B["kzk"], B["vbk"]], writes=[f"pb{kvb}"])
            for h in range(NH):
                S.op("dve", lambda e, h=h: e.scalar_tensor_tensor(out=T32[:, h, :], in0=T32[:, h, :], scalar=cdec[h],
                                                                  in1=pb[kvb][:, h * 128:(h + 1) * 128], op0=ALU.mult, op1=ALU.add),
                     reads=["T32", f"pb{kvb}"], writes=["T32"])
            if need_sb:
                S.op("pool", lambda e: e.tensor_copy(out=Sb[:], in_=T32[:]), reads=["T32"], writes=["Sb"])

        def conv_front(n, rhs_of_k, rkeys, it_h, it_c):
            sl = ring_get(it_h)
            for c in range(4):
                bank = 4 + c
                for k in range(8):
                    S.op("pe", lambda e, k=k, c=c, bank=bank, sl=sl: e.matmul(pb[bank][:, 0:n], lhsT=r8(sl)[:, k, c * 128:(c + 1) * 128],
                                                                             rhs=rhs_of_k(k), start=(k == 0), stop=(k == 7)),
                         reads=rkeys + [f"ring{sl}"], writes=[f"pb{bank}"])
                S.op("act", lambda e, c=c, bank=bank: e.activation(out=chb[:, c, 2:2 + n], in_=pb[bank][:, 0:n], func=AF.Copy),
                     reads=[f"pb{bank}"], writes=[f"chb{c}"])
                yield 0
            ring_release(it_h)
            sl = ring_get(it_c)
            for c in range(4):
                bank = 4 + c
                for k in range(8):
                    S.op("pe", lambda e, k=k, c=c, bank=bank, sl=sl: e.matmul(pb[bank][:, 0:n], lhsT=r8(sl)[:, k, c * 128:(c + 1) * 128],
                                                                             rhs=rhs_of_k(k), start=(k == 0), stop=(k == 7)),
                         reads=rkeys + [f"ring{sl}"], writes=[f"pb{bank}"])
                S.op("dve", lambda e, c=c, bank=bank: e.tensor_tensor(out=chb[:, c, 2:2 + n], in0=pb[bank][:, 0:n], in1=chb[:, c, 2:2 + n], op=ALU.mult),
                     reads=[f"pb{bank}", f"chb{c}"], writes=[f"chb{c}"])
                yield 0
            ring_release(it_c)

        psets = [(set0, 5, 6), (set1, 7, 4)]

        def prefix_front(t):
            B, kb, vbk_ = psets[t % 2]
            xs, cs, tcol = t % 4, t % 2, (t % 2) * 128
            if cast_state["done"] < len(groups):
                cast_step()
            load_tile(xp, csp, t, xs, cs)
            norm_to_uT(xs, g1t, "g1t", uT, "uT", tcol, B)
            tm_proj(1, kb, tcol)
            tm_proj(2, vbk_, tcol)

        def prefix_back(t):
            B, kb, vbk_ = psets[t % 2]
            cs, tcol = t % 2, (t % 2) * 128
            rotary(kb, cs, B["kr"], B["krk"], B)
            S.op("act", lambda e: e.activation(out=B["vb"], in_=pb[vbk_][:], func=AF.Copy), reads=[f"pb{vbk_}"], writes=[B["vbk"]])
            kv_state(kb, B, need_sb=(t == nt_pre - 1))
            if t == nt_pre - 1:
                for _ in conv_front(128, lambda k: uT[:, k, tcol:tcol + 128], [f"uT{t % 2}"], ("h", -1), ("C", -1)):
                    pass
                for c in range(4):
                    S.op("pool", lambda e, c=c: e.tensor_copy(out=chb[:, c, 0:2], in_=chb[:, c, 128:130]),
                         reads=[f"chb{c}"], writes=[f"chb{c}"])

        npre = nt_pre if STOP != "casts" else 0
        if npre:
            prefix_front(0)
        for t in range(npre):
            if t + 1 < npre:
                prefix_front(t + 1)
            prefix_back(t)

        while cast_state["done"] < len(groups):
            cast_step()
        ukeys = ["uT0", "uT1", "uT2", "uT3"]
        u2keys = ["u2T0", "u2T1", "u2T2", "u2T3"]
        mkeys = ["mixc0", "mixc1", "mixc2", "mixc3"]

        def mixer(b):
            xs0 = (b % 2) * 4
            for tt in range(4):
                t = b * 4 + tt
                load_tile(xm, None, t, xs0 + tt, 0)
            for tt in range(4):
                norm_to_uT(xs0 + tt, g1t, "g1t", uT, "uT", tt * 128)
                yield 0
            yield 1
            yield from conv_front(512, lambda k: uT[:, k, :], ukeys, ("h", b), ("C", b))
            yield 1
            slB = ring_get(("B", b))
            for c in range(4):
                bank = 4 + c
                for k in range(8):
                    S.op("pe", lambda e, k=k, c=c, bank=bank, slB=slB: e.matmul(pb[bank][:], lhsT=r8(slB)[:, k, c * 128:(c + 1) * 128],
                                                                               rhs=uT[:, k, :], start=(k == 0), stop=(k == 7)),
                         reads=ukeys + [f"ring{slB}"], writes=[f"pb{bank}"])
                ck = f"chb{c}"
                S.op("act", lambda e, c=c: e.activation(out=ctmp[:], in_=chb[:, c, 0:512], func=AF.Copy, scale=cwt[:, c * 3:c * 3 + 1]),
                     reads=[ck, "cwt"], writes=["ctmp"])
                S.op("dve", lambda e, c=c: e.scalar_tensor_tensor(out=ctmp[:], in0=chb[:, c, 1:513], scalar=cwt[:, c * 3 + 1:c * 3 + 2],
                                                                  in1=ctmp[:], op0=ALU.mult, op1=ALU.add),
                     reads=[ck, "cwt", "ctmp"], writes=["ctmp"])
                S.op("dve", lambda e, c=c: e.scalar_tensor_tensor(out=ctmp[:], in0=chb[:, c, 2:514], scalar=cwt[:, c * 3 + 2:c * 3 + 3],
                                                                  in1=ctmp[:], op0=ALU.mult, op1=ALU.add),
                     reads=[ck, "cwt", "ctmp"], writes=["ctmp"])
                S.op("pool", lambda e, c=c: e.tensor_copy(out=chb[:, c, 0:2], in_=chb[:, c, 512:514]), reads=[ck], writes=[ck])
                S.op("dve", lambda e, bank=bank: e.tensor_tensor(out=cyv[:], in0=pb[bank][:], in1=ctmp[:], op=ALU.mult),
                     reads=[f"pb{bank}", "ctmp"], writes=["cyv"])
                S.op("act", lambda e: e.activation(out=csq[:], in_=cyv[:], func=AF.Square), reads=["cyv"], writes=["csq"])
                yield 0
                S.op("pe", lambda e: e.matmul(pb[3][:], lhsT=onesb[:], rhs=csq[:], start=True, stop=True),
                     reads=["onesb", "csq"], writes=["pb3"])
                S.op("act", lambda e: e.activation(out=csq[:], in_=pb[3][:], func=AF.Sqrt, bias=epst[:], scale=1.0),
                     reads=["pb3", "epst"], writes=["csq"])
                S.op("dve", lambda e: e.reciprocal(out=csq[:], in_=csq[:]), reads=["csq"], writes=["csq"])
                S.op("dve", lambda e, c=c: e.scalar_tensor_tensor(out=mixT[:, c, :], in0=cyv[:], scalar=cgt[:, c:c + 1], in1=csq[:],
                                                                  op0=ALU.mult, op1=ALU.mult),
                     reads=["cyv", "csq", "cgt"], writes=[f"mixc{c}"])
                yield 0
            ring_release(("B", b))
            yield 1
            for tt in range(4):
                t = b * 4 + tt
                cs = t % 2
                tc0 = tt * 128
                S.dma("act", lambda e, t=t, cs=cs: e.dma_start(out=cst[cs][:], in_=csm[t]), f"cs{cs}", writes=[f"cst{cs}"])
                for gi in range(4):
                    tm_proj(gi, 4 + gi, tc0)
                    yield 0
                rotary(4, cs, qr[:], "qr")
                rotary(5, cs, kr[:], "kr")
                S.op("act", lambda e: e.activation(out=vb[:], in_=pb[6][:], func=AF.Copy), reads=["pb6"], writes=["vb"])
                S.op("act", lambda e: e.activation(out=sg[:], in_=pb[7][:], func=AF.Silu), reads=["pb7"], writes=["sg"])
                yield 0
                for h in range(NH):
                    S.op("pe", lambda e, h=h: e.transpose(out=pbf(TRB)[:, h, :], in_=qr[:, h * 128:(h + 1) * 128], identity=ident[:]),
                         reads=["qr", "ident"], writes=[f"pb{TRB}"])
                S.op("act", lambda e: e.activation(out=qT[:], in_=pbf(TRB)[:, 0:4, :], func=AF.Copy), reads=[f"pb{TRB}"], writes=["qT"])
                S.op("dve", lambda e: e.tensor_tensor(out=qxT[:], in0=pbf(TRB)[:, 0:4, :], in1=xit[:], op=ALU.mult),
                     reads=[f"pb{TRB}", "xit"], writes=["qxT"])
                yield 0
                for h in range(NH):
                    S.op("pe", lambda e, h=h: e.transpose(out=pbf(3)[:, h, :], in_=kr[:, h * 128:(h + 1) * 128], identity=ident[:]),
                         reads=["kr", "ident"], writes=["pb3"])
                S.op("act", lambda e: e.activation(out=kT[:], in_=pbf(3)[:, 0:4, :], func=AF.Copy), reads=["pb3"], writes=["kT"])
                yield 0
                for h in range(NH):
                    S.op("pe", lambda e, h=h: e.matmul(pb[3][:, h * 128:(h + 1) * 128], lhsT=kT[:, h, :], rhs=qT[:, h, :], start=True, stop=True),
                         reads=["kT", "qT"], writes=["pb3"])
                S.op("dve", lambda e: e.tensor_tensor(out=sc[:], in0=pb[3][:].rearrange("p (h n) -> p h n", h=4), in1=maskt[:], op=ALU.mult),
                     reads=["pb3", "maskt"], writes=["sc"])
                yield 0
                for h in range(NH):
                    S.op("pe", lambda e, h=h: e.matmul(pb[4][:, h * 128:(h + 1) * 128], lhsT=sc[:, h, :], rhs=vb[:, h * 128:(h + 1) * 128],
                                                       start=True, stop=False), reads=["sc", "vb"], writes=["pb4"])
                    S.op("pe", lambda e, h=h: e.matmul(pb[4][:, h * 128:(h + 1) * 128], lhsT=qxT[:, h, :], rhs=Sb[:, h, :],
                                                       start=False, stop=True), reads=["qxT", "Sb"], writes=["pb4"])
                kv_state(5)
                yield 0
                for h in range(NH):
                    S.op("act", lambda e, h=h: e.activation(out=junk[:, 0:128], in_=pb[4][:, h * 128:(h + 1) * 128], func=AF.Square,
                                                            accum_out=ssr[:, h:h + 1]), reads=["pb4"], writes=["junk", "ssr"])
                rstd_small(ssr[:], rr[:], 128, "ssr", "rr")
                for h in range(NH):
                    S.op("dve", lambda e, h=h: e.scalar_tensor_tensor(out=y2[:, h * 128:(h + 1) * 128], in0=pb[4][:, h * 128:(h + 1) * 128],
                                                                      scalar=rr[:, h:h + 1], in1=sg[:, h * 128:(h + 1) * 128],
                                                                      op0=ALU.mult, op1=ALU.mult),
                         reads=["pb4", "rr", "sg"], writes=["y2"])
                yield 0
                for h in range(NH):
                    S.op("pe", lambda e, h=h: e.transpose(out=pbf(TRB)[:, h, :], in_=y2[:, h * 128:(h + 1) * 128], identity=ident[:]),
                         reads=["y2", "ident"], writes=[f"pb{TRB}"])
                for h in range(NH):
                    S.op("act", lambda e, h=h, tc0=tc0: e.activation(out=mixT[:, 4 + h, tc0:tc0 + 128], in_=pbf(TRB)[:, h, :], func=AF.Copy,
                                                                     scale=rgt[:, h:h + 1]),
                         reads=[f"pb{TRB}", "rgt"], writes=[f"mixr{tt}"])
                yield 1
            slo = [ring_get(("wo0", b)), ring_get(("wo1", b))]
            for tt in range(4):
                tc0 = tt * 128
                xs = xs0 + tt
                for hf in range(2):
                    bank = 6 + hf
                    for c in range(8):
                        S.op("pe", lambda e, c=c, bank=bank, tc0=tc0, so=slo[hf]: e.matmul(pb[bank][:], lhsT=mixT[:, c, tc0:tc0 + 128],
                                                                                          rhs=r8(so)[:, c, :], start=(c == 0), stop=(c == 7)),
                             reads=mkeys + [f"mixr{tt}", f"ring{slo[hf]}"], writes=[f"pb{bank}"])
                    S.op("dve", lambda e, hf=hf, bank=bank, xs=xs: e.tensor_tensor(out=xt[xs][:, hf * 512:(hf + 1) * 512], in0=pb[bank][:],
                                                                                  in1=xt[xs][:, hf * 512:(hf + 1) * 512], op=ALU.add),
                         reads=[f"pb{bank}", f"xt{xs}"], writes=[f"xt{xs}"])
                yield 0
            ring_release(("wo0", b))
            ring_release(("wo1", b))
            yield 2
            for tt in range(4):
                norm_to_uT(xs0 + tt, g2t, "g2t", u2T, "u2T", tt * 128)
                yield 2
            yield 1

        def ffn_up_chunk(b, g, jj, sl):
            j = g * 4 + jj
            bank = j % 2
            for k in range(8):
                S.op("pe", lambda e, k=k: e.matmul(pb[bank][:], lhsT=r8(sl)[:, k, jj * 128:(jj + 1) * 128],
                                                   rhs=u2T[:, k, :], start=(k == 0), stop=(k == 7)),
                     reads=u2keys + [f"ring{sl}"], writes=[f"pb{bank}"])
            ri = j % 2
            if j % 2 == 0:
                S.op("act", lambda e: e.activation(out=rl[ri][:], in_=pb[bank][:], func=AF.Relu),
                     reads=[f"pb{bank}"], writes=[f"rl{ri}"])
            else:
                S.op("dve", lambda e: e.tensor_scalar(out=rl[ri][:], in0=pb[bank][:], scalar1=0.0, scalar2=None, op0=ALU.max),
                     reads=[f"pb{bank}"], writes=[f"rl{ri}"])
            S.op("pool", lambda e: e.tensor_tensor(out=hT[:, j, :], in0=rl[ri][:], in1=rl[ri][:], op=ALU.mult),
                 reads=[f"rl{ri}"], writes=[f"hT{j}"])

        def ffn_down(b):
            for g in range(8):
                sl = ring_get(("wd", b, g))
                for tt in range(4):
                    for hf in range(2):
                        bank = tt * 2 + hf
                        for jj in range(4):
                            j = g * 4 + jj
                            S.op("pe", lambda e, jj=jj, j=j, hf=hf, bank=bank, tt=tt, sl=sl: e.matmul(
                                pb[bank][:], lhsT=hT[:, j, tt * 128:(tt + 1) * 128], rhs=r4(sl)[:, jj, hf * 512:(hf + 1) * 512],
                                start=(j == 0), stop=(j == 31)),
                                reads=[f"hT{j}", f"ring{sl}"], writes=[f"pb{bank}"])
                ring_release(("wd", b, g))

        def finish(b):
            xs0 = (b % 2) * 4
            for tt in range(4):
                t = b * 4 + tt
                xs = xs0 + tt
                xk = f"xt{xs}"
                for hf in range(2):
                    bank = tt * 2 + hf
                    S.op("dve", lambda e, hf=hf, bank=bank, xs=xs: e.tensor_tensor(out=xt[xs][:, hf * 512:(hf + 1) * 512], in0=pb[bank][:],
                                                                                  in1=xt[xs][:, hf * 512:(hf + 1) * 512], op=ALU.add),
                         reads=[f"pb{bank}", xk], writes=[xk])
                S.op("act", lambda e, xs=xs: e.activation(out=junk[:], in_=xt[xs][:], func=AF.Square, accum_out=ss[:]),
                     reads=[xk], writes=["junk", "ss"])
                rstd_small(ss[:], rs[:], D, "ss", "rs")
                S.op("act", lambda e, xs=xs: e.activation(out=xt[xs][:], in_=xt[xs][:], func=AF.Copy, scale=rs[:]),
                     reads=[xk, "rs"], writes=[xk])
                S.op("pool", lambda e, xs=xs: e.tensor_tensor(out=xt[xs][:], in0=xt[xs][:], in1=gft[:], op=ALU.mult),
                     reads=[xk, "gft"], writes=[xk])
                S.dma("pool", lambda e, t=t, xs=xs: e.dma_start(out=y[t * 128:(t + 1) * 128, :], in_=xt[xs][:]), f"out{xs}", reads=[xk])

        if STOP not in ("casts", "prefix"):
            for _ in mixer(0):
                pass
            for b in range(nblk):
                gen = mixer(b + 1) if b + 1 < nblk else None
                for g in range(8):
                    sl = ring_get(("wu", b, g))
                    at_boundary = gen is None or g == 0
                    hold = False
                    for jj in range(4):
                        ffn_up_chunk(b, g, jj, sl)
                        if not at_boundary and not hold:
                            v = next(gen)
                            at_boundary = (v == 1)
                            hold = (v == 2)
                    ring_release(("wu", b, g))
                    if gen is not None and g != 0:
                        for _ in range(2 if g == 1 else 1):
                            while not at_boundary:
                                at_boundary = (next(gen) == 1)
                            at_boundary = False
                ffn_down(b)
                finish(b)
        S.emit(st)
    return nc


def _host_consts():
    f32 = np.float32
    lg = np.log(f32(1.0) - f32(2.0) ** (f32(-5.0) - np.arange(NH, dtype=f32))).astype(f32)
    idx = np.arange(128, dtype=f32)
    diff = idx[:, None] - idx[None, :]
    intra = np.where(diff[None] >= 0, np.exp(lg[:, None, None] * np.maximum(diff, 0.0)[None]), 0.0).astype(f32)
    scale = f32(128.0 ** -0.5)
    maskT = np.ascontiguousarray(np.transpose(intra, (2, 0, 1)) * scale).astype(f32).reshape(128, 512)
    zeta = np.exp(lg[:, None] * (f32(127.0) - idx)[None]).astype(f32)
    ztt = np.ascontiguousarray(zeta.T * scale).astype(f32)
    xi = np.exp(lg[:, None] * (idx + f32(1.0))[None]).astype(f32)
    xit = np.ascontiguousarray(np.broadcast_to(xi.reshape(1, 512), (128, 512))).astype(f32)
    ident = np.eye(128, dtype=f32).astype(ml_dtypes.bfloat16)
    onesblk = np.zeros((128, 128), f32)
    onesblk[:64, :64] = 1.0 / 64
    onesblk[64:, 64:] = 1.0 / 64
    return dict(maskt=maskT, ztt=ztt, xit=xit, ident=ident, onesblk=onesblk)


def _cs_table(pos0, ntiles):
    f32 = np.float32
    half = 64
    inv_freq = (f32(1.0) / (f32(10000.0) ** (np.arange(half, dtype=f32) / f32(half)))).astype(f32)
    pos = (pos0 + np.arange(ntiles * 128)).astype(f32)
    ang = (pos[:, None] * inv_freq[None, :]).astype(f32)
    c = np.cos(ang).astype(f32)
    s = np.sin(ang).astype(f32)
    tab = np.concatenate([c, c, -s, s], axis=1).astype(f32)
    return np.ascontiguousarray(tab.reshape(ntiles, 128, 256))


_NC_CACHE = {}


def kernel(x, norm1_g, w_in, conv_w, conv_norm_g, ret_norm_g, w_out, norm2_g, w_up, w_down, final_norm_g):
    x = np.asarray(x, dtype=np.float32)
    B, SEQ, _ = x.shape
    half = SEQ // 2
    nt = half // 128
    key = (nt, nt)
    if key not in _NC_CACHE:
        _NC_CACHE[key] = build_nc(nt, nt)
    nc = _NC_CACHE[key]
    f32 = np.float32
    consts = _host_consts()
    shared = dict(
        w_in=np.ascontiguousarray(w_in, dtype=f32), w_out=np.ascontiguousarray(w_out, dtype=f32),
        w_up=np.ascontiguousarray(w_up, dtype=f32), w_down=np.ascontiguousarray(w_down, dtype=f32),
        g1t=np.ascontiguousarray(np.asarray(norm1_g, f32).reshape(8, 128).T),
        g2t=np.ascontiguousarray(np.asarray(norm2_g, f32).reshape(8, 128).T),
        gft=np.ascontiguousarray(np.broadcast_to(np.asarray(final_norm_g, f32)[None, :], (128, D))),
        cwt=np.ascontiguousarray(np.asarray(conv_w, f32).reshape(3, 4, 128).transpose(2, 1, 0).reshape(128, 12)),
        cgt=np.ascontiguousarray(np.asarray(conv_norm_g, f32).reshape(4, 128).T),
        rgt=np.ascontiguousarray(np.asarray(ret_norm_g, f32).reshape(4, 128).T),
        **consts,
    )
    cs_lo = _cs_table(0, nt)
    cs_hi = _cs_table(half, nt)
    in_maps = []
    for c in range(8):
        b, hf = c // 2, c % 2
        m = dict(shared)
        m["xm"] = np.ascontiguousarray(x[b, hf * half:(hf + 1) * half])
        m["xp"] = np.ascontiguousarray(x[b, 0:half]) if hf == 1 else np.zeros((half, D), f32)
        m["csm"] = cs_hi if hf == 1 else cs_lo
        m["csp"] = cs_lo
        in_maps.append(m)
    res = run_bass_kernel_spmd(nc, in_maps, core_ids=list(range(8)))
    if DEBUG:
        kernel.dbg = [dict(r) for r in res.results]
    out = np.empty((B, SEQ, D), f32)
    for c in range(8):
        b, hf = c // 2, c % 2
        out[b, hf * half:(hf + 1) * half] = res.results[c]["y"]
    return out
```

```python
import numpy as np
import ml_dtypes
from contextlib import ExitStack
import concourse.bass as bass
import concourse.mybir as mybir
from concourse.bass_utils import run_bass_kernel_spmd

F32 = mybir.dt.float32
BF16 = mybir.dt.bfloat16
AF = mybir.ActivationFunctionType
ALU = mybir.AluOpType

D = 1024
NH = 4
EPS = 1e-6
NRING = 4
DEBUG = False
STOP = None


class Sched:
    ENGS = ("pe", "act", "dve", "pool", "sp")

    def __init__(self, nc):
        self.nc = nc
        self.ops = []
        self.last_w = {}
        self.readers = {}

    def _deps(self, reads, writes):
        deps = {}
        for k in reads:
            w = self.last_w.get(k)
            if w is not None:
                deps[w] = True
        for k in writes:
            w = self.last_w.get(k)
            if w is not None:
                deps.setdefault(w, False)
            for r in self.readers.get(k, ()):
                deps.setdefault(r, False)
        return deps

    def _commit(self, idx, reads, writes):
        for k in reads:
            self.readers.setdefault(k, []).append(idx)
        for k in writes:
            self.last_w[k] = idx
            self.readers[k] = []

    def op(self, eng, fn, reads=(), writes=()):
        writes = list(writes) + [k for k in reads if k.startswith("pb") and k not in writes]
        deps = self._deps(reads, writes)
        idx = len(self.ops)
        self.ops.append(dict(eng=eng, fn=fn, deps=deps, kind="c", sig=False))
        self._commit(idx, reads, writes)
        return idx

    def dma(self, eng, fn, sem, reads=(), writes=()):
        deps = self._deps(reads, writes)
        idx = len(self.ops)
        self.ops.append(dict(eng=eng, fn=fn, deps=deps, kind="d", sem=sem))
        self._commit(idx, reads, writes)
        return idx

    def emit(self, stack):
        nc = self.nc
        ops = self.ops

        def skip(p, o):
            return (p["kind"] == "c" and o["kind"] == "c" and p["eng"] == o["eng"])

        for o in ops:
            for d, raw in o["deps"].items():
                p = ops[d]
                if p["kind"] == "d":
                    continue
                if skip(p, o) and (p["eng"] == "pe" or not raw):
                    continue
                p["sig"] = True
        esem = {e: stack.enter_context(nc.semaphore("s_" + e)) for e in ("pe", "act", "dve", "pool")}
        dsem, dcnt = {}, {}
        cnt = {e: 0 for e in esem}
        for o in ops:
            if o["kind"] == "c":
                if o["sig"]:
                    cnt[o["eng"]] += 1
                    o["val"] = cnt[o["eng"]]
            else:
                s = o["sem"]
                if s not in dsem:
                    dsem[s] = stack.enter_context(nc.semaphore("d_" + s))
                    dcnt[s] = 0
                dcnt[s] += 16
                o["val"] = dcnt[s]
        per_eng = {e: [] for e in self.ENGS}
        for i, o in enumerate(ops):
            per_eng[o["eng"]].append(i)
        seen = {e: {} for e in self.ENGS}

        def run(eng_name, engine):
            sn = seen[eng_name]
            for i in per_eng[eng_name]:
                o = ops[i]
                waits = {}
                for d, raw in o["deps"].items():
                    p = ops[d]
                    if p["kind"] == "d":
                        key, h = ("d", p["sem"]), dsem[p["sem"]]
                    else:
                        if skip(p, o) and (eng_name == "pe" or not raw):
                            continue
                        key, h = ("c", p["eng"]), esem[p["eng"]]
                    v = p["val"]
                    if key not in waits or waits[key][1] < v:
                        waits[key] = (h, v)
                for key, (h, v) in waits.items():
                    if sn.get(key, 0) >= v:
                        continue
                    sn[key] = v
                    engine.wait_ge(h, v)
                ins = o["fn"](engine)
                if o["kind"] == "d":
                    ins.then_inc(dsem[o["sem"]], 16)
                elif o["sig"]:
                    ins.then_inc(esem[o["eng"]], 1)
            if eng_name == "sp":
                for s in dsem:
                    engine.wait_ge(dsem[s], dcnt[s])

        block = stack.enter_context(nc.Block())
        block.tensor(lambda e: run("pe", e))
        block.scalar(lambda e: run("act", e))
        block.vector(lambda e: run("dve", e))
        block.gpsimd(lambda e: run("pool", e))
        block.sync(lambda e: run("sp", e))


def build_nc(nt_main=32, nt_pre=32):
    assert nt_main % 4 == 0
    nblk = nt_main // 4
    lg = [float(np.log(np.float32(1.0) - np.float32(2.0) ** np.float32(-5.0 - h))) for h in range(NH)]
    cdec = [float(np.exp(np.float32(lg[h]) * np.float32(128.0))) for h in range(NH)]

    nc = bass.Bass("TRN2", target_bir_lowering=False)

    def din(name, shape, dt=F32):
        return nc.dram_tensor(name, shape, dt, kind="ExternalInput").ap()

    xm = din("xm", [nt_main * 128, D])
    xp = din("xp", [nt_pre * 128, D])
    csm = din("csm", [nt_main, 128, 256])
    csp = din("csp", [nt_pre, 128, 256])
    w_in = din("w_in", [D, 3584])
    w_out = din("w_out", [D, D])
    w_up = din("w_up", [D, 4096])
    w_down = din("w_down", [4096, D])
    d_g1 = din("g1t", [128, 8])
    d_g2 = din("g2t", [128, 8])
    d_gf = din("gft", [128, D])
    d_cw = din("cwt", [128, 12])
    d_cg = din("cgt", [128, 4])
    d_rg = din("rgt", [128, 4])
    d_zt = din("ztt", [128, 4])
    d_mask = din("maskt", [128, 512])
    d_xi = din("xit", [128, 512])
    d_id = din("ident", [128, 128], BF16)
    d_ones = din("onesblk", [128, 128])
    y = nc.dram_tensor("y", [nt_main * 128, D], F32, kind="ExternalOutput").ap()
    if DEBUG:
        dbg_x1 = nc.dram_tensor("dbg_x1", [nt_main * 128, D], F32, kind="ExternalOutput").ap()
        dbg_mix = nc.dram_tensor("dbg_mix", [nt_main // 4, 128, 4096], F32, kind="ExternalOutput").ap()
    wc_scr = nc.dram_tensor("wc_scr", [3, 128, 4096], BF16, kind="Internal").ap()
    wo_scr = nc.dram_tensor("wo_scr", [2, 128, 4096], BF16, kind="Internal").ap()
    wu_scr = nc.dram_tensor("wu_scr", [8, 128, 4096], BF16, kind="Internal").ap()
    wd_scr = nc.dram_tensor("wd_scr", [8, 128, 4096], BF16, kind="Internal").ap()

    with ExitStack() as st:
        S = Sched(nc)

        def sb(name, shape, dt=F32):
            return st.enter_context(nc.sbuf_tensor("s_" + name, shape, dt))

        wr = sb("wr", [128, 8, 2048], BF16)
        ring = [sb(f"ring{i}", [128, 4096], BF16) for i in range(NRING)]
        xt = [sb(f"xt{i}", [128, D]) for i in range(8)]
        cst = [sb(f"cst{i}", [128, 256]) for i in range(2)]
        uT = sb("uT", [128, 8, 512], BF16)
        u2T = sb("u2T", [128, 8, 512], BF16)
        mixT = sb("mixT", [128, 8, 512], BF16)
        hT = sb("hT", [128, 32, 512], BF16)
        chb = sb("chb", [128, 4, 514])
        ctmp = sb("ctmp", [128, 512])
        cyv = sb("cyv", [128, 512])
        csq = sb("csq", [128, 512])
        junk = sb("junk", [128, D], BF16)
        xn = sb("xn", [128, D], BF16)
        ss = sb("ss", [128, 1])
        rs = sb("rs", [128, 1])
        rA = sb("rA", [128, 512])
        rB = sb("rB", [128, 512])
        qr = sb("qr", [128, 512], BF16)
        kr = sb("kr", [128, 512], BF16)
        kz = sb("kz", [128, 512], BF16)
        vb = sb("vb", [128, 512], BF16)
        sg = sb("sg", [128, 512])
        qT = sb("qT", [128, 4, 128], BF16)
        qxT = sb("qxT", [128, 4, 128], BF16)
        kT = sb("kT", [128, 4, 128], BF16)
        sc = sb("sc", [128, 4, 128], BF16)
        T32 = sb("T32", [128, 4, 128])
        Sb = sb("Sb", [128, 4, 128], BF16)
        y2 = sb("y2", [128, 512], BF16)
        ssr = sb("ssr", [128, 4])
        rr = sb("rr", [128, 4])
        rl = [sb(f"rl{i}", [128, 512]) for i in range(2)]
        g1t = sb("g1t", [128, 8]); g2t = sb("g2t", [128, 8]); gft = sb("gft", [128, D])
        cwt = sb("cwt", [128, 12]); cgt = sb("cgt", [128, 4]); rgt = sb("rgt", [128, 4])
        ztt = sb("ztt", [128, 4]); maskt = sb("maskt", [128, 4, 128]); xit = sb("xit", [128, 4, 128])
        ident = sb("ident", [128, 128], BF16); onesb = sb("onesb", [128, 128])
        epst = sb("epst", [128, 1])
        pb = [st.enter_context(nc.psum_tensor(f"pb{i}", [128, 512], F32)) for i in range(8)]

        def pbf(i):
            return pb[i][:].bitcast(BF16).rearrange("p (k n) -> p k n", n=128)

        for dst, src, key in ((g1t, d_g1, "g1t"), (g2t, d_g2, "g2t"), (gft, d_gf, "gft"), (cwt, d_cw, "cwt"),
                              (cgt, d_cg, "cgt"), (rgt, d_rg, "rgt"), (ztt, d_zt, "ztt"), (ident, d_id, "ident"),
                              (onesb, d_ones, "onesb")):
            S.dma("sp", lambda e, dst=dst, src=src: e.dma_start(out=dst[:], in_=src), "c_" + key, writes=[key])
        S.dma("sp", lambda e: e.dma_start(out=maskt[:], in_=d_mask.rearrange("p (h n) -> p h n", h=4)), "c_maskt", writes=["maskt"])
        S.dma("sp", lambda e: e.dma_start(out=xit[:], in_=d_xi.rearrange("p (h n) -> p h n", h=4)), "c_xit", writes=["xit"])
        S.op("pool", lambda e: e.memset(epst[:], EPS), writes=["epst"])
        S.op("pool", lambda e: e.memset(T32[:], 0.0), writes=["T32"])
        S.op("pool", lambda e: e.memset(Sb[:], 0.0), writes=["Sb"])
        S.op("pool", lambda e: e.memset(chb[:], 0.0), writes=["chb0", "chb1", "chb2", "chb3"])
        stg = [hT[:, 16 * i:16 * (i + 1), :].rearrange("p a b -> p (a b)").bitcast(F32) for i in range(2)]
        stgk = [[f"hT{j}" for j in range(16 * i, 16 * (i + 1))] for i in range(2)]
        groups = []
        for gi in (1, 2, 0, 3):
            groups.append((w_in[:, 1536 + gi * 512:1536 + (gi + 1) * 512].rearrange("(k p) c -> p k c", p=128), 8,
                           wr[:, :, gi * 512:(gi + 1) * 512], None, f"wr{gi}"))
        for si, col0 in ((0, 1024), (1, 512), (2, 0)):
            groups.append((w_in[:, col0:col0 + 512].rearrange("(k p) c -> p k c", p=128), 8, None, wc_scr[si], f"wc_scr{si}"))
        for hf in range(2):
            groups.append((w_out[:, hf * 512:(hf + 1) * 512].rearrange("(k p) c -> p k c", p=128), 8, None, wo_scr[hf], f"wo_scr{hf}"))
        for g in range(8):
            groups.append((w_up[:, g * 512:(g + 1) * 512].rearrange("(k p) c -> p k c", p=128), 8, None, wu_scr[g], f"wu_scr{g}"))
        for g in range(8):
            groups.append((w_down[g * 512:(g + 1) * 512, :].rearrange("(j p) c -> p j c", p=128), 4, None, wd_scr[g], f"wd_scr{g}"))

        def cast_load(i):
            src, a, dsb, scr, key = groups[i]
            si = i % 2
            S.dma("sp", lambda e: e.dma_start(out=stg[si].rearrange("p (a b) -> p a b", a=a), in_=src), f"stg{si}", writes=stgk[si])

        def cast_do(i):
            src, a, dsb, scr, key = groups[i]
            si = i % 2
            eng = "act" if i % 2 == 0 else "dve"
            if dsb is not None:
                dst, wk = dsb, [key]
                src_v = stg[si].rearrange("p (a b) -> p a b", a=a)
            else:
                slot = i % NRING
                dst, wk = ring[slot][:], [f"ring{slot}"]
                src_v = stg[si]
            if eng == "act":
                S.op("act", lambda e: e.activation(out=dst, in_=src_v, func=AF.Copy), reads=stgk[si], writes=wk)
            else:
                S.op("dve", lambda e: e.tensor_copy(out=dst, in_=src_v), reads=stgk[si], writes=wk)
            if scr is not None:
                S.dma("sp", lambda e: e.dma_start(out=scr, in_=dst), "k_" + key, reads=wk, writes=[key])

        cast_state = dict(loaded=0, done=0)

        def cast_step():
            if cast_state["loaded"] < len(groups) and cast_state["loaded"] <= cast_state["done"] + 1:
                cast_load(cast_state["loaded"])
                cast_state["loaded"] += 1
            if cast_state["loaded"] < len(groups) and cast_state["loaded"] <= cast_state["done"] + 1:
                cast_load(cast_state["loaded"])
                cast_state["loaded"] += 1
            if cast_state["done"] < cast_state["loaded"]:
                cast_do(cast_state["done"])
                cast_state["done"] += 1

        n_upfront = 4 if nt_pre >= 28 else len(groups)
        while cast_state["done"] < n_upfront:
            cast_step()

        items = []
        ridx = {}

        def add_item(name, src, key):
            ridx[name] = len(items)
            items.append((src, key))

        def add_mixer_items(b, part):
            if part == 1:
                add_item(("h", b), wc_scr[0], "wc_scr0"); add_item(("C", b), wc_scr[1], "wc_scr1")
            elif part == 2:
                add_item(("B", b), wc_scr[2], "wc_scr2")
            elif part == 7:
                add_item(("wo0", b), wo_scr[0], "wo_scr0"); add_item(("wo1", b), wo_scr[1], "wo_scr1")

        add_item(("h", -1), wc_scr[0], "wc_scr0"); add_item(("C", -1), wc_scr[1], "wc_scr1")
        for part in range(8):
            add_mixer_items(0, part)
        for b in range(nblk):
            for g in range(8):
                add_item(("wu", b, g), wu_scr[g], f"wu_scr{g}")
                if b + 1 < nblk:
                    add_mixer_items(b + 1, g)
            for g in range(8):
                add_item(("wd", b, g), wd_scr[g], f"wd_scr{g}")
        rstate = dict(issued=0, released=set(), nget=0)

        def ring_pump(upto):
            while rstate["issued"] < len(items):
                i = rstate["issued"]
                if i >= NRING and (i - NRING) not in rstate["released"]:
                    assert i > upto, f"ring deadlock at item {i}"
                    break
                if i > upto + NRING - 1:
                    break
                src, skey = items[i]
                slot = i % NRING
                S.dma("sp", lambda e, slot=slot, src=src: e.dma_start(out=ring[slot][:], in_=src), f"ring{slot}",
                      reads=[skey], writes=[f"ring{slot}"])
                rstate["issued"] += 1

        def ring_get(name):
            i = ridx[name]
            assert i >= rstate["nget"], f"ring order violated at {name}"
            rstate["nget"] = i
            ring_pump(i)
            assert rstate["issued"] > i
            return i % NRING

        def ring_release(name):
            i = ridx[name]
            rstate["released"].add(i)
            ring_pump(i)

        def r8(slot):
            return ring[slot][:].rearrange("p (k c) -> p k c", k=8)

        def r4(slot):
            return ring[slot][:].rearrange("p (j c) -> p j c", j=4)

        TRB = 2
        def load_tile(xsrc, cssrc, t, xs, cs):
            S.dma("act", lambda e: e.dma_start(out=xt[xs][:], in_=xsrc[t * 128:(t + 1) * 128, :]), f"x{xs}", writes=[f"xt{xs}"])
            if cssrc is not None:
                S.dma("act", lambda e: e.dma_start(out=cst[cs][:], in_=cssrc[t]), f"cs{cs}", writes=[f"cst{cs}"])

        def rstd_small(src, dst, n, srck, dstk):
            S.op("act", lambda e: e.activation(out=dst, in_=src, func=AF.Sqrt, bias=epst[:], scale=1.0 / n),
                 reads=[srck, "epst"], writes=[dstk])
            S.op("dve", lambda e: e.reciprocal(out=dst, in_=dst), reads=[dstk], writes=[dstk])

        ss1 = sb("ss1", [128, 1]); rs1 = sb("rs1", [128, 1])
        qr2 = sb("qr2", [128, 512], BF16); kr2 = sb("kr2", [128, 512], BF16); vb2 = sb("vb2", [128, 512], BF16)
        set0 = dict(xn=xn[:], xnk="xn", rA=rA[:], rAk="rA", rB=rB[:], rBk="rB", kr=kr[:], krk="kr", kz=kz[:], kzk="kz",
                    vb=vb[:], vbk="vb", ss=ss[:], ssk="ss", rs=rs[:], rsk="rs", trb=2)
        set1 = dict(xn=xt[4][:, 0:512].bitcast(BF16), xnk="xt4", rA=xt[5][:, 0:512], rAk="xt5", rB=xt[5][:, 512:1024], rBk="xt5",
                    kr=xt[6][:, 0:256].bitcast(BF16), krk="xt6", kz=xt[6][:, 256:512].bitcast(BF16), kzk="xt6",
                    vb=xt[7][:, 0:256].bitcast(BF16), vbk="xt7", ss=ss1[:], ssk="ss1", rs=rs1[:], rsk="rs1", trb=3)

        def norm_to_uT(xs, gt, gk, dstT, dk, tcol, B=set0):
            xk = f"xt{xs}"
            trb = B["trb"]
            S.op("act", lambda e: e.activation(out=junk[:], in_=xt[xs][:], func=AF.Square, accum_out=B["ss"]),
                 reads=[xk], writes=["junk", B["ssk"]])
            rstd_small(B["ss"], B["rs"], D, B["ssk"], B["rsk"])
            S.op("act", lambda e: e.activation(out=B["xn"], in_=xt[xs][:], func=AF.Copy, scale=B["rs"]),
                 reads=[xk, B["rsk"]], writes=[B["xnk"]])
            for k in range(8):
                S.op("pe", lambda e, k=k: e.transpose(out=pbf(trb)[:, k, :], in_=B["xn"][:, k * 128:(k + 1) * 128], identity=ident[:]),
                     reads=[B["xnk"], "ident"], writes=[f"pb{trb}"])
            S.op("dve", lambda e: e.tensor_tensor(out=dstT[:, :, tcol:tcol + 128], in0=pbf(trb),
                                                  in1=gt[:].unsqueeze(2).broadcast_to([128, 8, 128]), op=ALU.mult),
                 reads=[f"pb{trb}", gk], writes=[f"{dk}{tcol // 128}"])

        def tm_proj(gi, bank, tcol):
            for k in range(8):
                S.op("pe", lambda e, k=k: e.matmul(pb[bank][:], lhsT=uT[:, k, tcol:tcol + 128], rhs=wr[:, k, gi * 512:(gi + 1) * 512],
                                                   start=(k == 0), stop=(k == 7)),
                     reads=[f"uT{tcol // 128}", f"wr{gi}"], writes=[f"pb{bank}"])

        def rotary(bank, cs, dst, dstk, B=set0):
            pv = pb[bank][:].rearrange("p (h two j) -> p h two j", two=2, j=64)
            psw = pv[:, :, ::-1, :]
            cc = cst[cs][:, 0:128].rearrange("p (two j) -> p two j", two=2).unsqueeze(1).broadcast_to([128, 4, 2, 64])
            s_ = cst[cs][:, 128:256].rearrange("p (two j) -> p two j", two=2).unsqueeze(1).broadcast_to([128, 4, 2, 64])
            a4 = B["rA"].rearrange("p (h two j) -> p h two j", two=2, j=64)
            b4 = B["rB"].rearrange("p (h two j) -> p h two j", two=2, j=64)
            S.op("dve", lambda e: e.tensor_tensor(out=a4, in0=pv, in1=cc, op=ALU.mult), reads=[f"pb{bank}", f"cst{cs}"], writes=[B["rAk"]])
            S.op("dve", lambda e: e.tensor_tensor(out=b4, in0=psw, in1=s_, op=ALU.mult), reads=[f"pb{bank}", f"cst{cs}"], writes=[B["rBk"]])
            S.op("pool", lambda e: e.tensor_tensor(out=dst, in0=B["rA"], in1=B["rB"], op=ALU.add), reads=[B["rAk"], B["rBk"]], writes=[dstk])

        def kv_state(kvb, B=set0, need_sb=True):
            kr_, kz_, vb_ = B["kr"], B["kz"], B["vb"]
            S.op("pool", lambda e: e.tensor_tensor(out=kz_.rearrange("p (h n) -> p h n", h=4),
                                                   in0=kr_.rearrange("p (h n) -> p h n", h=4),
                                                   in1=ztt[:].unsqueeze(2).broadcast_to([128, 4, 128]), op=ALU.mult),
                 reads=[B["krk"], "ztt"], writes=[B["kzk"]])
            for h in range(NH):
                S.op("pe", lambda e, h=h: e.matmul(pb[kvb][:, h * 128:(h + 1) * 128], lhsT=kz_[:, h * 128:(h + 1) * 128],
                                                   rhs=vb_[:, h * 128:(h + 1) * 128], start=True, stop=True),
                     reads=[B["kzk"], B["vbk"]], writes=[f"pb{kvb}"])
            for h in range(NH):
                S.op("dve", lambda e, h=h: e.scalar_tensor_tensor(out=T32[:, h, :], in0=T32[:, h, :], scalar=cdec[h],
                                                                  in1=pb[kvb][:, h * 128:(h + 1) * 128], op0=ALU.mult, op1=ALU.add),
                     reads=["T32", f"pb{kvb}"], writes=["T32"])
            if need_sb:
                S.op("pool", lambda e: e.tensor_copy(out=Sb[:], in_=T32[:]), reads=["T32"], writes=["Sb"])

        def conv_front(n, rhs_of_k, rkeys, it_h, it_c):
            sl = ring_get(it_h)
            for c in range(4):
                bank = 4 + c
                for k in range(8):
                    S.op("pe", lambda e, k=k, c=c, bank=bank, sl=sl: e.matmul(pb[bank][:, 0:n], lhsT=r8(sl)[:, k, c * 128:(c + 1) * 128],
                                                                             rhs=rhs_of_k(k), start=(k == 0), stop=(k == 7)),
                         reads=rkeys + [f"ring{sl}"], writes=[f"pb{bank}"])
                S.op("act", lambda e, c=c, bank=bank: e.activation(out=chb[:, c, 2:2 + n], in_=pb[bank][:, 0:n], func=AF.Copy),
                     reads=[f"pb{bank}"], writes=[f"chb{c}"])
                yield 0
            ring_release(it_h)
            sl = ring_get(it_c)
            for c in range(4):
                bank = 4 + c
                for k in range(8):
                    S.op("pe", lambda e, k=k, c=c, bank=bank, sl=sl: e.matmul(pb[bank][:, 0:n], lhsT=r8(sl)[:, k, c * 128:(c + 1) * 128],
                                                                             rhs=rhs_of_k(k), start=(k == 0), stop=(k == 7)),
                         reads=rkeys + [f"ring{sl}"], writes=[f"pb{bank}"])
                S.op("dve", lambda e, c=c, bank=bank: e.tensor_tensor(out=chb[:, c, 2:2 + n], in0=pb[bank][:, 0:n], in1=chb[:, c, 2:2 + n], op=ALU.mult),
                     reads=[f"pb{bank}", f"chb{c}"], writes=[f"chb{c}"])
                yield 0
            ring_release(it_c)

        psets = [(set0, 5, 6), (set1, 7, 4)]

        def prefix_front(t):
            B, kb, vbk_ = psets[t % 2]
            xs, cs, tcol = t % 4, t % 2, (t % 2) * 128
            if cast_state["done"] < len(groups):
                cast_step()
            load_tile(xp, csp, t, xs, cs)
            norm_to_uT(xs, g1t, "g1t", uT, "uT", tcol, B)
            tm_proj(1, kb, tcol)
            tm_proj(2, vbk_, tcol)

        def prefix_back(t):
            B, kb, vbk_ = psets[t % 2]
            cs, tcol = t % 2, (t % 2) * 128
            rotary(kb, cs, B["kr"], B["krk"], B)
            S.op("act", lambda e: e.activation(out=B["vb"], in_=pb[vbk_][:], func=AF.Copy), reads=[f"pb{vbk_}"], writes=[B["vbk"]])
            kv_state(kb, B, need_sb=(t == nt_pre - 1))
            if t == nt_pre - 1:
                for _ in conv_front(128, lambda k: uT[:, k, tcol:tcol + 128], [f"uT{t % 2}"], ("h", -1), ("C", -1)):
                    pass
                for c in range(4):
                    S.op("pool", lambda e, c=c: e.tensor_copy(out=chb[:, c, 0:2], in_=chb[:, c, 128:130]),
                         reads=[f"chb{c}"], writes=[f"chb{c}"])

        npre = nt_pre if STOP != "casts" else 0
        if npre:
            prefix_front(0)
        for t in range(npre):
            if t + 1 < npre:
                prefix_front(t + 1)
            prefix_back(t)

        while cast_state["done"] < len(groups):
            cast_step()
        ukeys = ["uT0", "uT1", "uT2", "uT3"]
        u2keys = ["u2T0", "u2T1", "u2T2", "u2T3"]
        mkeys = ["mixc0", "mixc1", "mixc2", "mixc3"]

        def mixer(b):
            xs0 = (b % 2) * 4
            for tt in range(4):
                t = b * 4 + tt
                load_tile(xm, None, t, xs0 + tt, 0)
            for tt in range(4):
                norm_to_uT(xs0 + tt, g1t, "g1t", uT, "uT", tt * 128)
                yield 0
            yield 1
            yield from conv_front(512, lambda k: uT[:, k, :], ukeys, ("h", b), ("C", b))
            yield 1
            slB = ring_get(("B", b))
            for c in range(4):
                bank = 4 + c
                for k in range(8):
                    S.op("pe", lambda e, k=k, c=c, bank=bank, slB=slB: e.matmul(pb[bank][:], lhsT=r8(slB)[:, k, c * 128:(c + 1) * 128],
                                                                               rhs=uT[:, k, :], start=(k == 0), stop=(k == 7)),
                         reads=ukeys + [f"ring{slB}"], writes=[f"pb{bank}"])
                ck = f"chb{c}"
                S.op("act", lambda e, c=c: e.activation(out=ctmp[:], in_=chb[:, c, 0:512], func=AF.Copy, scale=cwt[:, c * 3:c * 3 + 1]),
                     reads=[ck, "cwt"], writes=["ctmp"])
                S.op("dve", lambda e, c=c: e.scalar_tensor_tensor(out=ctmp[:], in0=chb[:, c, 1:513], scalar=cwt[:, c * 3 + 1:c * 3 + 2],
                                                                  in1=ctmp[:], op0=ALU.mult, op1=ALU.add),
                     reads=[ck, "cwt", "ctmp"], writes=["ctmp"])
                S.op("dve", lambda e, c=c: e.scalar_tensor_tensor(out=ctmp[:], in0=chb[:, c, 2:514], scalar=cwt[:, c * 3 + 2:c * 3 + 3],
                                                                  in1=ctmp[:], op0=ALU.mult, op1=ALU.add),
                     reads=[ck, "cwt", "ctmp"], writes=["ctmp"])
                S.op("pool", lambda e, c=c: e.tensor_copy(out=chb[:, c, 0:2], in_=chb[:, c, 512:514]), reads=[ck], writes=[ck])
                S.op("dve", lambda e, bank=bank: e.tensor_tensor(out=cyv[:], in0=pb[bank][:], in1=ctmp[:], op=ALU.mult),
                     reads=[f"pb{bank}", "ctmp"], writes=["cyv"])
                S.op("act", lambda e: e.activation(out=csq[:], in_=cyv[:], func=AF.Square), reads=["cyv"], writes=["csq"])
                yield 0
                S.op("pe", lambda e: e.matmul(pb[3][:], lhsT=onesb[:], rhs=csq[:], start=True, stop=True),
                     reads=["onesb", "csq"], writes=["pb3"])
                S.op("act", lambda e: e.activation(out=csq[:], in_=pb[3][:], func=AF.Sqrt, bias=epst[:], scale=1.0),
                     reads=["pb3", "epst"], writes=["csq"])
                S.op("dve", lambda e: e.reciprocal(out=csq[:], in_=csq[:]), reads=["csq"], writes=["csq"])
                S.op("dve", lambda e, c=c: e.scalar_tensor_tensor(out=mixT[:, c, :], in0=cyv[:], scalar=cgt[:, c:c + 1], in1=csq[:],
                                                                  op0=ALU.mult, op1=ALU.mult),
                     reads=["cyv", "csq", "cgt"], writes=[f"mixc{c}"])
                yield 0
            ring_release(("B", b))
            yield 1
            qrs, krs, vbs, sgs = [qr, qr2], [kr, kr2], [vb, vb2], [sg, csq]
            sfx = ["", "2"]

            def c_front(tt):
                t = b * 4 + tt
                cs, tc0, P = t % 2, tt * 128, tt % 2
                S.dma("act", lambda e, t=t, cs=cs: e.dma_start(out=cst[cs][:], in_=csm[t]), f"cs{cs}", writes=[f"cst{cs}"])
                for gi in range(4):
                    tm_proj(gi, 4 + gi, tc0)
                    yield 0
                rotary(4, cs, qrs[P][:], "qr" + sfx[P])
                rotary(5, cs, krs[P][:], "kr" + sfx[P])
                S.op("act", lambda e: e.activation(out=vbs[P][:], in_=pb[6][:], func=AF.Copy), reads=["pb6"], writes=["vb" + sfx[P]])
                S.op("act", lambda e: e.activation(out=sgs[P][:], in_=pb[7][:], func=AF.Silu), reads=["pb7"], writes=[("sg", "csq")[P]])
                yield 0

            def c_back(tt):
                tc0, P = tt * 128, tt % 2
                qr_, kr_, vb_, sg_ = qrs[P], krs[P], vbs[P], sgs[P]
                qk, kk, vk, sk = "qr" + sfx[P], "kr" + sfx[P], "vb" + sfx[P], ("sg", "csq")[P]
                BC = dict(set0, kr=kr_[:], krk=kk, vb=vb_[:], vbk=vk)
                for h in range(NH):
                    S.op("pe", lambda e, h=h: e.transpose(out=pbf(2)[:, h, :], in_=qr_[:, h * 128:(h + 1) * 128], identity=ident[:]),
                         reads=[qk, "ident"], writes=["pb2"])
                S.op("act", lambda e: e.activation(out=qT[:], in_=pbf(2)[:, 0:4, :], func=AF.Copy), reads=["pb2"], writes=["qT"])
                S.op("dve", lambda e: e.tensor_tensor(out=qxT[:], in0=pbf(2)[:, 0:4, :], in1=xit[:], op=ALU.mult),
                     reads=["pb2", "xit"], writes=["qxT"])
                yield 0
                for h in range(NH):
                    S.op("pe", lambda e, h=h: e.transpose(out=pbf(3)[:, h, :], in_=kr_[:, h * 128:(h + 1) * 128], identity=ident[:]),
                         reads=[kk, "ident"], writes=["pb3"])
                S.op("act", lambda e: e.activation(out=kT[:], in_=pbf(3)[:, 0:4, :], func=AF.Copy), reads=["pb3"], writes=["kT"])
                yield 0
                for h in range(NH):
                    S.op("pe", lambda e, h=h: e.matmul(pb[3][:, h * 128:(h + 1) * 128], lhsT=kT[:, h, :], rhs=qT[:, h, :], start=True, stop=True),
                         reads=["kT", "qT"], writes=["pb3"])
                S.op("dve", lambda e: e.tensor_tensor(out=sc[:], in0=pb[3][:].rearrange("p (h n) -> p h n", h=4), in1=maskt[:], op=ALU.mult),
                     reads=["pb3", "maskt"], writes=["sc"])
                yield 0
                for h in range(NH):
                    S.op("pe", lambda e, h=h: e.matmul(pb[2][:, h * 128:(h + 1) * 128], lhsT=sc[:, h, :], rhs=vb_[:, h * 128:(h + 1) * 128],
                                                       start=True, stop=False), reads=["sc", vk], writes=["pb2"])
                    S.op("pe", lambda e, h=h: e.matmul(pb[2][:, h * 128:(h + 1) * 128], lhsT=qxT[:, h, :], rhs=Sb[:, h, :],
                                                       start=False, stop=True), reads=["qxT", "Sb"], writes=["pb2"])
                kv_state(3, BC)
                yield 0
                for h in range(NH):
                    S.op("act", lambda e, h=h: e.activation(out=junk[:, 0:128], in_=pb[2][:, h * 128:(h + 1) * 128], func=AF.Square,
                                                            accum_out=ssr[:, h:h + 1]), reads=["pb2"], writes=["junk", "ssr"])
                rstd_small(ssr[:], rr[:], 128, "ssr", "rr")
                for h in range(NH):
                    S.op("dve", lambda e, h=h: e.scalar_tensor_tensor(out=y2[:, h * 128:(h + 1) * 128], in0=pb[2][:, h * 128:(h + 1) * 128],
                                                                      scalar=rr[:, h:h + 1], in1=sg_[:, h * 128:(h + 1) * 128],
                                                                      op0=ALU.mult, op1=ALU.mult),
                         reads=["pb2", "rr", sk], writes=["y2"])
                yield 0
                for h in range(NH):
                    S.op("pe", lambda e, h=h: e.transpose(out=pbf(2)[:, h, :], in_=y2[:, h * 128:(h + 1) * 128], identity=ident[:]),
                         reads=["y2", "ident"], writes=["pb2"])
                for h in range(NH):
                    S.op("act", lambda e, h=h, tc0=tc0: e.activation(out=mixT[:, 4 + h, tc0:tc0 + 128], in_=pbf(2)[:, h, :], func=AF.Copy,
                                                                     scale=rgt[:, h:h + 1]),
                         reads=["pb2", "rgt"], writes=[f"mixr{tt}"])
                yield 0

            yield from c_front(0)
            for tt in range(4):
                if tt + 1 < 4:
                    yield from c_front(tt + 1)
                yield from c_back(tt)
                yield 1
            slo = [ring_get(("wo0", b)), ring_get(("wo1", b))]
            for tt in range(4):
                tc0 = tt * 128
                xs = xs0 + tt
                for hf in range(2):
                    bank = 6 + hf
                    for c in range(8):
                        S.op("pe", lambda e, c=c, bank=bank, tc0=tc0, so=slo[hf]: e.matmul(pb[bank][:], lhsT=mixT[:, c, tc0:tc0 + 128],
                                                                                          rhs=r8(so)[:, c, :], start=(c == 0), stop=(c == 7)),
                             reads=mkeys + [f"mixr{tt}", f"ring{slo[hf]}"], writes=[f"pb{bank}"])
                    S.op("dve", lambda e, hf=hf, bank=bank, xs=xs: e.tensor_tensor(out=xt[xs][:, hf * 512:(hf + 1) * 512], in0=pb[bank][:],
                                                                                  in1=xt[xs][:, hf * 512:(hf + 1) * 512], op=ALU.add),
                         reads=[f"pb{bank}", f"xt{xs}"], writes=[f"xt{xs}"])
                yield 0
            ring_release(("wo0", b))
            ring_release(("wo1", b))
            yield 2
            for tt in range(4):
                norm_to_uT(xs0 + tt, g2t, "g2t", u2T, "u2T", tt * 128)
                yield 2
            yield 1

        def ffn_up_chunk(b, g, jj, sl):
            j = g * 4 + jj
            bank = j % 2
            for k in range(8):
                S.op("pe", lambda e, k=k: e.matmul(pb[bank][:], lhsT=r8(sl)[:, k, jj * 128:(jj + 1) * 128],
                                                   rhs=u2T[:, k, :], start=(k == 0), stop=(k == 7)),
                     reads=u2keys + [f"ring{sl}"], writes=[f"pb{bank}"])
            ri = j % 2
            if j % 2 == 0:
                S.op("act", lambda e: e.activation(out=rl[ri][:], in_=pb[bank][:], func=AF.Relu),
                     reads=[f"pb{bank}"], writes=[f"rl{ri}"])
            else:
                S.op("dve", lambda e: e.tensor_scalar(out=rl[ri][:], in0=pb[bank][:], scalar1=0.0, scalar2=None, op0=ALU.max),
                     reads=[f"pb{bank}"], writes=[f"rl{ri}"])
            S.op("pool", lambda e: e.tensor_tensor(out=hT[:, j, :], in0=rl[ri][:], in1=rl[ri][:], op=ALU.mult),
                 reads=[f"rl{ri}"], writes=[f"hT{j}"])

        def ffn_down(b):
            for g in range(8):
                sl = ring_get(("wd", b, g))
                for tt in range(4):
                    for hf in range(2):
                        bank = tt * 2 + hf
                        for jj in range(4):
                            j = g * 4 + jj
                            S.op("pe", lambda e, jj=jj, j=j, hf=hf, bank=bank, tt=tt, sl=sl: e.matmul(
                                pb[bank][:], lhsT=hT[:, j, tt * 128:(tt + 1) * 128], rhs=r4(sl)[:, jj, hf * 512:(hf + 1) * 512],
                                start=(j == 0), stop=(j == 31)),
                                reads=[f"hT{j}", f"ring{sl}"], writes=[f"pb{bank}"])
                ring_release(("wd", b, g))

        def finish(b):
            xs0 = (b % 2) * 4
            for tt in range(4):
                t = b * 4 + tt
                xs = xs0 + tt
                xk = f"xt{xs}"
                for hf in range(2):
                    bank = tt * 2 + hf
                    S.op("dve", lambda e, hf=hf, bank=bank, xs=xs: e.tensor_tensor(out=xt[xs][:, hf * 512:(hf + 1) * 512], in0=pb[bank][:],
                                                                                  in1=xt[xs][:, hf * 512:(hf + 1) * 512], op=ALU.add),
                         reads=[f"pb{bank}", xk], writes=[xk])
                S.op("act", lambda e, xs=xs: e.activation(out=junk[:], in_=xt[xs][:], func=AF.Square, accum_out=ss[:]),
                     reads=[xk], writes=["junk", "ss"])
                rstd_small(ss[:], rs[:], D, "ss", "rs")
                S.op("act", lambda e, xs=xs: e.activation(out=xt[xs][:], in_=xt[xs][:], func=AF.Copy, scale=rs[:]),
                     reads=[xk, "rs"], writes=[xk])
                S.op("pool", lambda e, xs=xs: e.tensor_tensor(out=xt[xs][:], in0=xt[xs][:], in1=gft[:], op=ALU.mult),
                     reads=[xk, "gft"], writes=[xk])
                S.dma("pool", lambda e, t=t, xs=xs: e.dma_start(out=y[t * 128:(t + 1) * 128, :], in_=xt[xs][:]), f"out{xs}", reads=[xk])

        if STOP not in ("casts", "prefix"):
            for _ in mixer(0):
                pass
            for b in range(nblk):
                gen = mixer(b + 1) if b + 1 < nblk else None
                for g in range(8):
                    sl = ring_get(("wu", b, g))
                    at_boundary = gen is None or g == 0
                    hold = False
                    for jj in range(4):
                        ffn_up_chunk(b, g, jj, sl)
                        if not at_boundary and not hold:
                            v = next(gen)
                            at_boundary = (v == 1)
                            hold = (v == 2)
                    ring_release(("wu", b, g))
                    if gen is not None and g != 0:
                        for _ in range(2 if g == 1 else 1):
                            while not at_boundary:
                                at_boundary = (next(gen) == 1)
                            at_boundary = False
                ffn_down(b)
                finish(b)
        S.emit(st)
    return nc


def _host_consts():
    f32 = np.float32
    lg = np.log(f32(1.0) - f32(2.0) ** (f32(-5.0) - np.arange(NH, dtype=f32))).astype(f32)
    idx = np.arange(128, dtype=f32)
    diff = idx[:, None] - idx[None, :]
    intra = np.where(diff[None] >= 0, np.exp(lg[:, None, None] * np.maximum(diff, 0.0)[None]), 0.0).astype(f32)
    scale = f32(128.0 ** -0.5)
    maskT = np.ascontiguousarray(np.transpose(intra, (2, 0, 1)) * scale).astype(f32).reshape(128, 512)
    zeta = np.exp(lg[:, None] * (f32(127.0) - idx)[None]).astype(f32)
    ztt = np.ascontiguousarray(zeta.T * scale).astype(f32)
    xi = np.exp(lg[:, None] * (idx + f32(1.0))[None]).astype(f32)
    xit = np.ascontiguousarray(np.broadcast_to(xi.reshape(1, 512), (128, 512))).astype(f32)
    ident = np.eye(128, dtype=f32).astype(ml_dtypes.bfloat16)
    onesblk = np.zeros((128, 128), f32)
    onesblk[:64, :64] = 1.0 / 64
    onesblk[64:, 64:] = 1.0 / 64
    return dict(maskt=maskT, ztt=ztt, xit=xit, ident=ident, onesblk=onesblk)


def _cs_table(pos0, ntiles):
    f32 = np.float32
    half = 64
    inv_freq = (f32(1.0) / (f32(10000.0) ** (np.arange(half, dtype=f32) / f32(half)))).astype(f32)
    pos = (pos0 + np.arange(ntiles * 128)).astype(f32)
    ang = (pos[:, None] * inv_freq[None, :]).astype(f32)
    c = np.cos(ang).astype(f32)
    s = np.sin(ang).astype(f32)
    tab = np.concatenate([c, c, -s, s], axis=1).astype(f32)
    return np.ascontiguousarray(tab.reshape(ntiles, 128, 256))


_NC_CACHE = {}


def kernel(x, norm1_g, w_in, conv_w, conv_norm_g, ret_norm_g, w_out, norm2_g, w_up, w_down, final_norm_g):
    x = np.asarray(x, dtype=np.float32)
    B, SEQ, _ = x.shape
    half = SEQ // 2
    nt = half // 128
    key = (nt, nt)
    if key not in _NC_CACHE:
        _NC_CACHE[key] = build_nc(nt, nt)
    nc = _NC_CACHE[key]
    f32 = np.float32
    consts = _host_consts()
    shared = dict(
        w_in=np.ascontiguousarray(w_in, dtype=f32), w_out=np.ascontiguousarray(w_out, dtype=f32),
        w_up=np.ascontiguousarray(w_up, dtype=f32), w_down=np.ascontiguousarray(w_down, dtype=f32),
        g1t=np.ascontiguousarray(np.asarray(norm1_g, f32).reshape(8, 128).T),
        g2t=np.ascontiguousarray(np.asarray(norm2_g, f32).reshape(8, 128).T),
        gft=np.ascontiguousarray(np.broadcast_to(np.asarray(final_norm_g, f32)[None, :], (128, D))),
        cwt=np.ascontiguousarray(np.asarray(conv_w, f32).reshape(3, 4, 128).transpose(2, 1, 0).reshape(128, 12)),
        cgt=np.ascontiguousarray(np.asarray(conv_norm_g, f32).reshape(4, 128).T),
        rgt=np.ascontiguousarray(np.asarray(ret_norm_g, f32).reshape(4, 128).T),
        **consts,
    )
    cs_lo = _cs_table(0, nt)
    cs_hi = _cs_table(half, nt)
    in_maps = []
    for c in range(8):
        b, hf = c // 2, c % 2
        m = dict(shared)
        m["xm"] = np.ascontiguousarray(x[b, hf * half:(hf + 1) * half])
        m["xp"] = np.ascontiguousarray(x[b, 0:half]) if hf == 1 else np.zeros((half, D), f32)
        m["csm"] = cs_hi if hf == 1 else cs_lo
        m["csp"] = cs_lo
        in_maps.append(m)
    res = run_bass_kernel_spmd(nc, in_maps, core_ids=list(range(8)))
    if DEBUG:
        kernel.dbg = [dict(r) for r in res.results]
    out = np.empty((B, SEQ, D), f32)
    for c in range(8):
        b, hf = c // 2, c % 2
        out[b, hf * half:(hf + 1) * half] = res.results[c]["y"]
    return out
```
